# Optimizing a Trainium2 kernel written in Bass

```python
import math
import jax, jax.numpy as jnp
from jax import lax
import numpy as np

D_MODEL = 1024
BATCH = 4
SEQ = 4096
DEPTH = 1

A_HEADS = 8
A_HEAD_DIM = 64
B_PATTERNS = ((128, 1), (512, 4), (2048, 16))
B_HEADS = 8
B_HEAD_DIM = 64
Q_BLOCK = 128
D_FF = 2816
CONV_WIDTH = 3
EPS = 1e-5
ALPHA = (2.0 * DEPTH) ** 0.25
BETA = (8.0 * DEPTH) ** -0.25

A_Q = 2 * A_HEADS * A_HEAD_DIM
A_K = 2 * A_HEADS * A_HEAD_DIM
A_V = A_HEADS * 2 * A_HEAD_DIM
B_QKV = 3 * len(B_PATTERNS) * B_HEADS * B_HEAD_DIM
GATE_COLS = 2 * D_MODEL
IN_COLS = A_Q + A_K + A_V + B_QKV + GATE_COLS
SPLITS = [A_Q, A_Q + A_K, A_Q + A_K + A_V, A_Q + A_K + A_V + B_QKV]

kernel_name = "hybrid_diffattn_dilated_deepnorm_block"


def lambda_init_fn(layer_idx):
    return 0.8 - 0.6 * math.exp(-0.3 * layer_idx)


def alibi_slopes(n):
    return jnp.asarray(np.power(np.float32(2.0), -8.0 * (np.arange(n, dtype=np.float32) + 1) / n), jnp.float32)


def layer_norm(x, g, b):
    xf = x.astype(jnp.float32)
    mu = jnp.mean(xf, axis=-1, keepdims=True)
    var = jnp.mean(jnp.square(xf - mu), axis=-1, keepdims=True)
    y = (xf - mu) * lax.rsqrt(var + EPS) * g.astype(jnp.float32) + b.astype(jnp.float32)
    return y.astype(x.dtype)


def diff_attention(q, k, v, lam, subln_w, lambda_init):
    bsz, seq = q.shape[0], q.shape[1]
    nblk = seq // Q_BLOCK
    slopes = jnp.repeat(alibi_slopes(A_HEADS), 2)
    qb = (q * A_HEAD_DIM ** -0.5).reshape(bsz, nblk, Q_BLOCK, 2 * A_HEADS, A_HEAD_DIM)
    qb = qb.transpose(1, 0, 3, 2, 4)
    kpos = jnp.arange(seq)

    def one_block(args):
        blk, qblk = args
        s = jnp.einsum('bmqd,bkmd->bmqk', qblk, k).astype(jnp.float32)
        dist = (blk * Q_BLOCK + jnp.arange(Q_BLOCK))[:, None] - kpos[None, :]
        s = jnp.where(dist >= 0, s - slopes[:, None, None] * dist, -jnp.inf)
        p = jax.nn.softmax(s, axis=-1).reshape(bsz, A_HEADS, 2, Q_BLOCK, seq)
        a = p[:, :, 0] - lam * p[:, :, 1]
        return jnp.einsum('bhqk,bkhe->bqhe', a.astype(v.dtype), v)

    o = lax.map(one_block, (jnp.arange(nblk), qb))
    o = o.transpose(1, 0, 2, 3, 4).reshape(bsz, seq, A_HEADS, 2 * A_HEAD_DIM).astype(jnp.float32)
    o = o * lax.rsqrt(jnp.mean(jnp.square(o), axis=-1, keepdims=True) + EPS)
    o = o * subln_w.astype(jnp.float32) * (1.0 - lambda_init)
    return o.reshape(bsz, seq, A_HEADS * 2 * A_HEAD_DIM).astype(q.dtype)


def dilated_group(q, k, v, slopes, window, dil):
    bsz, seq, nh, dh = q.shape
    steps = window // dil
    span = dil * steps
    s_pad = -(-seq // span) * span
    padw = ((0, 0), (0, s_pad - seq), (0, 0), (0, 0))
    q, k, v = jnp.pad(q, padw), jnp.pad(k, padw), jnp.pad(v, padw)
    nb = s_pad // span
    split = lambda t: t.reshape(bsz, nb, steps, dil, nh, dh)
    qs, ks, vs = split(q * dh ** -0.5), split(k), split(v)
    prev = lambda t: jnp.pad(t, ((0, 0), (1, 0), (0, 0), (0, 0), (0, 0), (0, 0)))[:, :-1]
    kk = jnp.concatenate([prev(ks), ks], axis=2)
    vv = jnp.concatenate([prev(vs), vs], axis=2)
    s = jnp.einsum('bnqrhd,bnkrhd->bnrhqk', qs, kk).astype(jnp.float32)
    qi = jnp.arange(steps)
    kj = jnp.arange(2 * steps)
    step = qi[:, None] + steps - kj[None, :]
    valid = ((step >= 0) & (step <= steps))[None] & \
        ((jnp.arange(nb)[:, None, None] * steps + kj[None, None, :] - steps) >= 0)
    bias = -slopes[:, None, None] * (step * dil)
    s = jnp.where(valid[None, :, None, None], s + bias, -jnp.inf)
    m = jnp.max(s, axis=-1, keepdims=True)
    e = jnp.exp(s - m)
    den = jnp.sum(e, axis=-1, keepdims=True)
    o = jnp.einsum('bnrhqk,bnkrhd->bnrhqd', e.astype(v.dtype), vv).astype(jnp.float32) / den
    lse = (m + jnp.log(den))[..., 0]
    o = o.transpose(0, 1, 4, 2, 3, 5).reshape(bsz, s_pad, nh, dh)[:, :seq]
    lse = lse.transpose(0, 1, 4, 2, 3).reshape(bsz, s_pad, nh)[:, :seq]
    return o, lse


def setup_inputs(seed: int = 0) -> dict:
    key = jax.random.key(seed)
    ks = jax.random.split(key, 20)
    f32 = jnp.float32
    nrm = lambda k, shape, scale: jax.random.normal(k, shape, f32) * scale
    col_scale = np.ones((IN_COLS,), np.float32)
    col_scale[A_Q + A_K:A_Q + A_K + A_V] = BETA
    b_v0 = A_Q + A_K + A_V + 2 * (B_QKV // 3)
    col_scale[b_v0:b_v0 + B_QKV // 3] = BETA
    w_in = nrm(ks[1], (DEPTH, D_MODEL, IN_COLS), D_MODEL ** -0.5) * jnp.asarray(col_scale)
    return {
        "x": jax.random.normal(ks[0], (BATCH, SEQ, D_MODEL), f32),
        "w_in": w_in,
        "b_gate": nrm(ks[2], (DEPTH, GATE_COLS), 0.01),
        "lambda_q1": nrm(ks[3], (DEPTH, A_HEAD_DIM), 0.1),
        "lambda_k1": nrm(ks[4], (DEPTH, A_HEAD_DIM), 0.1),
        "lambda_q2": nrm(ks[5], (DEPTH, A_HEAD_DIM), 0.1),
        "lambda_k2": nrm(ks[6], (DEPTH, A_HEAD_DIM), 0.1),
        "subln_w": 1.0 + nrm(ks[7], (DEPTH, 2 * A_HEAD_DIM), 0.01),
        "w_pa": nrm(ks[8], (DEPTH, A_V, D_MODEL), A_V ** -0.5),
        "w_pb": nrm(ks[9], (DEPTH, B_HEADS * B_HEAD_DIM, D_MODEL), (B_HEADS * B_HEAD_DIM) ** -0.5),
        "w_o": nrm(ks[10], (DEPTH, D_MODEL, D_MODEL), BETA * D_MODEL ** -0.5),
        "ln1_g": 1.0 + nrm(ks[11], (DEPTH, D_MODEL), 0.01),
        "ln1_b": nrm(ks[12], (DEPTH, D_MODEL), 0.01),
        "w_up": nrm(ks[13], (DEPTH, D_MODEL, 2 * D_FF), D_MODEL ** -0.5),
        "w_conv": nrm(ks[14], (DEPTH, CONV_WIDTH, 2 * D_FF), CONV_WIDTH ** -0.5),
        "b_conv": nrm(ks[15], (DEPTH, 2 * D_FF), 0.01),
        "w_down": nrm(ks[16], (DEPTH, D_FF, D_MODEL), BETA * D_FF ** -0.5),
        "ln2_g": 1.0 + nrm(ks[17], (DEPTH, D_MODEL), 0.01),
        "ln2_b": nrm(ks[18], (DEPTH, D_MODEL), 0.01),
    }


def reference(x, w_in, b_gate, lambda_q1, lambda_k1, lambda_q2, lambda_k2, subln_w,
              w_pa, w_pb, w_o, ln1_g, ln1_b, w_up, w_conv, b_conv, w_down, ln2_g, ln2_b):
    bsz, seq, _ = x.shape
    slopes_b = alibi_slopes(B_HEADS)
    for l in range(DEPTH):
        lambda_init = lambda_init_fn(l)
        proj = x @ w_in[l]
        qa, ka, va, pb, gl = jnp.split(proj, SPLITS, axis=-1)
        qa = qa.reshape(bsz, seq, 2 * A_HEADS, A_HEAD_DIM)
        ka = ka.reshape(bsz, seq, 2 * A_HEADS, A_HEAD_DIM)
        va = va.reshape(bsz, seq, A_HEADS, 2 * A_HEAD_DIM)
        lam = (jnp.exp(jnp.sum(lambda_q1[l].astype(jnp.float32) * lambda_k1[l].astype(jnp.float32)))
               - jnp.exp(jnp.sum(lambda_q2[l].astype(jnp.float32) * lambda_k2[l].astype(jnp.float32)))
               + lambda_init)
        oa = diff_attention(qa, ka, va, lam, subln_w[l], lambda_init)

        pb = pb.reshape(bsz, seq, 3, len(B_PATTERNS), B_HEADS, B_HEAD_DIM)
        outs, lses = [], []
        for g, (window, dil) in enumerate(B_PATTERNS):
            o, lse = dilated_group(pb[:, :, 0, g], pb[:, :, 1, g], pb[:, :, 2, g], slopes_b, window, dil)
            outs.append(o)
            lses.append(lse)
        wts = jax.nn.softmax(jnp.stack(lses), axis=0)
        ob = jnp.sum(wts[..., None] * jnp.stack(outs), axis=0)
        ob = ob.reshape(bsz, seq, B_HEADS * B_HEAD_DIM).astype(x.dtype)

        gates = jax.nn.sigmoid((gl + b_gate[l]).astype(jnp.float32)).astype(x.dtype)
        gates = gates.reshape(bsz, seq, 2, D_MODEL)
        y = gates[:, :, 0] * (oa @ w_pa[l]) + gates[:, :, 1] * (ob @ w_pb[l])
        x = layer_norm(ALPHA * x + y @ w_o[l], ln1_g[l], ln1_b[l])

        h = x @ w_up[l]
        hp = jnp.pad(h, ((0, 0), (CONV_WIDTH - 1, 0), (0, 0)))
        h = b_conv[l] + sum(hp[:, j:j + seq] * w_conv[l, j] for j in range(CONV_WIDTH))
        a, gv = jnp.split(h, 2, axis=-1)
        f = jax.nn.gelu(a, approximate=False) * gv
        x = layer_norm(ALPHA * x + f @ w_down[l], ln2_g[l], ln2_b[l])
    return x
```

```python
import math
from contextlib import ExitStack

import numpy as np
import concourse.bass as bass
import concourse.mybir as mybir
from concourse.bass_utils import run_bass_kernel_spmd

F32 = mybir.dt.float32
BF16 = mybir.dt.bfloat16
AF = mybir.ActivationFunctionType
ALU = mybir.AluOpType
AX = mybir.AxisListType

ENGS = ("pe", "act", "dve", "pool", "sp")

D = 1024
S = 4096
NT = 2048
HALO = 2
NTO = NT + HALO
DFF = 2816
NCP = 22
IN_COLS = 9728
ALPHA = 2.0 ** 0.25
LAMBDA_INIT = 0.8 - 0.6 * math.exp(0.0)
C_SUBLN = 1.0 - LAMBDA_INIT
EPS = 1e-5
NEG = -30000.0
GROUP_DIL = (1, 4, 16)
NTAB = 74
SKIP_MARGIN = 150.0


class T:
    def __init__(self, name, ap):
        self.name = name
        self.ap = ap
        self.last_w = None
        self.readers = {}
        self.dma_sem = None
        self.dma_cnt = 0
        self.dma_wcnt = 0
        self.group = None

    def __getitem__(self, k):
        return self.ap[k]


class Op:
    __slots__ = ("eng", "fn", "waits", "inc", "idx", "is_dma")

    def __init__(self, eng, fn):
        self.eng = eng
        self.fn = fn
        self.waits = []
        self.inc = False
        self.idx = 0
        self.is_dma = False


class Emitter:
    def __init__(self, nc, stack):
        self.nc = nc
        self.stack = stack
        self.ops = {e: [] for e in ENGS}
        self.cops = {e: [] for e in ENGS}
        self.sems = {e: stack.enter_context(nc.semaphore("s_" + e)) for e in ENGS}
        self.waited = {e: {} for e in ENGS}
        self.tiles = []
        self.free_dsems = []
        self.ndsem = 0

    def tile(self, name, ap):
        t = T(name, ap)
        self.tiles.append(t)
        return t

    def _dsem(self, t):
        if t.dma_sem is None:
            t.dma_sem = self.stack.enter_context(self.nc.semaphore("d%d" % self.ndsem))
            self.ndsem += 1
        return t.dma_sem

    def _need(self, op, src_eng, idx):
        if idx is None or idx <= 0:
            return
        e = op.eng
        if src_eng == e and e == "pe":
            return
        w = self.waited[e]
        if w.get(src_eng, 0) >= idx:
            return
        w[src_eng] = idx
        op.waits.append(("eng", src_eng, idx))
        self.cops[src_eng][idx - 1].inc = True

    def _need_dma(self, op, t, writes_only=False):
        cnt = t.dma_wcnt if writes_only else t.dma_cnt
        if t.dma_sem is None or cnt == 0:
            return
        key = ("dma", id(t))
        w = self.waited[op.eng]
        if w.get(key, 0) >= cnt:
            return
        w[key] = cnt
        op.waits.append(("dma", t, cnt * 16))

    def op(self, eng, fn, reads=(), writes=()):
        o = Op(eng, fn)
        self.cops[eng].append(o)
        o.idx = len(self.cops[eng])
        for t in reads:
            if t.last_w is not None:
                self._need(o, *t.last_w)
            self._need_dma(o, t, writes_only=True)
        for t in writes:
            if t.last_w is not None:
                self._need(o, *t.last_w)
            for re_, ri in t.readers.items():
                self._need(o, re_, ri)
            self._need_dma(o, t)
        for t in reads:
            if t.readers.get(eng, 0) < o.idx:
                t.readers[eng] = o.idx
        for t in writes:
            t.last_w = (eng, o.idx)
            t.readers = {}
        self.ops[eng].append(o)
        return o

    def dma(self, eng, out_ap, in_ap, tile, is_load, group=None, also_read=()):
        t = tile
        sem = self._dsem(t)

        def fn(e, out_ap=out_ap, in_ap=in_ap, sem=sem):
            return e.dma_start(out=out_ap, in_=in_ap).then_inc(sem, 16)

        o = Op(eng, fn)
        o.is_dma = True
        if t.last_w is not None:
            self._need(o, *t.last_w)
        if is_load:
            for re_, ri in t.readers.items():
                self._need(o, re_, ri)
        if not (group is not None and t.group == group):
            self._need_dma(o, t)
        for t2 in also_read:
            if t2.last_w is not None:
                self._need(o, *t2.last_w)
        t.group = group
        t.dma_cnt += 1
        if is_load:
            t.dma_wcnt = t.dma_cnt
        for t2 in also_read:
            t2.dma_sem = sem
            t2.dma_cnt = t.dma_cnt
        if is_load:
            t.last_w = None
            t.readers = {}
        self.ops[eng].append(o)
        return o

    def barrier(self):
        snap = {e: len(self.cops[e]) for e in ENGS}
        for e in ENGS:
            o = Op(e, None)
            self.cops[e].append(o)
            o.idx = len(self.cops[e])
            for src in ENGS:
                if src != e:
                    self._need(o, src, snap[src])
            for t in self.tiles:
                self._need_dma(o, t)
            self.ops[e].append(o)
        self.tiles = [t for t in self.tiles if not getattr(t, "dead", False)]

    def emit(self):
        nc = self.nc
        em = self
        pref = {}
        for e in ENGS:
            c = 0
            p = [0]
            for o in em.cops[e]:
                if o.inc:
                    c += 1
                p.append(c)
            pref[e] = p
        with nc.Block() as block:
            def run(ename, eh):
                for o in em.ops[ename]:
                    for kind, src, val in o.waits:
                        if kind == "eng":
                            eh.wait_ge(em.sems[src], pref[src][val])
                        else:
                            eh.wait_ge(src.dma_sem, val)
                    ins = None
                    if o.fn is not None:
                        ins = o.fn(eh)
                    if o.is_dma:
                        continue
                    if o.inc:
                        if ins is None:
                            ins = eh.nop()
                        ins.then_inc(em.sems[ename], 1)

            @block.tensor
            def _(e):
                run("pe", e)

            @block.scalar
            def _(e):
                run("act", e)

            @block.vector
            def _(e):
                run("dve", e)

            @block.gpsimd
            def _(e):
                run("pool", e)

            @block.sync
            def _(e):
                run("sp", e)

    def mm(self, out_t, out_ap, lhsT, rhs, start, stop, reads, skip=False):
        return self.op("pe", lambda e: e.matmul(out_ap, lhsT=lhsT, rhs=rhs, start=start, stop=stop,
                                                skip_group_check=skip),
                       reads=reads, writes=[out_t])

    def act(self, out_t, out_ap, in_ap, func, reads, bias=None, scale=None):
        kw = {}
        if bias is not None:
            kw["bias"] = bias
        if scale is not None:
            kw["scale"] = scale
        return self.op("act", lambda e: e.activation(out=out_ap, in_=in_ap, func=func, **kw),
                       reads=reads, writes=[out_t])

    def v(self, eng, name, writes, reads, **kw):
        return self.op(eng, lambda e: getattr(e, name)(**kw), reads=reads, writes=writes)


class Arena:
    def __init__(self, em, nbytes):
        self.em = em
        self.n = nbytes
        self.h = em.stack.enter_context(em.nc.sbuf_tensor("arena", [128, nbytes // 2], BF16))
        self.lo = 0
        self.hi = nbytes

    def alloc(self, name, shape, dt, top=False, at=None):
        esz = 4 if dt == F32 else 2
        n = int(np.prod(shape[1:])) * esz
        n = (n + 63) // 64 * 64
        if at is not None:
            off = at
        elif top:
            self.hi -= n
            off = self.hi
        else:
            off = self.lo
            self.lo += n
        assert self.lo <= self.hi, "arena overflow at %s: lo=%d hi=%d" % (name, self.lo, self.hi)
        nel = int(np.prod(shape[1:]))
        a = self.h[:, off // 2: off // 2 + nel * esz // 2]
        if dt == F32:
            a = a.bitcast(F32)
        if len(shape) == 3:
            a = a.rearrange("p (a b) -> p a b", a=shape[1])
        elif len(shape) == 4:
            a = a.rearrange("p (a b c) -> p a b c", a=shape[1], b=shape[2])
        if shape[0] < 128:
            a = a[0:shape[0]]
        t = self.em.tile(name, a)
        t.off = off
        t.nbytes = n
        return t


def slopes8():
    return [2.0 ** (-(h + 1)) for h in range(8)]


def biasA_layout():
    cols = {}
    n = 0
    for r in range(5):
        njs = 16 if r == 0 else 16 + 4 * r
        for j in range(njs):
            for h in range(8):
                ns = 1 if (r == 0 or h >= 2) else (2 if h == 1 else 4)
                for s in range(ns):
                    cols[(r, j, h, s)] = n
                    n += 1
    return cols, n


BIAS_COLS, NBIAS = biasA_layout()
CST = {}
_o = 0
for _name, _n in (("bg", 16), ("lam4", 256), ("subln", 128), ("wc", 132), ("bc", 44), ("flag", 1),
                  ("biasA", NBIAS)):
    CST[_name] = _o
    _o += _n
NCST = _o


def nsub(r, h):
    return 1 if (r == 0 or h >= 2) else (2 if h == 1 else 4)


def build_biasA(half):
    m = slopes8()
    out = np.zeros((128, NBIAS), np.float32)
    p = np.arange(128, dtype=np.float64)
    for (r, j, h, s), c in BIAS_COLS.items():
        if r == 0:
            qc = 2047.0
        else:
            ws = 512 // nsub(r, h)
            qc = 2048 + 512 * (r - 1) + ws * s + ws / 2
        if half == 0 and r >= 1 and j < 16:
            out[:, c] = NEG
        else:
            out[:, c] = m[h] * (128 * j + p - qc)
    return out


def build_tabB(half):
    m = slopes8()
    kj = np.arange(128)[:, None].astype(np.float64)
    qi = np.arange(128)[None, :].astype(np.float64)
    tabs = np.zeros((128, NTAB, 128), np.float32)
    for g, dil in enumerate(GROUP_DIL):
        for h in range(8):
            cur = np.where(qi - kj >= 0, -m[h] * dil * (qi - kj), NEG)
            st = qi + 128 - kj
            prev = np.where(st <= 128, -m[h] * dil * st, NEG)
            prevx = prev if half == 1 else np.full_like(prev, NEG)
            b = (g * 8 + h) * 3
            tabs[:, b + 0] = cur
            tabs[:, b + 1] = prev
            tabs[:, b + 2] = prevx
    tabs[:, 72] = np.where(kj <= qi, 0.0, NEG)
    tabs[:, 73] = np.where(kj <= qi, 1.0, 0.0)
    return tabs.reshape(128, NTAB * 128)


def prep_inputs(x, w_in, b_gate, lambda_q1, lambda_k1, lambda_q2, lambda_k2, subln_w,
                w_pa, w_pb, w_o, ln1_g, ln1_b, w_up, w_conv, b_conv, w_down, ln2_g, ln2_b):
    f = np.float32
    rep = lambda v: np.ascontiguousarray(np.broadcast_to(np.asarray(v, f).reshape(1, -1), (128, np.asarray(v).size)))
    shared = {
        "w_in": np.ascontiguousarray(w_in[0], f), "w_pa": np.ascontiguousarray(w_pa[0], f),
        "w_pb": np.ascontiguousarray(w_pb[0], f), "w_o": np.ascontiguousarray(w_o[0], f),
        "w_up": np.ascontiguousarray(w_up[0], f), "w_down": np.ascontiguousarray(w_down[0], f),
        "lnp": np.ascontiguousarray(np.concatenate([rep(ln1_g[0]), rep(ln1_b[0]), rep(ln2_g[0]), rep(ln2_b[0])], 1)),
    }
    cst_common = np.zeros((128, NCST), f)
    cst_common[:, CST["bg"]:CST["bg"] + 16] = np.asarray(b_gate[0], f).reshape(16, 128).T
    cst_common[:, CST["lam4"]:CST["lam4"] + 256] = np.concatenate(
        [rep(lambda_q1[0]), rep(lambda_k1[0]), rep(lambda_q2[0]), rep(lambda_k2[0])], 1)
    cst_common[:, CST["subln"]:CST["subln"] + 128] = rep(subln_w[0])
    wc = np.asarray(w_conv[0], f).reshape(3, 44, 128)
    cst_common[:, CST["wc"]:CST["wc"] + 132] = wc.transpose(2, 0, 1).reshape(128, 132)
    cst_common[:, CST["bc"]:CST["bc"] + 44] = np.asarray(b_conv[0], f).reshape(44, 128).T
    tabs = [build_tabB(0), build_tabB(1)]
    biases = [build_biasA(0), build_biasA(1)]
    in_maps = []
    for c in range(8):
        b, half = c // 2, c % 2
        xb = np.asarray(x[b], f)
        if half == 1:
            xT = np.ascontiguousarray(xb.T)
            xown = np.ascontiguousarray(xb[2046:4096])
        else:
            xT = np.zeros((D, S), f)
            xT[:, 2048:] = xb[:2048].T
            xown = np.zeros((NTO, D), f)
            xown[2:] = xb[:2048]
        cst = cst_common.copy()
        cst[:, CST["flag"]] = float(half)
        cst[:, CST["biasA"]:CST["biasA"] + NBIAS] = biases[half]
        m = dict(shared)
        m.update({"xT": xT, "xown": xown, "cst": cst, "tabB": tabs[half]})
        in_maps.append(m)
    return in_maps


def build(stop_after=None, dbg=()):
    nc = bass.Bass("TRN2", target_bir_lowering=False)
    dr = {}
    for name, shape in (("xT", [D, S]), ("xown", [NTO, D]), ("w_in", [D, IN_COLS]), ("w_pa", [D, D]),
                        ("w_pb", [512, D]), ("w_o", [D, D]), ("w_up", [D, 2 * DFF]), ("w_down", [DFF, D]),
                        ("lnp", [128, 4 * D]), ("cst", [128, NCST]), ("tabB", [128, NTAB * 128])):
        dr[name] = nc.dram_tensor(name, shape, F32, kind="ExternalInput").ap()
    y_out = nc.dram_tensor("y", [NT, D], F32, kind="ExternalOutput").ap()
    x1s = nc.dram_tensor("x1s", [NT, D], F32).ap()
    dbg_out = {}
    for name, shape in dbg:
        dbg_out[name] = nc.dram_tensor(name, shape, F32, kind="ExternalOutput").ap()

    w_in_r = dr["w_in"].rearrange("(kc p) n -> p kc n", p=128)
    xT_r = dr["xT"].rearrange("(kc p) t -> p kc t", p=128)
    m_sl = slopes8()

    with ExitStack() as st:
        em = Emitter(nc, st)
        ar = Arena(em, 212000)
        ps_all = st.enter_context(nc.psum_tensor("ps_all", [128, 4096], F32))
        banks = [em.tile("bank%d" % i, ps_all[:, 512 * i:512 * (i + 1)]) for i in range(8)]

        def done():
            em.barrier()
            em.emit()
            return nc

        def dump(name, t, ap):
            if name not in dbg_out:
                return
            em.dma("sp", dbg_out[name], ap, t, False)

        cst = ar.alloc("cst", [128, NCST], F32)
        ident = ar.alloc("ident", [128, 128], BF16)
        i32 = ar.alloc("i32", [128, 128], F32)
        small = ar.alloc("small", [128, 64], F32)
        neghalf = ar.alloc("neghalf", [128, 16], F32)
        sublnc = ar.alloc("sublnc", [128, 128], F32)
        ones32 = ar.alloc("ones32", [128, 64], F32)
        em.dma("sp", cst[:], dr["cst"], cst, True)
        em.v("pool", "memset", [i32], [], ap=i32[:], constant=0.0)
        em.op("pool", lambda e: e.affine_select(out=i32[:], in_=i32[:], pattern=[[-1, 128]],
                                                 compare_op=ALU.not_equal, fill=1.0, base=0,
                                                 channel_multiplier=1), reads=[i32], writes=[i32])
        em.v("dve", "tensor_copy", [ident], [i32], out=ident[:], in_=i32[:])
        em.v("pool", "memset", [neghalf], [], ap=neghalf[:], constant=-0.5)
        em.v("pool", "memset", [ones32], [], ap=ones32[:], constant=1.0)
        L0 = CST["lam4"]
        lamt = ar.alloc("lamt", [128, 128], F32)
        em.v("dve", "tensor_tensor", [lamt], [cst], out=lamt[:, 0:64], in0=cst[:, L0:L0 + 64],
             in1=cst[:, L0 + 64:L0 + 128], op=ALU.mult)
        em.v("dve", "tensor_tensor", [lamt], [cst, lamt], out=lamt[:, 64:128], in0=cst[:, L0 + 128:L0 + 192],
             in1=cst[:, L0 + 192:L0 + 256], op=ALU.mult)
        em.v("dve", "reduce_sum", [small], [lamt], out=small[:, 1:2], in_=lamt[:, 0:64], axis=AX.X)
        em.v("dve", "reduce_sum", [small], [lamt, small], out=small[:, 2:3], in_=lamt[:, 64:128], axis=AX.X)
        em.act(small, small[:, 3:5], small[:, 1:3], AF.Exp, [small])
        em.v("dve", "tensor_tensor", [small], [small], out=small[:, 5:6], in0=small[:, 4:5], in1=small[:, 3:4],
             op=ALU.subtract)
        em.v("dve", "tensor_scalar", [small], [small], out=small[:, 0:1], in0=small[:, 5:6],
             scalar1=-LAMBDA_INIT, scalar2=None, op0=ALU.add)
        em.v("dve", "tensor_scalar", [sublnc], [cst], out=sublnc[:], in0=cst[:, CST["subln"]:CST["subln"] + 128],
             scalar1=C_SUBLN, scalar2=None, op0=ALU.mult)
        neglam = small[:, 0:1]
        flag_ap = cst[:, CST["flag"]:CST["flag"] + 1]
        BA = CST["biasA"]

        M0 = (ar.lo, ar.hi)
        xT_all = ar.alloc("xT_bf", [128, 8, S], BF16)
        xTv = []
        for i in range(8):
            v_ = em.tile("xTv%d" % i, xT_all.ap)
            em.dma("pool", v_[:, :, i * 512:(i + 1) * 512], xT_r[:, :, i * 512:(i + 1) * 512], v_, True)
            xTv.append(v_)
        xT = xT_all.ap

        def xview(ctx0):
            return xTv[ctx0 // 512]

        oaT = ar.alloc("oaT", [128, 8, NTO], BF16)
        obT = ar.alloc("obT", [128, 4, NTO], BF16)
        M1 = (ar.lo, ar.hi)
        maskAt = ar.alloc("maskA", [128, 128], BF16)
        em.dma("pool", maskAt[:], dr["tabB"][:, 73 * 128:74 * 128], maskAt, True)
        maskA = maskAt.ap
        Qp = [ar.alloc("Qp%d" % m, [128, NTO], BF16) for m in range(2)]
        KT = ar.alloc("KT", [128, S], BF16)
        VV = ar.alloc("VV", [128, 32 * 4 * 129 + 128], BF16)
        V2 = VV.ap[:, 0:32 * 2 * 129].rearrange("p (b h e) -> p b h e", b=32, h=2)
        em.v("pool", "memset", [VV], [], ap=V2[:, :, :, 128:129], constant=1.0)
        v2end = VV.off + 32 * 2 * 129 * 2
        KTb = ar.alloc("KTb", [128, S], BF16, at=v2end)
        Qpb = [ar.alloc("Qpb%d" % m, [128, NTO], BF16, at=v2end + 8192 + m * 4104) for m in range(2)]
        assert v2end + 8192 + 2 * 4104 <= VV.off + VV.nbytes
        Wq = ar.alloc("Wq", [128, 8, 128], BF16)
        Wk = ar.alloc("Wk", [128, 8, 128], BF16)
        Wv = ar.alloc("Wv", [128, 8, 512], BF16)
        Wv2t = ar.alloc("Wv2", [128, 8, 256], BF16, at=Wv.off)
        Wv2 = Wv2t.ap
        Wqb_ = ar.alloc("WqB", [128, 8, 128], BF16, at=Wv.off + 4096)
        Wkb_ = ar.alloc("WkB", [128, 8, 128], BF16, at=Wv.off + 6144)
        KTs = [KT, KTb]
        Qps = [Qp, Qpb]
        Wqs = [Wq, Wqb_]
        Wks = [Wk, Wkb_]
        for qq in Qps:
            em.v("pool", "memset", [qq[0]], [], ap=qq[0][64:128, :], constant=0.0)
            em.v("pool", "memset", [qq[1]], [], ap=qq[1][0:64, :], constant=0.0)
        QT = [(0, 2, 2046)] + [(2 + 512 * i, 512, 2048 + 512 * i) for i in range(4)]

        mA = (ar.lo, ar.hi)
        P = [ar.alloc("P%d" % b, [128, 2, 512], BF16) for b in range(2)]
        tA = ar.alloc("tA", [128, 4, 128], F32)
        tO = ar.alloc("tO", [128, 4, 128], F32)
        oab = ar.alloc("oab", [128, 4, 128], BF16)
        rc = ar.alloc("rc", [128, 24], F32)
        Uc = ar.alloc("Uc", [128, 8, 129], F32)
        Sb = [[banks[0], banks[2]], [banks[1], banks[3]]]
        Spair = [ps_all[:, 1024 * b:1024 * (b + 1)].rearrange("p (m c) -> p m c", m=2) for b in range(2)]
        Ub = [banks[4], banks[5], banks[6]]
        X = banks[7]
        Uv = []
        Ut = []
        for u in range(8):
            bk, slot = Ub[u // 3], u % 3
            Uv.append(bk.ap[:, slot * 129:(slot + 1) * 129])
            Ut.append(bk)
        Xbf = X.ap.bitcast(BF16)
        pool5 = [banks[0], banks[1], banks[2], banks[3], banks[7]]
        rot = [0]

        def nextbank():
            b = pool5[rot[0] % 5]
            rot[0] += 1
            return b

        def kq_units(h, bankfn):
            pb = h % 2
            kt, qp, wq, wk = KTs[pb], Qps[pb], Wqs[pb], Wks[pb]
            units = []

            def wload():
                em.dma("pool", wq[:], w_in_r[:, :, h * 128:(h + 1) * 128], wq, True)
                em.dma("pool", wk[:], w_in_r[:, :, 1024 + h * 128:1024 + (h + 1) * 128], wk, True)
            units.append(wload)
            for tt in range(8):
                def ku(tt=tt):
                    bk = bankfn()
                    for kc in range(8):
                        em.mm(bk, bk[:, 0:512], wk[:, kc, :], xT[:, kc, tt * 512:(tt + 1) * 512], kc == 0, kc == 7,
                              [wk, xTv[tt]])
                    em.v("dve", "tensor_copy", [kt], [bk], out=kt[:, tt * 512:(tt + 1) * 512], in_=bk[:, 0:512])
                units.append(ku)
            for (t0, w, c0) in QT:
                def qu(t0=t0, w=w, c0=c0):
                    bk = bankfn()
                    for kc in range(8):
                        em.mm(bk, bk[:, 0:w], wq[:, kc, :], xT[:, kc, c0:c0 + w], kc == 0, kc == 7, [wq, xview(c0)])
                    em.v("dve", "tensor_scalar", [qp[0]], [bk], out=qp[0][:, t0:t0 + w], in0=bk[:, 0:w],
                         scalar1=0.125, scalar2=None, op0=ALU.mult)
                units.append(qu)
            return units

        for u_ in kq_units(0, nextbank):
            u_()
        deferred = []
        for h in range(8):
            hh = h % 2
            KTh, Qph = KTs[h % 2], Qps[h % 2]
            if hh == 0:
                em.dma("pool", Wv2, w_in_r[:, :, 2048 + h * 128:2048 + (h + 2) * 128], Wv2t, True)
                for blk in range(32):
                    bk = nextbank()
                    for kc in range(8):
                        em.mm(bk, bk[:, 0:256], xT[:, kc, blk * 128:(blk + 1) * 128], Wv2[:, kc, :], kc == 0, kc == 7,
                              [Wv2t, xview(blk * 128)])
                    em.v("dve", "tensor_copy", [VV], [bk], out=V2[:, blk, :, 0:128],
                         in_=bk[:, 0:256].rearrange("p (h e) -> p h e", h=2))
            pending = kq_units(h + 1, lambda: X) if h < 7 else []
            plan = []
            for r in range(5):
                t0, w, _c = QT[r]
                if r == 0:
                    steps = [(j, 0, False) for j in range(15)] + [(15, 0, True)]
                    nfull = 15
                else:
                    nfull = 16 + 4 * (r - 1)
                    steps = [(j, 0, False) for j in range(nfull)] + [(nfull + d, 128 * d, True) for d in range(4)]
                ns = nsub(r, h)
                ws = w // ns
                qc0 = 2047.0 if r == 0 else 2048 + 512 * (r - 1) + ws / 2.0
                keep = [st_ for st_ in steps if st_[2] or m_sl[h] * (qc0 - (128 * st_[0] + 127)) <= SKIP_MARGIN]
                if not any(not st_[2] for st_ in keep):
                    keep = [st_ for st_ in steps if not st_[2]][-1:] + keep
                plan.append((keep, nfull, ns, ws))
            total_steps = sum(len(p_[0]) for p_ in plan)
            rate = len(pending) / float(max(total_steps, 1))
            credit = 0.0
            for r in range(5):
                t0, w, _c = QT[r]
                steps, nfull, ns, ws = plan[r]
                if r == 0:
                    dw, mc0, nblk, nq = 2, 126, 1, 2
                else:
                    dw, mc0, nblk, nq = 128, 0, 4, 128

                def QK(si):
                    j, c0, diag = steps[si]
                    buf = si % 2
                    for m in range(2):
                        sb = Sb[m][buf]
                        em.mm(sb, sb[:, c0:w], KTh[64 * m:64 * m + 64, j * 128:(j + 1) * 128],
                              Qph[0][64 * m:64 * m + 64, t0 + c0:t0 + w], True, True, [KTh, Qph[0]])

                def EXP(si):
                    j, c0, diag = steps[si]
                    buf = si % 2
                    for s_ in range(ns):
                        a, b_ = max(s_ * ws, c0), (s_ + 1) * ws
                        if b_ <= a:
                            continue
                        col = BA + BIAS_COLS[(r, j, h, s_)]
                        em.act(P[buf], P[buf][:, :, a:b_], Spair[buf][:, :, a:b_], AF.Exp,
                               [Sb[0][buf], Sb[1][buf], cst], bias=cst[:, col:col + 1])

                def PV(si):
                    j, c0, diag = steps[si]
                    buf = si % 2
                    if diag:
                        em.v("pool", "tensor_tensor", [P[buf]], [P[buf], maskAt], out=P[buf][:, :, c0:c0 + dw],
                             in0=P[buf][:, :, c0:c0 + dw],
                             in1=maskA[:, mc0:mc0 + dw].unsqueeze(1).to_broadcast([128, 2, dw]), op=ALU.mult)
                    seen = set()
                    for m in range(2):
                        for b_ in range(nblk):
                            if 128 * b_ < c0:
                                continue
                            u = m * 4 + b_
                            last = nfull + b_
                            first = (si == 0 and (u // 3) not in seen)
                            seen.add(u // 3)
                            em.mm(Ut[u], Uv[u][0:nq, 0:129], P[buf][:, m, 128 * b_:128 * b_ + nq],
                                  V2[:, j, hh, :], first, j == last, [P[buf], VV], skip=True)

                QK(0)
                for si in range(len(steps)):
                    EXP(si)
                    if si + 1 < len(steps):
                        QK(si + 1)
                    PV(si)
                    credit += rate
                    while credit >= 1.0 and pending:
                        credit -= 1.0
                        pending.pop(0)()
                for d_ in deferred:
                    d_()
                deferred = []
                nb = nblk
                Ucv = Uc.ap
                if nb == 4:
                    for bkI, (ua, ub) in enumerate(((0, 3), (3, 6), (6, 8))):
                        em.v("dve", "tensor_copy", [Uc], [Ub[bkI]],
                             out=Ucv[:, ua:ub, :].rearrange("p a b -> p (a b)"), in_=Ub[bkI][:, 0:(ub - ua) * 129])
                else:
                    em.v("dve", "tensor_copy", [Uc], [Ut[0]], out=Ucv[0:nq, 0, :], in_=Uv[0][0:nq, :])
                    em.v("dve", "tensor_copy", [Uc], [Ut[4]], out=Ucv[0:nq, 4, :], in_=Uv[4][0:nq, :])
                r0 = rc[0:nq, 0:nb]
                r1 = rc[0:nq, 4:4 + nb]
                em.v("dve", "reciprocal", [rc], [Uc], out=r0, in_=Ucv[0:nq, 0:nb, 128])
                em.v("dve", "reciprocal", [rc], [Uc, rc], out=r1, in_=Ucv[0:nq, 4:4 + nb, 128])
                a4 = tA[0:nq, 0:nb, :]
                o4 = tO[0:nq, 0:nb, :]
                bc = lambda ap2, nq=nq, nb=nb: ap2.unsqueeze(2).to_broadcast([nq, nb, 128])
                em.v("dve", "scalar_tensor_tensor", [tA], [Uc, rc, small], out=a4, in0=Ucv[0:nq, 4:4 + nb, 0:128],
                     scalar=neglam[0:nq, :], in1=bc(r1), op0=ALU.mult, op1=ALU.mult)
                em.v("dve", "tensor_tensor", [tO], [Uc, rc], out=o4, in0=Ucv[0:nq, 0:nb, 0:128], in1=bc(r0), op=ALU.mult)
                em.v("dve", "tensor_tensor", [tO], [tO, tA], out=o4, in0=o4, in1=a4, op=ALU.add)
                em.v("dve", "tensor_tensor", [tA], [tO], out=a4, in0=o4, in1=o4, op=ALU.mult)
                em.v("dve", "reduce_sum", [rc], [tA], out=rc[0:nq, 8:8 + nb], in_=a4, axis=AX.X)
                em.v("dve", "tensor_scalar", [rc], [rc], out=rc[0:nq, 12:12 + nb], in0=rc[0:nq, 8:8 + nb],
                     scalar1=1.0 / 128, scalar2=EPS, op0=ALU.mult, op1=ALU.add)
                em.v("pool", "tensor_tensor", [rc], [rc, neghalf], out=rc[0:nq, 16:16 + nb], in0=rc[0:nq, 12:12 + nb],
                     in1=neghalf[0:nq, 0:nb], op=ALU.pow)
                em.v("dve", "tensor_tensor", [tO], [tO, rc], out=o4, in0=o4, in1=bc(rc[0:nq, 16:16 + nb]), op=ALU.mult)
                em.v("dve", "tensor_tensor", [oab], [tO, sublnc], out=oab[0:nq, 0:nb, :], in0=o4,
                     in1=sublnc[0:nq, :].unsqueeze(1).to_broadcast([nq, nb, 128]), op=ALU.mult)

                def part2(nq=nq, nb=nb, t0=t0, h=h):
                    for b_ in range(nb):
                        em.op("pe", lambda e, b_=b_: e.transpose(Xbf[:, b_ * nq:(b_ + 1) * nq], oab[0:nq, b_, :],
                                                                 ident[0:nq, 0:nq]),
                              reads=[oab, ident], writes=[X])
                    em.v("dve", "tensor_copy", [oaT], [X], out=oaT[:, h, t0:t0 + nb * nq], in_=Xbf[:, 0:nb * nq])
                deferred.append(part2)
            while pending:
                pending.pop(0)()
        for d_ in deferred:
            d_()
        def dump3(name, src, nchunk):
            if name not in dbg_out:
                return
            em.barrier()
            stg = VV.ap[:, 0:2 * NTO].bitcast(F32)
            for c_ in range(nchunk):
                em.v("dve", "tensor_copy", [VV], [src], out=stg, in_=src[:, c_, :])
                em.dma("sp", dbg_out[name][:, c_, :], stg, VV, False)
            em.barrier()

        dump3("oaT", oaT, 8)
        if stop_after == "A":
            return done()
        em.barrier()
        ar.lo, ar.hi = mA

        tabB = ar.alloc("tabB", [128, 72, 128], BF16)
        em.dma("pool", tabB[:], dr["tabB"][:, 0:72 * 128].rearrange("p (t q) -> p t q", t=72), tabB, True)
        em.v("pool", "memset", [Qp[0]], [], ap=Qp[0][64:128, :], constant=0.0)
        em.v("pool", "memset", [Qp[1]], [], ap=Qp[1][0:64, :], constant=0.0)
        Pb = [ar.alloc("Pb%d" % b, [128, 2, 128], BF16) for b in range(2)]
        shiftI = ar.alloc("shiftI", [128, 128], BF16)
        em.v("pool", "memset", [shiftI], [], ap=shiftI[:], constant=0.0)
        em.v("dve", "tensor_copy", [shiftI], [ident], out=shiftI[0:64, 64:128], in_=ident[0:64, 0:64])
        vtail = VV.off + 8320
        acc = [ar.alloc("acc%d" % i, [128, NTO], F32, at=vtail + i * 8200) for i in range(2)]
        rE = ar.alloc("rE", [128, NTO], F32, at=vtail + 2 * 8200)
        assert vtail + 3 * 8200 <= VV.off + VV.nbytes
        obtmp = ar.alloc("obtmp", [128, NTO], BF16, at=Wv.off)
        Wvb = ar.alloc("Wvb", [128, 8, 128], BF16, at=Wv.off + 4160)
        assert 4160 + 2048 <= Wv.nbytes
        Vp = VV.ap[:, 0:32 * 2 * 65].rearrange("p (b h e) -> p b h e", b=32, h=2)
        Wqb, Wkb = Wq, Wk
        SbB = [banks[0], banks[1]]
        ObB = [banks[2], banks[3]]
        poolB = [banks[4], banks[5], banks[6], banks[7]]
        rotB = [0]

        def nextbankB():
            b = poolB[rotB[0] % 4]
            rotB[0] += 1
            return b

        def chunk_tokens(g, c, r_):
            dil = GROUP_DIL[g]
            s0 = 128 * dil * c + r_
            return s0, dil

        for hp in range(4):
            for g in range(3):
                dil = GROUP_DIL[g]
                base = 3072 + g * 512 + hp * 128
                em.dma("pool", Wqb[:], w_in_r[:, :, base:base + 128], Wqb, True)
                em.dma("pool", Wkb[:], w_in_r[:, :, base + 1536:base + 1536 + 128], Wkb, True)
                em.dma("pool", Wvb[:], w_in_r[:, :, base + 3072:base + 3072 + 128], Wvb, True)
                first_tok = {0: 14 * 128, 1: 2 * 512, 2: 0}[g]
                for tt in range(first_tok // 512, 8):
                    bk = nextbankB()
                    for kc in range(8):
                        em.mm(bk, bk[:, 0:512], Wkb[:, kc, :], xT[:, kc, tt * 512:(tt + 1) * 512], kc == 0, kc == 7,
                              [Wkb, xTv[tt]])
                    em.v("dve", "tensor_copy", [KT], [bk], out=KT[:, tt * 512:(tt + 1) * 512], in_=bk[:, 0:512])
                for (t0, w, c0) in QT:
                    bk = nextbankB()
                    for kc in range(8):
                        em.mm(bk, bk[:, 0:w], Wqb[:, kc, :], xT[:, kc, c0:c0 + w], kc == 0, kc == 7, [Wqb, xview(c0)])
                    em.v("dve", "tensor_scalar", [Qp[0]], [bk], out=Qp[0][0:64, t0:t0 + w], in0=bk[0:64, 0:w],
                         scalar1=0.125, scalar2=None, op0=ALU.mult)
                    em.v("dve", "tensor_scalar", [Qp[1]], [bk], out=Qp[1][64:128, t0:t0 + w], in0=bk[64:128, 0:w],
                         scalar1=0.125, scalar2=None, op0=ALU.mult)
                if g == 0 and hp == 0:
                    pass
                for cid in range(32):
                    c_, r_ = cid // dil, cid % dil
                    s0 = 128 * dil * c_ + r_
                    if 128 * dil * c_ < first_tok:
                        continue
                    bk = nextbankB()
                    for kc in range(8):
                        em.mm(bk, bk[:, 0:128], xT[:, kc, s0:s0 + 127 * dil + 1:dil], Wvb[:, kc, :], kc == 0, kc == 7,
                              [Wvb] + xTv)
                    em.v("dve", "tensor_copy", [VV], [bk], out=Vp[:, cid, :, 0:64],
                         in_=bk[:, 0:128].rearrange("p (h e) -> p h e", h=2))
                if g == 0 and hp == 0:
                    em.v("pool", "memset", [VV], [], ap=Vp[:, :, :, 64:65], constant=1.0)
                items = []
                if g == 0:
                    items.append((15, 126, 128, 0, 1, 1))
                    for c_ in range(16, 32):
                        items.append((c_, 0, 128, 2 + 128 * (c_ - 16), 1, 2 if c_ == 16 else 1))
                elif g == 1:
                    items.append((3 * 4 + 2, 127, 128, 0, 1, 1))
                    items.append((3 * 4 + 3, 127, 128, 1, 1, 1))
                    for c_ in range(4, 8):
                        for r_ in range(4):
                            items.append((c_ * 4 + r_, 0, 128, 2 + 512 * (c_ - 4) + r_, 4, 2 if c_ == 4 else 1))
                else:
                    items.append((14, 127, 128, 0, 1, None))
                    items.append((15, 127, 128, 1, 1, None))
                    for r_ in range(16):
                        items.append((16 + r_, 0, 128, 2 + r_, 16, 2))
                for hd in range(2):
                    head = 2 * hp + hd
                    tb = (g * 8 + head) * 3

                    def kchunk(cid):
                        c_, r_ = cid // dil, cid % dil
                        s0 = 128 * dil * c_ + r_
                        return KT[:, s0:s0 + 127 * dil + 1:dil]

                    def QKb(ii):
                        cid, i0, i1, tq, tstep, pk = items[ii]
                        n = i1 - i0
                        sb = SbB[ii % 2]
                        qa = Qp[hd][:, tq:tq + (n - 1) * tstep + 1:tstep]
                        if pk is not None:
                            em.mm(sb, sb[:, 0:n], kchunk(cid - dil), qa, True, False, [KT, Qp[hd]])
                            em.mm(sb, sb[:, 0:n], ident[:], tabB[:, tb + pk, i0:i1], False, True, [ident, tabB])
                        em.mm(sb, sb[:, 128:128 + n], kchunk(cid), qa, True, False, [KT, Qp[hd]])
                        em.mm(sb, sb[:, 128:128 + n], ident[:], tabB[:, tb + 0, i0:i1], False, True, [ident, tabB])

                    def EXPb(ii):
                        cid, i0, i1, tq, tstep, pk = items[ii]
                        n = i1 - i0
                        sb = SbB[ii % 2]
                        pb = Pb[ii % 2]
                        if pk is not None and n == 128:
                            em.act(pb, pb[:].rearrange("p a b -> p (a b)"), sb[:, 0:256], AF.Exp, [sb])
                        else:
                            if pk is not None:
                                em.act(pb, pb[:, 0, 0:n], sb[:, 0:n], AF.Exp, [sb])
                            em.act(pb, pb[:, 1, 0:n], sb[:, 128:128 + n], AF.Exp, [sb])

                    def PVb(ii):
                        cid, i0, i1, tq, tstep, pk = items[ii]
                        n = i1 - i0
                        pb = Pb[ii % 2]
                        ob = ObB[ii % 2]
                        if pk is not None:
                            em.mm(ob, ob[0:65, 0:n], Vp[:, cid - dil, hd, :], pb[:, 0, 0:n], True, False, [VV, pb])
                        em.mm(ob, ob[0:65, 0:n], Vp[:, cid, hd, :], pb[:, 1, 0:n], pk is None, True, [VV, pb])
                        dst = acc[hd][0:65, tq:tq + (n - 1) * tstep + 1:tstep]
                        if g == 0:
                            em.v("dve", "tensor_copy", [acc[hd]], [ob], out=dst, in_=ob[0:65, 0:n])
                        else:
                            em.v("dve", "tensor_tensor", [acc[hd]], [ob, acc[hd]], out=dst, in0=dst, in1=ob[0:65, 0:n],
                                 op=ALU.add)

                    QKb(0)
                    for ii in range(len(items)):
                        EXPb(ii)
                        if ii + 1 < len(items):
                            QKb(ii + 1)
                        PVb(ii)
            for hd in range(2):
                em.act(rE, rE[64:65, :], acc[hd][64:65, :], AF.Ln, [acc[hd]])
                em.act(rE, rE[64:65, :], rE[64:65, :], AF.Exp, [rE], scale=-1.0)
                for c0 in range(0, NTO, 512):
                    w = min(512, NTO - c0)
                    bk = nextbankB()
                    em.mm(bk, bk[0:64, 0:w], ones32[64:65, :], rE[64:65, c0:c0 + w], True, True, [ones32, rE])
                    if hd == 0:
                        em.v("dve", "tensor_tensor", [obT], [bk, acc[hd]], out=obT[0:64, hp, c0:c0 + w],
                             in0=acc[hd][0:64, c0:c0 + w], in1=bk[0:64, 0:w], op=ALU.mult)
                    else:
                        em.v("dve", "tensor_tensor", [obtmp], [bk, acc[hd]], out=obtmp[0:64, c0:c0 + w],
                             in0=acc[hd][0:64, c0:c0 + w], in1=bk[0:64, 0:w], op=ALU.mult)
                if hd == 1:
                    for c0 in range(0, NTO, 512):
                        w = min(512, NTO - c0)
                        bk = nextbankB()
                        em.mm(bk, bk[:, 0:w], shiftI[0:64, :], obtmp[0:64, c0:c0 + w], True, True, [shiftI, obtmp])
                        em.v("dve", "tensor_copy", [obT], [bk], out=obT[64:128, hp, c0:c0 + w], in_=bk[64:128, 0:w])
        dump3("obT", obT, 4)
        if stop_after == "B":
            return done()
        em.barrier()
        ar.lo, ar.hi = M1

        w_pa_r = dr["w_pa"].rearrange("(kc p) n -> p kc n", p=128)
        w_pb_r = dr["w_pb"].rearrange("(kc p) n -> p kc n", p=128)
        w_o_r = dr["w_o"].rearrange("(kc p) n -> p kc n", p=128)
        w_up_r = dr["w_up"].rearrange("(kc p) n -> p kc n", p=128)
        w_down_r = dr["w_down"].rearrange("(c p) n -> p c n", p=128)
        yT = ar.alloc("yT", [128, 8, NTO], BF16, top=True)
        WgA = [ar.alloc("WgA%d" % i, [128, 8, 128], BF16) for i in range(2)]
        WgB = [ar.alloc("WgB%d" % i, [128, 8, 128], BF16) for i in range(2)]
        Wpa = [ar.alloc("Wpa%d" % i, [128, 8, 128], BF16) for i in range(2)]
        Wpb = [ar.alloc("Wpb%d" % i, [128, 4, 128], BF16) for i in range(2)]
        g0s = [ar.alloc("g0s%d" % i, [128, 512], F32) for i in range(2)]
        g1s = [ar.alloc("g1s%d" % i, [128, 512], F32) for i in range(2)]
        tt = ar.alloc("tt", [128, 512], F32)
        tu = ar.alloc("tu", [128, 512], F32)
        BG = CST["bg"]
        it = 0
        for c in range(8):
            bi = c % 2
            em.dma("pool", WgA[bi][:], w_in_r[:, :, 7680 + c * 128:7680 + (c + 1) * 128], WgA[bi], True)
            em.dma("pool", WgB[bi][:], w_in_r[:, :, 8704 + c * 128:8704 + (c + 1) * 128], WgB[bi], True)
            em.dma("pool", Wpa[bi][:], w_pa_r[:, :, c * 128:(c + 1) * 128], Wpa[bi], True)
            em.dma("pool", Wpb[bi][:], w_pb_r[:, :, c * 128:(c + 1) * 128], Wpb[bi], True)
            for (t0, w, c0) in QT:
                st_ = it % 2
                it += 1
                G0, G1, PA, PB = banks[st_ * 4:st_ * 4 + 4]
                for kc in range(8):
                    em.mm(G0, G0[:, 0:w], WgA[bi][:, kc, :], xT[:, kc, c0:c0 + w], kc == 0, kc == 7, [WgA[bi], xview(c0)])
                for kc in range(8):
                    em.mm(G1, G1[:, 0:w], WgB[bi][:, kc, :], xT[:, kc, c0:c0 + w], kc == 0, kc == 7, [WgB[bi], xview(c0)])
                for fc in range(8):
                    em.mm(PA, PA[:, 0:w], Wpa[bi][:, fc, :], oaT[:, fc, t0:t0 + w], fc == 0, fc == 7, [Wpa[bi], oaT])
                for fc in range(4):
                    em.mm(PB, PB[:, 0:w], Wpb[bi][:, fc, :], obT[:, fc, t0:t0 + w], fc == 0, fc == 3, [Wpb[bi], obT])
                em.act(g0s[st_], g0s[st_][:, 0:w], G0[:, 0:w], AF.Sigmoid, [G0, cst], bias=cst[:, BG + c:BG + c + 1])
                em.act(g1s[st_], g1s[st_][:, 0:w], G1[:, 0:w], AF.Sigmoid, [G1, cst], bias=cst[:, BG + 8 + c:BG + 9 + c])
                em.v("dve", "tensor_tensor", [tt], [g0s[st_], PA], out=tt[:, 0:w], in0=g0s[st_][:, 0:w], in1=PA[:, 0:w],
                     op=ALU.mult)
                em.v("dve", "tensor_tensor", [tu], [g1s[st_], PB], out=tu[:, 0:w], in0=g1s[st_][:, 0:w], in1=PB[:, 0:w],
                     op=ALU.mult)
                em.v("dve", "tensor_tensor", [yT], [tt, tu], out=yT[:, c, t0:t0 + w], in0=tt[:, 0:w], in1=tu[:, 0:w],
                     op=ALU.add)
        dump3("yT", yT, 8)
        if stop_after == "C1":
            return done()
        em.barrier()
        ar.lo = M0[0]

        x1T = ar.alloc("x1T", [128, 8, NTO], BF16)
        M2 = ar.lo
        w_o = ar.alloc("w_o", [128, 8, D], BF16)
        lnp = ar.alloc("lnp", [128, 2, D], F32)
        xo = [ar.alloc("xo%d" % i, [128, D], F32) for i in range(2)]
        rrs = [ar.alloc("rr%d" % i, [128, D], F32) for i in range(2)]
        xns = [ar.alloc("xn%d" % i, [128, D], F32) for i in range(2)]
        stts = [ar.alloc("stt%d" % i, [128, 2, 6], F32) for i in range(2)]
        mvs = [ar.alloc("mv%d" % i, [128, 8], F32) for i in range(2)]
        x1t = [ar.alloc("x1t%d" % i, [128, D], F32) for i in range(2)]
        x1bs = [ar.alloc("x1b%d" % i, [128, D], BF16) for i in range(2)]
        stt = ar.alloc("stt", [128, 2, 6], F32)
        mv = ar.alloc("mv", [128, 8], F32)
        for hf in range(2):
            em.dma("pool", w_o[:, :, hf * 512:(hf + 1) * 512], w_o_r[:, :, hf * 512:(hf + 1) * 512], w_o, True, group="wo")
        em.dma("sp", lnp[:].rearrange("p a b -> p (a b)"), dr["lnp"][:, 0:2 * D], lnp, True)

        def layer_norm(src, xn, dst, nq, lnt, stt_, mv_):
            for c_ in range(2):
                em.v("dve", "bn_stats", [stt_], [src], out=stt_[0:nq, c_, :], in_=src[0:nq, c_ * 512:(c_ + 1) * 512])
            em.v("dve", "bn_aggr", [mv_], [stt_], out=mv_[0:nq, 0:2], in_=stt_[0:nq].rearrange("p c s -> p (c s)"))
            em.v("dve", "tensor_scalar", [mv_], [mv_], out=mv_[0:nq, 2:3], in0=mv_[0:nq, 1:2], scalar1=EPS, scalar2=None,
                 op0=ALU.add)
            em.v("pool", "tensor_tensor", [mv_], [mv_, neghalf], out=mv_[0:nq, 3:4], in0=mv_[0:nq, 2:3],
                 in1=neghalf[0:nq, 0:1], op=ALU.pow)
            em.v("dve", "tensor_scalar", [mv_], [mv_], out=mv_[0:nq, 4:5], in0=mv_[0:nq, 0:1], scalar1=mv_[0:nq, 3:4],
                 scalar2=-1.0, op0=ALU.mult, op1=ALU.mult)
            em.act(xn, xn[0:nq, :], src[0:nq, :], AF.Identity, [src, mv_], bias=mv_[0:nq, 4:5], scale=mv_[0:nq, 3:4])
            em.v("dve", "tensor_tensor", [xn], [xn, lnt], out=xn[0:nq, :], in0=xn[0:nq, :], in1=lnt[0:nq, 0, :],
                 op=ALU.mult)
            em.v("pool", "tensor_tensor", [dst], [xn, lnt], out=dst[0:nq, :], in0=xn[0:nq, :], in1=lnt[0:nq, 1, :],
                 op=ALU.add)

        BLK = [(0, 2)] + [(2 + 128 * i, 128) for i in range(16)]
        def c2_z(bi):
            t0, nq = BLK[bi]
            s_ = bi % 2
            Z = [banks[s_ * 2], banks[s_ * 2 + 1]]
            em.dma("sp", xo[s_][0:nq, :], dr["xown"][t0:t0 + nq, :], xo[s_], True)
            for n in range(2):
                for cc in range(8):
                    em.mm(Z[n], Z[n][0:nq, 0:512], yT[:, cc, t0:t0 + nq], w_o[:, cc, n * 512:(n + 1) * 512], cc == 0,
                          cc == 7, [yT, w_o])

        def c2_ln(bi):
            t0, nq = BLK[bi]
            s_ = bi % 2
            Z = [banks[s_ * 2], banks[s_ * 2 + 1]]
            for n in range(2):
                em.v("dve", "scalar_tensor_tensor", [rrs[s_]], [xo[s_], Z[n]], out=rrs[s_][0:nq, n * 512:(n + 1) * 512],
                     in0=xo[s_][0:nq, n * 512:(n + 1) * 512], scalar=ALPHA, in1=Z[n][0:nq, 0:512], op0=ALU.mult,
                     op1=ALU.add)
            layer_norm(rrs[s_], xns[s_], x1t[s_], nq, lnp, stts[s_], mvs[s_])
            if nq == 128:
                em.dma("sp", x1s[t0 - 2:t0 - 2 + 128, :], x1t[s_][:], x1t[s_], False)
            em.act(x1bs[s_], x1bs[s_][0:nq, :], x1t[s_][0:nq, :], AF.Copy, [x1t[s_]])

        def c2_tr(bi):
            t0, nq = BLK[bi]
            s_ = bi % 2
            TR = banks[4 + s_]
            TRbf = TR.ap.bitcast(BF16)
            x1b = x1bs[s_]
            for cc in range(8):
                em.op("pe", lambda e, cc=cc, nq=nq, TRbf=TRbf, x1b=x1b: e.transpose(TRbf[:, cc * nq:(cc + 1) * nq],
                                                                                     x1b[0:nq, cc * 128:(cc + 1) * 128],
                                                                                     ident[0:nq, 0:nq]),
                      reads=[x1b, ident], writes=[TR])
            em.v("dve", "tensor_copy", [x1T], [TR], out=x1T[:, :, t0:t0 + nq],
                 in_=TRbf[:, 0:8 * nq].rearrange("p (c q) -> p c q", c=8))

        c2_z(0)
        for bi in range(len(BLK)):
            if bi + 1 < len(BLK):
                c2_z(bi + 1)
            c2_ln(bi)
            if bi >= 1:
                c2_tr(bi - 1)
        c2_tr(len(BLK) - 1)
        dump3("x1T", x1T, 8)
        if stop_after == "C2":
            return done()
        em.barrier()
        ar.lo = M2
        ar.hi = M0[1]

        fT = ar.alloc("fT", [128, NCP, NT], BF16)
        M3 = ar.lo
        hA = [ar.alloc("hA%d" % i, [128, NTO], F32) for i in range(2)]
        hG = [ar.alloc("hG%d" % i, [128, NTO], F32) for i in range(2)]
        ca = ar.alloc("ca", [128, NT], F32)
        cg = ar.alloc("cg", [128, NT], F32)
        WuA = [ar.alloc("WuA%d" % i, [128, 8, 128], BF16) for i in range(2)]
        WuG = [ar.alloc("WuG%d" % i, [128, 8, 128], BF16) for i in range(2)]
        UT = [(0, 512), (512, 512), (1024, 512), (1536, 512), (2048, 2)]
        WC, BC = CST["wc"], CST["bc"]
        it = 0
        for cp in range(NCP):
            s_ = cp % 2
            em.dma("pool", WuA[s_][:], w_up_r[:, :, cp * 128:(cp + 1) * 128], WuA[s_], True)
            em.dma("pool", WuG[s_][:], w_up_r[:, :, DFF + cp * 128:DFF + (cp + 1) * 128], WuG[s_], True)
            for (u0, w) in UT:
                bs = it % 4
                it += 1
                A_, G_ = banks[2 * bs], banks[2 * bs + 1]
                for kc in range(8):
                    em.mm(A_, A_[:, 0:w], WuA[s_][:, kc, :], x1T[:, kc, u0:u0 + w], kc == 0, kc == 7, [WuA[s_], x1T])
                for kc in range(8):
                    em.mm(G_, G_[:, 0:w], WuG[s_][:, kc, :], x1T[:, kc, u0:u0 + w], kc == 0, kc == 7, [WuG[s_], x1T])
                em.act(hA[s_], hA[s_][:, u0:u0 + w], A_[:, 0:w], AF.Copy, [A_])
                em.act(hG[s_], hG[s_][:, u0:u0 + w], G_[:, 0:w], AF.Copy, [G_])
            for (hb, cc_, col) in ((hA[s_], ca, cp), (hG[s_], cg, NCP + cp)):
                em.v("dve", "tensor_scalar", [hb], [hb, cst], out=hb[:, 0:2], in0=hb[:, 0:2], scalar1=flag_ap,
                     scalar2=None, op0=ALU.mult)
                em.act(cc_, cc_[:], hb[:, 2:NTO], AF.Identity, [hb, cst], bias=cst[:, BC + col:BC + col + 1],
                       scale=cst[:, WC + 88 + col:WC + 88 + col + 1])
                em.v("dve", "scalar_tensor_tensor", [cc_], [hb, cst, cc_], out=cc_[:], in0=hb[:, 1:NTO - 1],
                     scalar=cst[:, WC + 44 + col:WC + 44 + col + 1], in1=cc_[:], op0=ALU.mult, op1=ALU.add)
                em.v("dve", "scalar_tensor_tensor", [cc_], [hb, cst, cc_], out=cc_[:], in0=hb[:, 0:NT],
                     scalar=cst[:, WC + col:WC + col + 1], in1=cc_[:], op0=ALU.mult, op1=ALU.add)
            em.act(ca, ca[:], ca[:], AF.Gelu, [ca])
            em.v("dve", "tensor_tensor", [fT], [ca, cg], out=fT[:, cp, :], in0=ca[:], in1=cg[:], op=ALU.mult)
        if stop_after == "D1":
            return done()
        em.barrier()
        ar.lo = M3

        w_dn = ar.alloc("w_dn", [128, NCP, D], BF16)
        lnp2 = ar.alloc("lnp2", [128, 2, D], F32)
        xb_ = [ar.alloc("xb%d" % i, [128, D], F32) for i in range(2)]
        r2s = [ar.alloc("r2_%d" % i, [128, D], F32, at=x1T.off + i * 4096) for i in range(2)]
        xn2 = [ar.alloc("xn2_%d" % i, [128, D], F32, at=x1T.off + 8192 + i * 4096) for i in range(2)]
        stt2 = [ar.alloc("stt2_%d" % i, [128, 2, 6], F32, at=x1T.off + 16384 + i * 64) for i in range(2)]
        mv2 = [ar.alloc("mv2_%d" % i, [128, 8], F32, at=x1T.off + 16640 + i * 64) for i in range(2)]
        ot = [ar.alloc("ot%d" % i, [128, D], F32) for i in range(2)]
        stt = ar.alloc("stt2", [128, 2, 6], F32)
        mv = ar.alloc("mv2", [128, 8], F32)
        for hf in range(2):
            em.dma("pool", w_dn[:, hf * 11:(hf + 1) * 11, :], w_down_r[:, hf * 11:(hf + 1) * 11, :], w_dn, True, group="wd")
        em.dma("sp", lnp2[:].rearrange("p a b -> p (a b)"), dr["lnp"][:, 2 * D:4 * D], lnp2, True)
        em.dma("sp", xb_[0][:], x1s[0:128, :], xb_[0], True)
        for blk in range(16):
            s_ = blk % 2
            Z = [banks[s_ * 2], banks[s_ * 2 + 1]]
            if blk + 1 < 16:
                em.dma("sp", xb_[1 - s_][:], x1s[(blk + 1) * 128:(blk + 2) * 128, :], xb_[1 - s_], True)
            for n in range(2):
                for cp in range(NCP):
                    em.mm(Z[n], Z[n][:, 0:512], fT[:, cp, blk * 128:(blk + 1) * 128], w_dn[:, cp, n * 512:(n + 1) * 512],
                          cp == 0, cp == NCP - 1, [fT, w_dn])
                em.v("dve", "scalar_tensor_tensor", [r2s[s_]], [xb_[s_], Z[n]], out=r2s[s_][:, n * 512:(n + 1) * 512],
                     in0=xb_[s_][:, n * 512:(n + 1) * 512], scalar=ALPHA, in1=Z[n][:, 0:512], op0=ALU.mult, op1=ALU.add)
            layer_norm(r2s[s_], xn2[s_], ot[s_], 128, lnp2, stt2[s_], mv2[s_])
            em.dma("sp", y_out[blk * 128:(blk + 1) * 128, :], ot[s_][:], ot[s_], False)
        return done()


def kernel(**inputs):
    in_maps = prep_inputs(**{k: np.asarray(v) for k, v in inputs.items()})
    nc = build()
    res = run_bass_kernel_spmd(nc, in_maps, core_ids=list(range(8)))
    out = np.zeros((4, S, D), np.float32)
    for c in range(8):
        b, half = c // 2, c % 2
        out[b, half * NT:(half + 1) * NT] = res.results[c]["y"]
    return out
```

```python
import math
from contextlib import ExitStack

import numpy as np
import concourse.bass as bass
import concourse.mybir as mybir
from concourse.bass_utils import run_bass_kernel_spmd

F32 = mybir.dt.float32
BF16 = mybir.dt.bfloat16
AF = mybir.ActivationFunctionType
ALU = mybir.AluOpType
AX = mybir.AxisListType

ENGS = ("pe", "act", "dve", "pool", "sp")

D = 1024
S = 4096
NT = 2048
HALO = 2
NTO = NT + HALO
DFF = 2816
NCP = 22
IN_COLS = 9728
ALPHA = 2.0 ** 0.25
LAMBDA_INIT = 0.8 - 0.6 * math.exp(0.0)
C_SUBLN = 1.0 - LAMBDA_INIT
EPS = 1e-5
NEG = -30000.0
GROUP_DIL = (1, 4, 16)
NTAB = 73
SKIP_MARGIN = 150.0


class T:
    def __init__(self, name, ap):
        self.name = name
        self.ap = ap
        self.last_w = None
        self.readers = {}
        self.dma_sem = None
        self.dma_cnt = 0
        self.dma_wcnt = 0
        self.group = None

    def __getitem__(self, k):
        return self.ap[k]


class Op:
    __slots__ = ("eng", "fn", "waits", "inc", "idx", "is_dma")

    def __init__(self, eng, fn):
        self.eng = eng
        self.fn = fn
        self.waits = []
        self.inc = False
        self.idx = 0
        self.is_dma = False


class Emitter:
    def __init__(self, nc, stack):
        self.nc = nc
        self.stack = stack
        self.ops = {e: [] for e in ENGS}
        self.cops = {e: [] for e in ENGS}
        self.sems = {e: stack.enter_context(nc.semaphore("s_" + e)) for e in ENGS}
        self.waited = {e: {} for e in ENGS}
        self.tiles = []
        self.free_dsems = []
        self.ndsem = 0

    def tile(self, name, ap):
        t = T(name, ap)
        self.tiles.append(t)
        return t

    def _dsem(self, t):
        if t.dma_sem is None:
            t.dma_sem = self.stack.enter_context(self.nc.semaphore("d%d" % self.ndsem))
            self.ndsem += 1
        return t.dma_sem

    def _need(self, op, src_eng, idx):
        if idx is None or idx <= 0:
            return
        e = op.eng
        if src_eng == e and e == "pe":
            return
        w = self.waited[e]
        if w.get(src_eng, 0) >= idx:
            return
        w[src_eng] = idx
        op.waits.append(("eng", src_eng, idx))
        self.cops[src_eng][idx - 1].inc = True

    def _need_dma(self, op, t, writes_only=False):
        cnt = t.dma_wcnt if writes_only else t.dma_cnt
        if t.dma_sem is None or cnt == 0:
            return
        key = ("dma", id(t))
        w = self.waited[op.eng]
        if w.get(key, 0) >= cnt:
            return
        w[key] = cnt
        op.waits.append(("dma", t, cnt * 16))

    def op(self, eng, fn, reads=(), writes=()):
        o = Op(eng, fn)
        self.cops[eng].append(o)
        o.idx = len(self.cops[eng])
        for t in reads:
            if t.last_w is not None:
                self._need(o, *t.last_w)
            self._need_dma(o, t, writes_only=True)
        for t in writes:
            if t.last_w is not None:
                self._need(o, *t.last_w)
            for re_, ri in t.readers.items():
                self._need(o, re_, ri)
            self._need_dma(o, t)
        for t in reads:
            if t.readers.get(eng, 0) < o.idx:
                t.readers[eng] = o.idx
        for t in writes:
            t.last_w = (eng, o.idx)
            t.readers = {}
        self.ops[eng].append(o)
        return o

    def dma(self, eng, out_ap, in_ap, tile, is_load, group=None, also_read=()):
        t = tile
        sem = self._dsem(t)

        def fn(e, out_ap=out_ap, in_ap=in_ap, sem=sem):
            return e.dma_start(out=out_ap, in_=in_ap).then_inc(sem, 16)

        o = Op(eng, fn)
        o.is_dma = True
        if t.last_w is not None:
            self._need(o, *t.last_w)
        if is_load:
            for re_, ri in t.readers.items():
                self._need(o, re_, ri)
        if not (group is not None and t.group == group):
            self._need_dma(o, t)
        for t2 in also_read:
            if t2.last_w is not None:
                self._need(o, *t2.last_w)
        t.group = group
        t.dma_cnt += 1
        if is_load:
            t.dma_wcnt = t.dma_cnt
        for t2 in also_read:
            t2.dma_sem = sem
            t2.dma_cnt = t.dma_cnt
        if is_load:
            t.last_w = None
            t.readers = {}
        self.ops[eng].append(o)
        return o

    def barrier(self):
        snap = {e: len(self.cops[e]) for e in ENGS}
        for e in ENGS:
            o = Op(e, None)
            self.cops[e].append(o)
            o.idx = len(self.cops[e])
            for src in ENGS:
                if src != e:
                    self._need(o, src, snap[src])
            for t in self.tiles:
                self._need_dma(o, t)
            self.ops[e].append(o)
        self.tiles = [t for t in self.tiles if not getattr(t, "dead", False)]

    def emit(self):
        nc = self.nc
        em = self
        pref = {}
        for e in ENGS:
            c = 0
            p = [0]
            for o in em.cops[e]:
                if o.inc:
                    c += 1
                p.append(c)
            pref[e] = p
        with nc.Block() as block:
            def run(ename, eh):
                for o in em.ops[ename]:
                    for kind, src, val in o.waits:
                        if kind == "eng":
                            eh.wait_ge(em.sems[src], pref[src][val])
                        else:
                            eh.wait_ge(src.dma_sem, val)
                    ins = None
                    if o.fn is not None:
                        ins = o.fn(eh)
                    if o.is_dma:
                        continue
                    if o.inc:
                        if ins is None:
                            ins = eh.nop()
                        ins.then_inc(em.sems[ename], 1)

            @block.tensor
            def _(e):
                run("pe", e)

            @block.scalar
            def _(e):
                run("act", e)

            @block.vector
            def _(e):
                run("dve", e)

            @block.gpsimd
            def _(e):
                run("pool", e)

            @block.sync
            def _(e):
                run("sp", e)

    def mm(self, out_t, out_ap, lhsT, rhs, start, stop, reads, skip=False):
        return self.op("pe", lambda e: e.matmul(out_ap, lhsT=lhsT, rhs=rhs, start=start, stop=stop,
                                                skip_group_check=skip),
                       reads=reads, writes=[out_t])

    def act(self, out_t, out_ap, in_ap, func, reads, bias=None, scale=None):
        kw = {}
        if bias is not None:
            kw["bias"] = bias
        if scale is not None:
            kw["scale"] = scale
        return self.op("act", lambda e: e.activation(out=out_ap, in_=in_ap, func=func, **kw),
                       reads=reads, writes=[out_t])

    def v(self, eng, name, writes, reads, **kw):
        return self.op(eng, lambda e: getattr(e, name)(**kw), reads=reads, writes=writes)


class Arena:
    def __init__(self, em, nbytes):
        self.em = em
        self.n = nbytes
        self.h = em.stack.enter_context(em.nc.sbuf_tensor("arena", [128, nbytes // 2], BF16))
        self.lo = 0
        self.hi = nbytes

    def alloc(self, name, shape, dt, top=False, at=None):
        esz = 4 if dt == F32 else 2
        n = int(np.prod(shape[1:])) * esz
        n = (n + 63) // 64 * 64
        if at is not None:
            off = at
        elif top:
            self.hi -= n
            off = self.hi
        else:
            off = self.lo
            self.lo += n
        assert self.lo <= self.hi, "arena overflow at %s: lo=%d hi=%d" % (name, self.lo, self.hi)
        nel = int(np.prod(shape[1:]))
        a = self.h[:, off // 2: off // 2 + nel * esz // 2]
        if dt == F32:
            a = a.bitcast(F32)
        if len(shape) == 3:
            a = a.rearrange("p (a b) -> p a b", a=shape[1])
        elif len(shape) == 4:
            a = a.rearrange("p (a b c) -> p a b c", a=shape[1], b=shape[2])
        if shape[0] < 128:
            a = a[0:shape[0]]
        t = self.em.tile(name, a)
        t.off = off
        t.nbytes = n
        return t


def slopes8():
    return [2.0 ** (-(h + 1)) for h in range(8)]


def biasA_layout():
    cols = {}
    n = 0
    for r in range(5):
        njs = 16 if r == 0 else 16 + 4 * r
        for j in range(njs):
            for h in range(8):
                ns = 1 if (r == 0 or h >= 2) else (2 if h == 1 else 4)
                for s in range(ns):
                    cols[(r, j, h, s)] = n
                    n += 1
    return cols, n


BIAS_COLS, NBIAS = biasA_layout()
CST = {}
_o = 0
for _name, _n in (("bg", 16), ("lam4", 256), ("subln", 128), ("wc", 132), ("bc", 44), ("flag", 1),
                  ("biasA", NBIAS)):
    CST[_name] = _o
    _o += _n
NCST = _o


def nsub(r, h):
    return 1 if (r == 0 or h >= 2) else (2 if h == 1 else 4)


def build_biasA(half):
    m = slopes8()
    out = np.zeros((128, NBIAS), np.float32)
    p = np.arange(128, dtype=np.float64)
    for (r, j, h, s), c in BIAS_COLS.items():
        if r == 0:
            qc = 2047.0
        else:
            ws = 512 // nsub(r, h)
            qc = 2048 + 512 * (r - 1) + ws * s + ws / 2
        if half == 0 and r >= 1 and j < 16:
            out[:, c] = NEG
        else:
            out[:, c] = m[h] * (128 * j + p - qc)
    return out


def build_tabB(half):
    m = slopes8()
    kj = np.arange(128)[:, None].astype(np.float64)
    qi = np.arange(128)[None, :].astype(np.float64)
    tabs = np.zeros((128, NTAB, 128), np.float32)
    for g, dil in enumerate(GROUP_DIL):
        for h in range(8):
            cur = np.where(qi - kj >= 0, -m[h] * dil * (qi - kj), NEG)
            st = qi + 128 - kj
            prev = np.where(st <= 128, -m[h] * dil * st, NEG)
            prevx = prev if half == 1 else np.full_like(prev, NEG)
            b = (g * 8 + h) * 3
            tabs[:, b + 0] = cur
            tabs[:, b + 1] = prev
            tabs[:, b + 2] = prevx
    tabs[:, 72] = np.where(kj <= qi, 0.0, NEG)
    return tabs.reshape(128, NTAB * 128)


def prep_inputs(x, w_in, b_gate, lambda_q1, lambda_k1, lambda_q2, lambda_k2, subln_w,
                w_pa, w_pb, w_o, ln1_g, ln1_b, w_up, w_conv, b_conv, w_down, ln2_g, ln2_b):
    f = np.float32
    rep = lambda v: np.ascontiguousarray(np.broadcast_to(np.asarray(v, f).reshape(1, -1), (128, np.asarray(v).size)))
    shared = {
        "w_in": np.ascontiguousarray(w_in[0], f), "w_pa": np.ascontiguousarray(w_pa[0], f),
        "w_pb": np.ascontiguousarray(w_pb[0], f), "w_o": np.ascontiguousarray(w_o[0], f),
        "w_up": np.ascontiguousarray(w_up[0], f), "w_down": np.ascontiguousarray(w_down[0], f),
        "lnp": np.ascontiguousarray(np.concatenate([rep(ln1_g[0]), rep(ln1_b[0]), rep(ln2_g[0]), rep(ln2_b[0])], 1)),
    }
    cst_common = np.zeros((128, NCST), f)
    cst_common[:, CST["bg"]:CST["bg"] + 16] = np.asarray(b_gate[0], f).reshape(16, 128).T
    cst_common[:, CST["lam4"]:CST["lam4"] + 256] = np.concatenate(
        [rep(lambda_q1[0]), rep(lambda_k1[0]), rep(lambda_q2[0]), rep(lambda_k2[0])], 1)
    cst_common[:, CST["subln"]:CST["subln"] + 128] = rep(subln_w[0])
    wc = np.asarray(w_conv[0], f).reshape(3, 44, 128)
    cst_common[:, CST["wc"]:CST["wc"] + 132] = wc.transpose(2, 0, 1).reshape(128, 132)
    cst_common[:, CST["bc"]:CST["bc"] + 44] = np.asarray(b_conv[0], f).reshape(44, 128).T
    tabs = [build_tabB(0), build_tabB(1)]
    biases = [build_biasA(0), build_biasA(1)]
    in_maps = []
    for c in range(8):
        b, half = c // 2, c % 2
        xb = np.asarray(x[b], f)
        if half == 1:
            xT = np.ascontiguousarray(xb.T)
            xown = np.ascontiguousarray(xb[2046:4096])
        else:
            xT = np.zeros((D, S), f)
            xT[:, 2048:] = xb[:2048].T
            xown = np.zeros((NTO, D), f)
            xown[2:] = xb[:2048]
        cst = cst_common.copy()
        cst[:, CST["flag"]] = float(half)
        cst[:, CST["biasA"]:CST["biasA"] + NBIAS] = biases[half]
        m = dict(shared)
        m.update({"xT": xT, "xown": xown, "cst": cst, "tabB": tabs[half]})
        in_maps.append(m)
    return in_maps


def build(stop_after=None, dbg=()):
    nc = bass.Bass("TRN2", target_bir_lowering=False)
    dr = {}
    for name, shape in (("xT", [D, S]), ("xown", [NTO, D]), ("w_in", [D, IN_COLS]), ("w_pa", [D, D]),
                        ("w_pb", [512, D]), ("w_o", [D, D]), ("w_up", [D, 2 * DFF]), ("w_down", [DFF, D]),
                        ("lnp", [128, 4 * D]), ("cst", [128, NCST]), ("tabB", [128, NTAB * 128])):
        dr[name] = nc.dram_tensor(name, shape, F32, kind="ExternalInput").ap()
    y_out = nc.dram_tensor("y", [NT, D], F32, kind="ExternalOutput").ap()
    x1s = nc.dram_tensor("x1s", [NT, D], F32).ap()
    dbg_out = {}
    for name, shape in dbg:
        dbg_out[name] = nc.dram_tensor(name, shape, F32, kind="ExternalOutput").ap()

    w_in_r = dr["w_in"].rearrange("(kc p) n -> p kc n", p=128)
    xT_r = dr["xT"].rearrange("(kc p) t -> p kc t", p=128)
    m_sl = slopes8()

    with ExitStack() as st:
        em = Emitter(nc, st)
        ar = Arena(em, 212000)
        ps_all = st.enter_context(nc.psum_tensor("ps_all", [128, 4096], F32))
        banks = [em.tile("bank%d" % i, ps_all[:, 512 * i:512 * (i + 1)]) for i in range(8)]

        def done():
            em.barrier()
            em.emit()
            return nc

        def dump(name, t, ap):
            if name not in dbg_out:
                return
            em.dma("sp", dbg_out[name], ap, t, False)

        cst = ar.alloc("cst", [128, NCST], F32)
        ident = ar.alloc("ident", [128, 128], BF16)
        i32 = ar.alloc("i32", [128, 128], F32)
        small = ar.alloc("small", [128, 64], F32)
        neghalf = ar.alloc("neghalf", [128, 16], F32)
        sublnc = ar.alloc("sublnc", [128, 128], F32)
        ones32 = ar.alloc("ones32", [128, 64], F32)
        em.dma("sp", cst[:], dr["cst"], cst, True)
        em.v("pool", "memset", [i32], [], ap=i32[:], constant=0.0)
        em.op("pool", lambda e: e.affine_select(out=i32[:], in_=i32[:], pattern=[[-1, 128]],
                                                 compare_op=ALU.not_equal, fill=1.0, base=0,
                                                 channel_multiplier=1), reads=[i32], writes=[i32])
        em.v("dve", "tensor_copy", [ident], [i32], out=ident[:], in_=i32[:])
        em.v("pool", "memset", [neghalf], [], ap=neghalf[:], constant=-0.5)
        em.v("pool", "memset", [ones32], [], ap=ones32[:], constant=1.0)
        L0 = CST["lam4"]
        lamt = ar.alloc("lamt", [128, 128], F32)
        em.v("dve", "tensor_tensor", [lamt], [cst], out=lamt[:, 0:64], in0=cst[:, L0:L0 + 64],
             in1=cst[:, L0 + 64:L0 + 128], op=ALU.mult)
        em.v("dve", "tensor_tensor", [lamt], [cst, lamt], out=lamt[:, 64:128], in0=cst[:, L0 + 128:L0 + 192],
             in1=cst[:, L0 + 192:L0 + 256], op=ALU.mult)
        em.v("dve", "reduce_sum", [small], [lamt], out=small[:, 1:2], in_=lamt[:, 0:64], axis=AX.X)
        em.v("dve", "reduce_sum", [small], [lamt, small], out=small[:, 2:3], in_=lamt[:, 64:128], axis=AX.X)
        em.act(small, small[:, 3:5], small[:, 1:3], AF.Exp, [small])
        em.v("dve", "tensor_tensor", [small], [small], out=small[:, 5:6], in0=small[:, 4:5], in1=small[:, 3:4],
             op=ALU.subtract)
        em.v("dve", "tensor_scalar", [small], [small], out=small[:, 0:1], in0=small[:, 5:6],
             scalar1=-LAMBDA_INIT, scalar2=None, op0=ALU.add)
        em.v("dve", "tensor_scalar", [sublnc], [cst], out=sublnc[:], in0=cst[:, CST["subln"]:CST["subln"] + 128],
             scalar1=C_SUBLN, scalar2=None, op0=ALU.mult)
        neglam = small[:, 0:1]
        flag_ap = cst[:, CST["flag"]:CST["flag"] + 1]
        BA = CST["biasA"]

        M0 = (ar.lo, ar.hi)
        xT_all = ar.alloc("xT_bf", [128, 8, S], BF16)
        xTv = []
        for i in range(8):
            v_ = em.tile("xTv%d" % i, xT_all.ap)
            em.dma("pool", v_[:, :, i * 512:(i + 1) * 512], xT_r[:, :, i * 512:(i + 1) * 512], v_, True)
            xTv.append(v_)
        xT = xT_all.ap

        def xview(ctx0):
            return xTv[ctx0 // 512]

        oaT = ar.alloc("oaT", [128, 8, NTO], BF16)
        obT = ar.alloc("obT", [128, 4, NTO], BF16)
        M1 = (ar.lo, ar.hi)
        maskAt = ar.alloc("maskA", [128, 128], BF16)
        em.dma("pool", maskAt[:], dr["tabB"][:, 72 * 128:73 * 128], maskAt, True)
        maskA = maskAt.ap
        Qp = [ar.alloc("Qp%d" % m, [128, NTO], BF16) for m in range(2)]
        KT = ar.alloc("KT", [128, S], BF16)
        VV = ar.alloc("VV", [128, 32 * 4 * 129 + 128], BF16)
        V2 = VV.ap[:, 0:32 * 2 * 129].rearrange("p (b h e) -> p b h e", b=32, h=2)
        em.v("pool", "memset", [VV], [], ap=V2[:, :, :, 128:129], constant=1.0)
        v2end = VV.off + 32 * 2 * 129 * 2
        KTb = ar.alloc("KTb", [128, S], BF16, at=v2end)
        Qpb = [ar.alloc("Qpb%d" % m, [128, NTO], BF16, at=v2end + 8192 + m * 4104) for m in range(2)]
        assert v2end + 8192 + 2 * 4104 <= VV.off + VV.nbytes
        Wq = ar.alloc("Wq", [128, 8, 128], BF16)
        Wk = ar.alloc("Wk", [128, 8, 128], BF16)
        Wv = ar.alloc("Wv", [128, 8, 512], BF16)
        Wv2t = ar.alloc("Wv2", [128, 8, 256], BF16, at=Wv.off)
        Wv2 = Wv2t.ap
        Wqb_ = ar.alloc("WqB", [128, 8, 128], BF16, at=Wv.off + 4096)
        Wkb_ = ar.alloc("WkB", [128, 8, 128], BF16, at=Wv.off + 6144)
        KTs = [KT, KTb]
        Qps = [Qp, Qpb]
        Wqs = [Wq, Wqb_]
        Wks = [Wk, Wkb_]
        for qq in Qps:
            em.v("pool", "memset", [qq[0]], [], ap=qq[0][64:128, :], constant=0.0)
            em.v("pool", "memset", [qq[1]], [], ap=qq[1][0:64, :], constant=0.0)
        QT = [(0, 2, 2046)] + [(2 + 512 * i, 512, 2048 + 512 * i) for i in range(4)]

        mA = (ar.lo, ar.hi)
        P = [ar.alloc("P%d" % b, [128, 2, 512], BF16) for b in range(2)]
        tA = ar.alloc("tA", [128, 4, 128], F32)
        tO = ar.alloc("tO", [128, 4, 128], F32)
        oab = ar.alloc("oab", [128, 4, 128], BF16)
        rc = ar.alloc("rc", [128, 24], F32)
        Uc = ar.alloc("Uc", [128, 8, 129], F32)
        Sb = [[banks[0], banks[2]], [banks[1], banks[3]]]
        Spair = [ps_all[:, 1024 * b:1024 * (b + 1)].rearrange("p (m c) -> p m c", m=2) for b in range(2)]
        Ub = [banks[4], banks[5], banks[6]]
        X = banks[7]
        Uv = []
        Ut = []
        for u in range(8):
            bk, slot = Ub[u // 3], u % 3
            Uv.append(bk.ap[:, slot * 129:(slot + 1) * 129])
            Ut.append(bk)
        Xbf = X.ap.bitcast(BF16)
        pool5 = [banks[0], banks[1], banks[2], banks[3], banks[7]]
        rot = [0]

        def nextbank():
            b = pool5[rot[0] % 5]
            rot[0] += 1
            return b

        def kq_units(h, bankfn):
            pb = h % 2
            kt, qp, wq, wk = KTs[pb], Qps[pb], Wqs[pb], Wks[pb]
            units = []

            def wload():
                em.dma("pool", wq[:], w_in_r[:, :, h * 128:(h + 1) * 128], wq, True)
                em.dma("pool", wk[:], w_in_r[:, :, 1024 + h * 128:1024 + (h + 1) * 128], wk, True)
            units.append([wload])
            for tt in range(8):
                cell = {}
                mic = []
                for kc in range(8):
                    def kmm(kc=kc, tt=tt, cell=cell):
                        if kc == 0:
                            cell["bk"] = bankfn()
                        bk = cell["bk"]
                        em.mm(bk, bk[:, 0:512], wk[:, kc, :], xT[:, kc, tt * 512:(tt + 1) * 512], kc == 0, kc == 7,
                              [wk, xTv[tt]])
                    mic.append(kmm)

                def kev(tt=tt, cell=cell):
                    bk = cell["bk"]
                    em.v("dve", "tensor_copy", [kt], [bk], out=kt[:, tt * 512:(tt + 1) * 512], in_=bk[:, 0:512])
                mic.append(kev)
                units.append(mic)
            for (t0, w, c0) in QT:
                cell = {}
                mic = []
                for kc in range(8):
                    def qmm(kc=kc, t0=t0, w=w, c0=c0, cell=cell):
                        if kc == 0:
                            cell["bk"] = bankfn()
                        bk = cell["bk"]
                        em.mm(bk, bk[:, 0:w], wq[:, kc, :], xT[:, kc, c0:c0 + w], kc == 0, kc == 7, [wq, xview(c0)])
                    mic.append(qmm)

                def qev(t0=t0, w=w, cell=cell):
                    bk = cell["bk"]
                    em.v("dve", "tensor_scalar", [qp[0]], [bk], out=qp[0][0:64, t0:t0 + w], in0=bk[0:64, 0:w],
                         scalar1=0.125, scalar2=None, op0=ALU.mult)
                    em.v("dve", "tensor_scalar", [qp[1]], [bk], out=qp[1][64:128, t0:t0 + w], in0=bk[64:128, 0:w],
                         scalar1=0.125, scalar2=None, op0=ALU.mult)
                mic.append(qev)
                units.append(mic)
            return units

        for u_ in kq_units(0, nextbank):
            for mi_ in u_:
                mi_()
        deferred = []
        for h in range(8):
            hh = h % 2
            KTh, Qph = KTs[h % 2], Qps[h % 2]
            if hh == 0:
                em.dma("pool", Wv2, w_in_r[:, :, 2048 + h * 128:2048 + (h + 2) * 128], Wv2t, True)
                for blk in range(32):
                    bk = nextbank()
                    for kc in range(8):
                        em.mm(bk, bk[:, 0:256], xT[:, kc, blk * 128:(blk + 1) * 128], Wv2[:, kc, :], kc == 0, kc == 7,
                              [Wv2t, xview(blk * 128)])
                    em.v("dve", "tensor_copy", [VV], [bk], out=V2[:, blk, :, 0:128],
                         in_=bk[:, 0:256].rearrange("p (h e) -> p h e", h=2))
            pend_units = kq_units(h + 1, lambda: X) if h < 7 else []
            pending = []

            def pump(n):
                while n > 0 and (pending or pend_units):
                    if not pending:
                        pending.extend(pend_units.pop(0))
                    pending.pop(0)()
                    n -= 1

            def flush_unit():
                while pending:
                    pending.pop(0)()
            plan = []
            for r in range(5):
                t0, w, _c = QT[r]
                if r == 0:
                    steps = [(j, 0, False) for j in range(15)] + [(15, 0, True)]
                    nfull = 15
                else:
                    nfull = 16 + 4 * (r - 1)
                    steps = [(j, 0, False) for j in range(nfull)] + [(nfull + d, 128 * d, True) for d in range(4)]
                ns = nsub(r, h)
                ws = w // ns
                qc0 = 2047.0 if r == 0 else 2048 + 512 * (r - 1) + ws / 2.0
                keep = [st_ for st_ in steps if st_[2] or m_sl[h] * (qc0 - (128 * st_[0] + 127)) <= SKIP_MARGIN]
                if not any(not st_[2] for st_ in keep):
                    keep = [st_ for st_ in steps if not st_[2]][-1:] + keep
                plan.append((keep, nfull, ns, ws))
            total_steps = sum(len(p_[0]) for p_ in plan)
            n_micro = sum(len(u_) for u_ in pend_units)
            rate = n_micro / float(max(total_steps, 1))
            credit = 0.0
            for r in range(5):
                t0, w, _c = QT[r]
                steps, nfull, ns, ws = plan[r]
                if r == 0:
                    dw, mc0, nblk, nq = 2, 126, 1, 2
                else:
                    dw, mc0, nblk, nq = 128, 0, 4, 128

                def QK(si):
                    j, c0, diag = steps[si]
                    buf = si % 2
                    kblk = KTh[:, j * 128:(j + 1) * 128]
                    for m in range(2):
                        sb = Sb[m][buf]
                        if not diag:
                            em.mm(sb, sb[:, 0:w], kblk, Qph[m][:, t0:t0 + w], True, True, [KTh, Qph[m]])
                        else:
                            em.mm(sb, sb[:, c0:c0 + dw], kblk, Qph[m][:, t0 + c0:t0 + c0 + dw], True, False, [KTh, Qph[m]])
                            em.mm(sb, sb[:, c0:c0 + dw], ident[:], maskA[:, mc0:mc0 + dw], False, True, [ident, maskAt])
                            if c0 + dw < w:
                                em.mm(sb, sb[:, c0 + dw:w], kblk, Qph[m][:, t0 + c0 + dw:t0 + w], True, True, [KTh, Qph[m]])

                def EXP(si):
                    j, c0, diag = steps[si]
                    buf = si % 2
                    for s_ in range(ns):
                        a, b_ = max(s_ * ws, c0), (s_ + 1) * ws
                        if b_ <= a:
                            continue
                        col = BA + BIAS_COLS[(r, j, h, s_)]
                        em.act(P[buf], P[buf][:, :, a:b_], Spair[buf][:, :, a:b_], AF.Exp,
                               [Sb[0][buf], Sb[1][buf], cst], bias=cst[:, col:col + 1])

                def PV(si):
                    j, c0, diag = steps[si]
                    buf = si % 2
                    seen = set()
                    for m in range(2):
                        for b_ in range(nblk):
                            if 128 * b_ < c0:
                                continue
                            u = m * 4 + b_
                            last = nfull + b_
                            first = (si == 0 and (u // 3) not in seen)
                            seen.add(u // 3)
                            em.mm(Ut[u], Uv[u][0:nq, 0:129], P[buf][:, m, 128 * b_:128 * b_ + nq],
                                  V2[:, j, hh, :], first, j == last, [P[buf], VV], skip=True)

                QK(0)
                for si in range(len(steps)):
                    EXP(si)
                    if si + 1 < len(steps):
                        QK(si + 1)
                    PV(si)
                    credit += rate
                    k_ = int(credit)
                    credit -= k_
                    pump(k_)
                if deferred:
                    flush_unit()
                for d_ in deferred:
                    d_()
                deferred = []
                nb = nblk
                Ucv = Uc.ap
                if nb == 4:
                    for bkI, (ua, ub) in enumerate(((0, 3), (3, 6), (6, 8))):
                        em.v("dve", "tensor_copy", [Uc], [Ub[bkI]],
                             out=Ucv[:, ua:ub, :].rearrange("p a b -> p (a b)"), in_=Ub[bkI][:, 0:(ub - ua) * 129])
                else:
                    em.v("dve", "tensor_copy", [Uc], [Ut[0]], out=Ucv[0:nq, 0, :], in_=Uv[0][0:nq, :])
                    em.v("dve", "tensor_copy", [Uc], [Ut[4]], out=Ucv[0:nq, 4, :], in_=Uv[4][0:nq, :])
                r0 = rc[0:nq, 0:nb]
                r1 = rc[0:nq, 4:4 + nb]
                em.v("dve", "reciprocal", [rc], [Uc], out=r0, in_=Ucv[0:nq, 0:nb, 128])
                em.v("dve", "reciprocal", [rc], [Uc, rc], out=r1, in_=Ucv[0:nq, 4:4 + nb, 128])
                a4 = tA[0:nq, 0:nb, :]
                o4 = tO[0:nq, 0:nb, :]
                bc = lambda ap2, nq=nq, nb=nb: ap2.unsqueeze(2).to_broadcast([nq, nb, 128])
                em.v("dve", "scalar_tensor_tensor", [tA], [Uc, rc, small], out=a4, in0=Ucv[0:nq, 4:4 + nb, 0:128],
                     scalar=neglam[0:nq, :], in1=bc(r1), op0=ALU.mult, op1=ALU.mult)
                em.v("dve", "tensor_tensor", [tO], [Uc, rc], out=o4, in0=Ucv[0:nq, 0:nb, 0:128], in1=bc(r0), op=ALU.mult)
                em.v("dve", "tensor_tensor", [tO], [tO, tA], out=o4, in0=o4, in1=a4, op=ALU.add)
                em.v("dve", "tensor_tensor", [tA], [tO], out=a4, in0=o4, in1=o4, op=ALU.mult)
                em.v("dve", "reduce_sum", [rc], [tA], out=rc[0:nq, 8:8 + nb], in_=a4, axis=AX.X)
                em.v("dve", "tensor_scalar", [rc], [rc], out=rc[0:nq, 12:12 + nb], in0=rc[0:nq, 8:8 + nb],
                     scalar1=1.0 / 128, scalar2=EPS, op0=ALU.mult, op1=ALU.add)
                em.v("pool", "tensor_tensor", [rc], [rc, neghalf], out=rc[0:nq, 16:16 + nb], in0=rc[0:nq, 12:12 + nb],
                     in1=neghalf[0:nq, 0:nb], op=ALU.pow)
                em.v("dve", "tensor_tensor", [tO], [tO, rc], out=o4, in0=o4, in1=bc(rc[0:nq, 16:16 + nb]), op=ALU.mult)
                em.v("dve", "tensor_tensor", [oab], [tO, sublnc], out=oab[0:nq, 0:nb, :], in0=o4,
                     in1=sublnc[0:nq, :].unsqueeze(1).to_broadcast([nq, nb, 128]), op=ALU.mult)

                def part2(nq=nq, nb=nb, t0=t0, h=h):
                    for b_ in range(nb):
                        em.op("pe", lambda e, b_=b_: e.transpose(Xbf[:, b_ * nq:(b_ + 1) * nq], oab[0:nq, b_, :],
                                                                 ident[0:nq, 0:nq]),
                              reads=[oab, ident], writes=[X])
                    em.v("dve", "tensor_copy", [oaT], [X], out=oaT[:, h, t0:t0 + nb * nq], in_=Xbf[:, 0:nb * nq])
                deferred.append(part2)
            pump(1 << 30)
        for d_ in deferred:
            d_()
        def dump3(name, src, nchunk):
            if name not in dbg_out:
                return
            em.barrier()
            stg = VV.ap[:, 0:2 * NTO].bitcast(F32)
            for c_ in range(nchunk):
                em.v("dve", "tensor_copy", [VV], [src], out=stg, in_=src[:, c_, :])
                em.dma("sp", dbg_out[name][:, c_, :], stg, VV, False)
            em.barrier()

        dump3("oaT", oaT, 8)
        if stop_after == "A":
            return done()
        em.barrier()
        ar.lo, ar.hi = mA

        tabB = ar.alloc("tabB", [128, NTAB - 1, 128], BF16)
        em.dma("pool", tabB[:], dr["tabB"][:, 0:72 * 128].rearrange("p (t q) -> p t q", t=NTAB - 1), tabB, True)
        Pb = [ar.alloc("Pb%d" % b, [128, 2, 128], BF16) for b in range(2)]
        shiftI = ar.alloc("shiftI", [128, 128], BF16)
        em.v("pool", "memset", [shiftI], [], ap=shiftI[:], constant=0.0)
        em.v("dve", "tensor_copy", [shiftI], [ident], out=shiftI[0:64, 64:128], in_=ident[0:64, 0:64])
        vtail = VV.off + 8320
        acc = [ar.alloc("acc%d" % i, [128, NTO], F32, at=vtail + i * 8200) for i in range(2)]
        rE = ar.alloc("rE", [128, NTO], F32, at=vtail + 2 * 8200)
        assert vtail + 3 * 8200 <= VV.off + VV.nbytes
        obtmp = ar.alloc("obtmp", [128, NTO], BF16, at=Wv.off)
        Wvb = ar.alloc("Wvb", [128, 8, 128], BF16, at=Wv.off + 4160)
        assert 4160 + 2048 <= Wv.nbytes
        Vp = VV.ap[:, 0:32 * 2 * 65].rearrange("p (b h e) -> p b h e", b=32, h=2)
        Wqb, Wkb = Wq, Wk
        SbB = [banks[0], banks[1]]
        ObB = [banks[2], banks[3]]
        poolB = [banks[4], banks[5], banks[6], banks[7]]
        rotB = [0]

        def nextbankB():
            b = poolB[rotB[0] % 4]
            rotB[0] += 1
            return b

        def chunk_tokens(g, c, r_):
            dil = GROUP_DIL[g]
            s0 = 128 * dil * c + r_
            return s0, dil

        for hp in range(4):
            for g in range(3):
                dil = GROUP_DIL[g]
                base = 3072 + g * 512 + hp * 128
                em.dma("pool", Wqb[:], w_in_r[:, :, base:base + 128], Wqb, True)
                em.dma("pool", Wkb[:], w_in_r[:, :, base + 1536:base + 1536 + 128], Wkb, True)
                em.dma("pool", Wvb[:], w_in_r[:, :, base + 3072:base + 3072 + 128], Wvb, True)
                first_tok = {0: 14 * 128, 1: 2 * 512, 2: 0}[g]
                for tt in range(first_tok // 512, 8):
                    bk = nextbankB()
                    for kc in range(8):
                        em.mm(bk, bk[:, 0:512], Wkb[:, kc, :], xT[:, kc, tt * 512:(tt + 1) * 512], kc == 0, kc == 7,
                              [Wkb, xTv[tt]])
                    em.v("dve", "tensor_copy", [KT], [bk], out=KT[:, tt * 512:(tt + 1) * 512], in_=bk[:, 0:512])
                for (t0, w, c0) in QT:
                    bk = nextbankB()
                    for kc in range(8):
                        em.mm(bk, bk[:, 0:w], Wqb[:, kc, :], xT[:, kc, c0:c0 + w], kc == 0, kc == 7, [Wqb, xview(c0)])
                    em.v("dve", "tensor_scalar", [Qp[0]], [bk], out=Qp[0][0:64, t0:t0 + w], in0=bk[0:64, 0:w],
                         scalar1=0.125, scalar2=None, op0=ALU.mult)
                    em.v("dve", "tensor_scalar", [Qp[1]], [bk], out=Qp[1][64:128, t0:t0 + w], in0=bk[64:128, 0:w],
                         scalar1=0.125, scalar2=None, op0=ALU.mult)
                if g == 0 and hp == 0:
                    pass
                for cid in range(32):
                    c_, r_ = cid // dil, cid % dil
                    s0 = 128 * dil * c_ + r_
                    if 128 * dil * c_ < first_tok:
                        continue
                    bk = nextbankB()
                    for kc in range(8):
                        em.mm(bk, bk[:, 0:128], xT[:, kc, s0:s0 + 127 * dil + 1:dil], Wvb[:, kc, :], kc == 0, kc == 7,
                              [Wvb] + xTv)
                    em.v("dve", "tensor_copy", [VV], [bk], out=Vp[:, cid, :, 0:64],
                         in_=bk[:, 0:128].rearrange("p (h e) -> p h e", h=2))
                if g == 0 and hp == 0:
                    em.v("pool", "memset", [VV], [], ap=Vp[:, :, :, 64:65], constant=1.0)
                items = []
                if g == 0:
                    items.append((15, 126, 128, 0, 1, 1))
                    for c_ in range(16, 32):
                        items.append((c_, 0, 128, 2 + 128 * (c_ - 16), 1, 2 if c_ == 16 else 1))
                elif g == 1:
                    items.append((3 * 4 + 2, 127, 128, 0, 1, 1))
                    items.append((3 * 4 + 3, 127, 128, 1, 1, 1))
                    for c_ in range(4, 8):
                        for r_ in range(4):
                            items.append((c_ * 4 + r_, 0, 128, 2 + 512 * (c_ - 4) + r_, 4, 2 if c_ == 4 else 1))
                else:
                    items.append((14, 127, 128, 0, 1, None))
                    items.append((15, 127, 128, 1, 1, None))
                    for r_ in range(16):
                        items.append((16 + r_, 0, 128, 2 + r_, 16, 2))
                for hd in range(2):
                    head = 2 * hp + hd
                    tb = (g * 8 + head) * 3

                    def kchunk(cid):
                        c_, r_ = cid // dil, cid % dil
                        s0 = 128 * dil * c_ + r_
                        return KT[:, s0:s0 + 127 * dil + 1:dil]

                    def QKb(ii):
                        cid, i0, i1, tq, tstep, pk = items[ii]
                        n = i1 - i0
                        sb = SbB[ii % 2]
                        qa = Qp[hd][:, tq:tq + (n - 1) * tstep + 1:tstep]
                        if pk is not None:
                            em.mm(sb, sb[:, 0:n], kchunk(cid - dil), qa, True, False, [KT, Qp[hd]])
                            em.mm(sb, sb[:, 0:n], ident[:], tabB[:, tb + pk, i0:i1], False, True, [ident, tabB])
                        em.mm(sb, sb[:, 128:128 + n], kchunk(cid), qa, True, False, [KT, Qp[hd]])
                        em.mm(sb, sb[:, 128:128 + n], ident[:], tabB[:, tb + 0, i0:i1], False, True, [ident, tabB])

                    def EXPb(ii):
                        cid, i0, i1, tq, tstep, pk = items[ii]
                        n = i1 - i0
                        sb = SbB[ii % 2]
                        pb = Pb[ii % 2]
                        if pk is not None and n == 128:
                            em.act(pb, pb[:].rearrange("p a b -> p (a b)"), sb[:, 0:256], AF.Exp, [sb])
                        else:
                            if pk is not None:
                                em.act(pb, pb[:, 0, 0:n], sb[:, 0:n], AF.Exp, [sb])
                            em.act(pb, pb[:, 1, 0:n], sb[:, 128:128 + n], AF.Exp, [sb])

                    def PVb(ii):
                        cid, i0, i1, tq, tstep, pk = items[ii]
                        n = i1 - i0
                        pb = Pb[ii % 2]
                        ob = ObB[ii % 2]
                        if pk is not None:
                            em.mm(ob, ob[0:65, 0:n], Vp[:, cid - dil, hd, :], pb[:, 0, 0:n], True, False, [VV, pb])
                        em.mm(ob, ob[0:65, 0:n], Vp[:, cid, hd, :], pb[:, 1, 0:n], pk is None, True, [VV, pb])
                        dst = acc[hd][0:65, tq:tq + (n - 1) * tstep + 1:tstep]
                        if g == 0:
                            em.v("dve", "tensor_copy", [acc[hd]], [ob], out=dst, in_=ob[0:65, 0:n])
                        else:
                            em.v("dve", "tensor_tensor", [acc[hd]], [ob, acc[hd]], out=dst, in0=dst, in1=ob[0:65, 0:n],
                                 op=ALU.add)

                    QKb(0)
                    for ii in range(len(items)):
                        EXPb(ii)
                        if ii + 1 < len(items):
                            QKb(ii + 1)
                        PVb(ii)
            for hd in range(2):
                em.act(rE, rE[64:65, :], acc[hd][64:65, :], AF.Ln, [acc[hd]])
                em.act(rE, rE[64:65, :], rE[64:65, :], AF.Exp, [rE], scale=-1.0)
                for c0 in range(0, NTO, 512):
                    w = min(512, NTO - c0)
                    bk = nextbankB()
                    em.mm(bk, bk[0:64, 0:w], ones32[64:65, :], rE[64:65, c0:c0 + w], True, True, [ones32, rE])
                    if hd == 0:
                        em.v("dve", "tensor_tensor", [obT], [bk, acc[hd]], out=obT[0:64, hp, c0:c0 + w],
                             in0=acc[hd][0:64, c0:c0 + w], in1=bk[0:64, 0:w], op=ALU.mult)
                    else:
                        em.v("dve", "tensor_tensor", [obtmp], [bk, acc[hd]], out=obtmp[0:64, c0:c0 + w],
                             in0=acc[hd][0:64, c0:c0 + w], in1=bk[0:64, 0:w], op=ALU.mult)
                if hd == 1:
                    for c0 in range(0, NTO, 512):
                        w = min(512, NTO - c0)
                        bk = nextbankB()
                        em.mm(bk, bk[:, 0:w], shiftI[0:64, :], obtmp[0:64, c0:c0 + w], True, True, [shiftI, obtmp])
                        em.v("dve", "tensor_copy", [obT], [bk], out=obT[64:128, hp, c0:c0 + w], in_=bk[64:128, 0:w])
        dump3("obT", obT, 4)
        if stop_after == "B":
            return done()
        em.barrier()
        ar.lo, ar.hi = M1

        w_pa_r = dr["w_pa"].rearrange("(kc p) n -> p kc n", p=128)
        w_pb_r = dr["w_pb"].rearrange("(kc p) n -> p kc n", p=128)
        w_o_r = dr["w_o"].rearrange("(kc p) n -> p kc n", p=128)
        w_up_r = dr["w_up"].rearrange("(kc p) n -> p kc n", p=128)
        w_down_r = dr["w_down"].rearrange("(c p) n -> p c n", p=128)
        yT = ar.alloc("yT", [128, 8, NTO], BF16, top=True)
        WgA = [ar.alloc("WgA%d" % i, [128, 8, 128], BF16) for i in range(2)]
        WgB = [ar.alloc("WgB%d" % i, [128, 8, 128], BF16) for i in range(2)]
        Wpa = [ar.alloc("Wpa%d" % i, [128, 8, 128], BF16) for i in range(2)]
        Wpb = [ar.alloc("Wpb%d" % i, [128, 4, 128], BF16) for i in range(2)]
        g0s = [ar.alloc("g0s%d" % i, [128, 512], F32) for i in range(2)]
        g1s = [ar.alloc("g1s%d" % i, [128, 512], F32) for i in range(2)]
        tt = ar.alloc("tt", [128, 512], F32)
        tu = ar.alloc("tu", [128, 512], F32)
        BG = CST["bg"]
        it = 0
        for c in range(8):
            bi = c % 2
            em.dma("pool", WgA[bi][:], w_in_r[:, :, 7680 + c * 128:7680 + (c + 1) * 128], WgA[bi], True)
            em.dma("pool", WgB[bi][:], w_in_r[:, :, 8704 + c * 128:8704 + (c + 1) * 128], WgB[bi], True)
            em.dma("pool", Wpa[bi][:], w_pa_r[:, :, c * 128:(c + 1) * 128], Wpa[bi], True)
            em.dma("pool", Wpb[bi][:], w_pb_r[:, :, c * 128:(c + 1) * 128], Wpb[bi], True)
            for (t0, w, c0) in QT:
                st_ = it % 2
                it += 1
                G0, G1, PA, PB = banks[st_ * 4:st_ * 4 + 4]
                for kc in range(8):
                    em.mm(G0, G0[:, 0:w], WgA[bi][:, kc, :], xT[:, kc, c0:c0 + w], kc == 0, kc == 7, [WgA[bi], xview(c0)])
                for kc in range(8):
                    em.mm(G1, G1[:, 0:w], WgB[bi][:, kc, :], xT[:, kc, c0:c0 + w], kc == 0, kc == 7, [WgB[bi], xview(c0)])
                for fc in range(8):
                    em.mm(PA, PA[:, 0:w], Wpa[bi][:, fc, :], oaT[:, fc, t0:t0 + w], fc == 0, fc == 7, [Wpa[bi], oaT])
                for fc in range(4):
                    em.mm(PB, PB[:, 0:w], Wpb[bi][:, fc, :], obT[:, fc, t0:t0 + w], fc == 0, fc == 3, [Wpb[bi], obT])
                em.act(g0s[st_], g0s[st_][:, 0:w], G0[:, 0:w], AF.Sigmoid, [G0, cst], bias=cst[:, BG + c:BG + c + 1])
                em.act(g1s[st_], g1s[st_][:, 0:w], G1[:, 0:w], AF.Sigmoid, [G1, cst], bias=cst[:, BG + 8 + c:BG + 9 + c])
                em.v("dve", "tensor_tensor", [tt], [g0s[st_], PA], out=tt[:, 0:w], in0=g0s[st_][:, 0:w], in1=PA[:, 0:w],
                     op=ALU.mult)
                em.v("dve", "tensor_tensor", [tu], [g1s[st_], PB], out=tu[:, 0:w], in0=g1s[st_][:, 0:w], in1=PB[:, 0:w],
                     op=ALU.mult)
                em.v("dve", "tensor_tensor", [yT], [tt, tu], out=yT[:, c, t0:t0 + w], in0=tt[:, 0:w], in1=tu[:, 0:w],
                     op=ALU.add)
        dump3("yT", yT, 8)
        if stop_after == "C1":
            return done()
        em.barrier()
        ar.lo = M0[0]

        x1T = ar.alloc("x1T", [128, 8, NTO], BF16)
        M2 = ar.lo
        w_o = ar.alloc("w_o", [128, 8, D], BF16)
        lnp = ar.alloc("lnp", [128, 2, D], F32)
        xo = [ar.alloc("xo%d" % i, [128, D], F32) for i in range(2)]
        rrs = [ar.alloc("rr%d" % i, [128, D], F32) for i in range(2)]
        xns = [ar.alloc("xn%d" % i, [128, D], F32) for i in range(2)]
        stts = [ar.alloc("stt%d" % i, [128, 2, 6], F32) for i in range(2)]
        mvs = [ar.alloc("mv%d" % i, [128, 8], F32) for i in range(2)]
        x1t = [ar.alloc("x1t%d" % i, [128, D], F32) for i in range(2)]
        x1bs = [ar.alloc("x1b%d" % i, [128, D], BF16) for i in range(2)]
        stt = ar.alloc("stt", [128, 2, 6], F32)
        mv = ar.alloc("mv", [128, 8], F32)
        for hf in range(2):
            em.dma("pool", w_o[:, :, hf * 512:(hf + 1) * 512], w_o_r[:, :, hf * 512:(hf + 1) * 512], w_o, True, group="wo")
        em.dma("sp", lnp[:].rearrange("p a b -> p (a b)"), dr["lnp"][:, 0:2 * D], lnp, True)

        def layer_norm(src, xn, dst, nq, lnt, stt_, mv_):
            for c_ in range(2):
                em.v("dve", "bn_stats", [stt_], [src], out=stt_[0:nq, c_, :], in_=src[0:nq, c_ * 512:(c_ + 1) * 512])
            em.v("dve", "bn_aggr", [mv_], [stt_], out=mv_[0:nq, 0:2], in_=stt_[0:nq].rearrange("p c s -> p (c s)"))
            em.v("dve", "tensor_scalar", [mv_], [mv_], out=mv_[0:nq, 2:3], in0=mv_[0:nq, 1:2], scalar1=EPS, scalar2=None,
                 op0=ALU.add)
            em.v("pool", "tensor_tensor", [mv_], [mv_, neghalf], out=mv_[0:nq, 3:4], in0=mv_[0:nq, 2:3],
                 in1=neghalf[0:nq, 0:1], op=ALU.pow)
            em.v("dve", "tensor_scalar", [mv_], [mv_], out=mv_[0:nq, 4:5], in0=mv_[0:nq, 0:1], scalar1=mv_[0:nq, 3:4],
                 scalar2=-1.0, op0=ALU.mult, op1=ALU.mult)
            em.act(xn, xn[0:nq, :], src[0:nq, :], AF.Identity, [src, mv_], bias=mv_[0:nq, 4:5], scale=mv_[0:nq, 3:4])
            em.v("dve", "tensor_tensor", [xn], [xn, lnt], out=xn[0:nq, :], in0=xn[0:nq, :], in1=lnt[0:nq, 0, :],
                 op=ALU.mult)
            em.v("pool", "tensor_tensor", [dst], [xn, lnt], out=dst[0:nq, :], in0=xn[0:nq, :], in1=lnt[0:nq, 1, :],
                 op=ALU.add)

        BLK = [(0, 2)] + [(2 + 128 * i, 128) for i in range(16)]
        def c2_z(bi):
            t0, nq = BLK[bi]
            s_ = bi % 2
            Z = [banks[s_ * 2], banks[s_ * 2 + 1]]
            em.dma("sp", xo[s_][0:nq, :], dr["xown"][t0:t0 + nq, :], xo[s_], True)
            for n in range(2):
                for cc in range(8):
                    em.mm(Z[n], Z[n][0:nq, 0:512], yT[:, cc, t0:t0 + nq], w_o[:, cc, n * 512:(n + 1) * 512], cc == 0,
                          cc == 7, [yT, w_o])

        def c2_ln(bi):
            t0, nq = BLK[bi]
            s_ = bi % 2
            Z = [banks[s_ * 2], banks[s_ * 2 + 1]]
            for n in range(2):
                em.v("dve", "scalar_tensor_tensor", [rrs[s_]], [xo[s_], Z[n]], out=rrs[s_][0:nq, n * 512:(n + 1) * 512],
                     in0=xo[s_][0:nq, n * 512:(n + 1) * 512], scalar=ALPHA, in1=Z[n][0:nq, 0:512], op0=ALU.mult,
                     op1=ALU.add)
            layer_norm(rrs[s_], xns[s_], x1t[s_], nq, lnp, stts[s_], mvs[s_])
            if nq == 128:
                em.dma("sp", x1s[t0 - 2:t0 - 2 + 128, :], x1t[s_][:], x1t[s_], False)
            em.act(x1bs[s_], x1bs[s_][0:nq, :], x1t[s_][0:nq, :], AF.Copy, [x1t[s_]])

        def c2_tr(bi):
            t0, nq = BLK[bi]
            s_ = bi % 2
            TR = banks[4 + s_]
            TRbf = TR.ap.bitcast(BF16)
            x1b = x1bs[s_]
            for cc in range(8):
                em.op("pe", lambda e, cc=cc, nq=nq, TRbf=TRbf, x1b=x1b: e.transpose(TRbf[:, cc * nq:(cc + 1) * nq],
                                                                                     x1b[0:nq, cc * 128:(cc + 1) * 128],
                                                                                     ident[0:nq, 0:nq]),
                      reads=[x1b, ident], writes=[TR])
            em.v("dve", "tensor_copy", [x1T], [TR], out=x1T[:, :, t0:t0 + nq],
                 in_=TRbf[:, 0:8 * nq].rearrange("p (c q) -> p c q", c=8))

        c2_z(0)
        for bi in range(len(BLK)):
            if bi + 1 < len(BLK):
                c2_z(bi + 1)
            c2_ln(bi)
            if bi >= 1:
                c2_tr(bi - 1)
        c2_tr(len(BLK) - 1)
        dump3("x1T", x1T, 8)
        if stop_after == "C2":
            return done()
        em.barrier()
        ar.lo = M2
        ar.hi = M0[1]

        fT = ar.alloc("fT", [128, NCP, NT], BF16)
        M3 = ar.lo
        hA = [ar.alloc("hA%d" % i, [128, NTO], F32) for i in range(2)]
        hG = [ar.alloc("hG%d" % i, [128, NTO], F32) for i in range(2)]
        ca = ar.alloc("ca", [128, NT], F32)
        cg = ar.alloc("cg", [128, NT], F32)
        WuA = [ar.alloc("WuA%d" % i, [128, 8, 128], BF16) for i in range(2)]
        WuG = [ar.alloc("WuG%d" % i, [128, 8, 128], BF16) for i in range(2)]
        UT = [(0, 512), (512, 512), (1024, 512), (1536, 512), (2048, 2)]
        WC, BC = CST["wc"], CST["bc"]
        it = 0
        for cp in range(NCP):
            s_ = cp % 2
            em.dma("pool", WuA[s_][:], w_up_r[:, :, cp * 128:(cp + 1) * 128], WuA[s_], True)
            em.dma("pool", WuG[s_][:], w_up_r[:, :, DFF + cp * 128:DFF + (cp + 1) * 128], WuG[s_], True)
            for (u0, w) in UT:
                bs = it % 4
                it += 1
                A_, G_ = banks[2 * bs], banks[2 * bs + 1]
                for kc in range(8):
                    em.mm(A_, A_[:, 0:w], WuA[s_][:, kc, :], x1T[:, kc, u0:u0 + w], kc == 0, kc == 7, [WuA[s_], x1T])
                for kc in range(8):
                    em.mm(G_, G_[:, 0:w], WuG[s_][:, kc, :], x1T[:, kc, u0:u0 + w], kc == 0, kc == 7, [WuG[s_], x1T])
                em.act(hA[s_], hA[s_][:, u0:u0 + w], A_[:, 0:w], AF.Copy, [A_])
                em.act(hG[s_], hG[s_][:, u0:u0 + w], G_[:, 0:w], AF.Copy, [G_])
            for (hb, cc_, col) in ((hA[s_], ca, cp), (hG[s_], cg, NCP + cp)):
                em.v("dve", "tensor_scalar", [hb], [hb, cst], out=hb[:, 0:2], in0=hb[:, 0:2], scalar1=flag_ap,
                     scalar2=None, op0=ALU.mult)
                em.act(cc_, cc_[:], hb[:, 2:NTO], AF.Identity, [hb, cst], bias=cst[:, BC + col:BC + col + 1],
                       scale=cst[:, WC + 88 + col:WC + 88 + col + 1])
                em.v("dve", "scalar_tensor_tensor", [cc_], [hb, cst, cc_], out=cc_[:], in0=hb[:, 1:NTO - 1],
                     scalar=cst[:, WC + 44 + col:WC + 44 + col + 1], in1=cc_[:], op0=ALU.mult, op1=ALU.add)
                em.v("dve", "scalar_tensor_tensor", [cc_], [hb, cst, cc_], out=cc_[:], in0=hb[:, 0:NT],
                     scalar=cst[:, WC + col:WC + col + 1], in1=cc_[:], op0=ALU.mult, op1=ALU.add)
            em.act(ca, ca[:], ca[:], AF.Gelu, [ca])
            em.v("dve", "tensor_tensor", [fT], [ca, cg], out=fT[:, cp, :], in0=ca[:], in1=cg[:], op=ALU.mult)
        if stop_after == "D1":
            return done()
        em.barrier()
        ar.lo = M3

        w_dn = ar.alloc("w_dn", [128, NCP, D], BF16)
        lnp2 = ar.alloc("lnp2", [128, 2, D], F32)
        xb_ = [ar.alloc("xb%d" % i, [128, D], F32) for i in range(2)]
        r2s = [ar.alloc("r2_%d" % i, [128, D], F32, at=x1T.off + i * 4096) for i in range(2)]
        xn2 = [ar.alloc("xn2_%d" % i, [128, D], F32, at=x1T.off + 8192 + i * 4096) for i in range(2)]
        stt2 = [ar.alloc("stt2_%d" % i, [128, 2, 6], F32, at=x1T.off + 16384 + i * 64) for i in range(2)]
        mv2 = [ar.alloc("mv2_%d" % i, [128, 8], F32, at=x1T.off + 16640 + i * 64) for i in range(2)]
        ot = [ar.alloc("ot%d" % i, [128, D], F32) for i in range(2)]
        stt = ar.alloc("stt2", [128, 2, 6], F32)
        mv = ar.alloc("mv2", [128, 8], F32)
        for hf in range(2):
            em.dma("pool", w_dn[:, hf * 11:(hf + 1) * 11, :], w_down_r[:, hf * 11:(hf + 1) * 11, :], w_dn, True, group="wd")
        em.dma("sp", lnp2[:].rearrange("p a b -> p (a b)"), dr["lnp"][:, 2 * D:4 * D], lnp2, True)
        em.dma("sp", xb_[0][:], x1s[0:128, :], xb_[0], True)
        for blk in range(16):
            s_ = blk % 2
            Z = [banks[s_ * 2], banks[s_ * 2 + 1]]
            if blk + 1 < 16:
                em.dma("sp", xb_[1 - s_][:], x1s[(blk + 1) * 128:(blk + 2) * 128, :], xb_[1 - s_], True)
            for n in range(2):
                for cp in range(NCP):
                    em.mm(Z[n], Z[n][:, 0:512], fT[:, cp, blk * 128:(blk + 1) * 128], w_dn[:, cp, n * 512:(n + 1) * 512],
                          cp == 0, cp == NCP - 1, [fT, w_dn])
                em.v("dve", "scalar_tensor_tensor", [r2s[s_]], [xb_[s_], Z[n]], out=r2s[s_][:, n * 512:(n + 1) * 512],
                     in0=xb_[s_][:, n * 512:(n + 1) * 512], scalar=ALPHA, in1=Z[n][:, 0:512], op0=ALU.mult, op1=ALU.add)
            layer_norm(r2s[s_], xn2[s_], ot[s_], 128, lnp2, stt2[s_], mv2[s_])
            em.dma("sp", y_out[blk * 128:(blk + 1) * 128, :], ot[s_][:], ot[s_], False)
        return done()


def kernel(**inputs):
    in_maps = prep_inputs(**{k: np.asarray(v) for k, v in inputs.items()})
    nc = build()
    res = run_bass_kernel_spmd(nc, in_maps, core_ids=list(range(8)))
    out = np.zeros((4, S, D), np.float32)
    for c in range(8):
        b, half = c // 2, c % 2
        out[b, half * NT:(half + 1) * NT] = res.results[c]["y"]
    return out
```

```python
import math
from contextlib import ExitStack

import numpy as np
import concourse.bass as bass
import concourse.mybir as mybir
from concourse.bass_utils import run_bass_kernel_spmd

F32 = mybir.dt.float32
BF16 = mybir.dt.bfloat16
AF = mybir.ActivationFunctionType
ALU = mybir.AluOpType
AX = mybir.AxisListType

ENGS = ("pe", "act", "dve", "pool", "sp")

D = 1024
S = 4096
NT = 2048
HALO = 2
NTO = NT + HALO
DFF = 2816
NCP = 22
IN_COLS = 9728
ALPHA = 2.0 ** 0.25
LAMBDA_INIT = 0.8 - 0.6 * math.exp(0.0)
C_SUBLN = 1.0 - LAMBDA_INIT
EPS = 1e-5
NEG = -30000.0
GROUP_DIL = (1, 4, 16)
NTAB = 73
SKIP_MARGIN = 150.0


class T:
    def __init__(self, name, ap):
        self.name = name
        self.ap = ap
        self.last_w = None
        self.readers = {}
        self.dma_sem = None
        self.dma_cnt = 0
        self.dma_wcnt = 0
        self.group = None

    def __getitem__(self, k):
        return self.ap[k]


class Op:
    __slots__ = ("eng", "fn", "waits", "inc", "idx", "is_dma")

    def __init__(self, eng, fn):
        self.eng = eng
        self.fn = fn
        self.waits = []
        self.inc = False
        self.idx = 0
        self.is_dma = False


class Emitter:
    def __init__(self, nc, stack):
        self.nc = nc
        self.stack = stack
        self.ops = {e: [] for e in ENGS}
        self.cops = {e: [] for e in ENGS}
        self.sems = {e: stack.enter_context(nc.semaphore("s_" + e)) for e in ENGS}
        self.waited = {e: {} for e in ENGS}
        self.tiles = []
        self.free_dsems = []
        self.ndsem = 0

    def tile(self, name, ap):
        t = T(name, ap)
        self.tiles.append(t)
        return t

    def _dsem(self, t):
        if t.dma_sem is None:
            t.dma_sem = self.stack.enter_context(self.nc.semaphore("d%d" % self.ndsem))
            self.ndsem += 1
        return t.dma_sem

    def _need(self, op, src_eng, idx):
        if idx is None or idx <= 0:
            return
        e = op.eng
        if src_eng == e and e == "pe":
            return
        w = self.waited[e]
        if w.get(src_eng, 0) >= idx:
            return
        w[src_eng] = idx
        op.waits.append(("eng", src_eng, idx))
        self.cops[src_eng][idx - 1].inc = True

    def _need_dma(self, op, t, writes_only=False):
        cnt = t.dma_wcnt if writes_only else t.dma_cnt
        if t.dma_sem is None or cnt == 0:
            return
        key = ("dma", id(t))
        w = self.waited[op.eng]
        if w.get(key, 0) >= cnt:
            return
        w[key] = cnt
        op.waits.append(("dma", t, cnt * 16))

    def op(self, eng, fn, reads=(), writes=()):
        o = Op(eng, fn)
        self.cops[eng].append(o)
        o.idx = len(self.cops[eng])
        for t in reads:
            if t.last_w is not None:
                self._need(o, *t.last_w)
            self._need_dma(o, t, writes_only=True)
        for t in writes:
            if t.last_w is not None:
                self._need(o, *t.last_w)
            for re_, ri in t.readers.items():
                self._need(o, re_, ri)
            self._need_dma(o, t)
        for t in reads:
            if t.readers.get(eng, 0) < o.idx:
                t.readers[eng] = o.idx
        for t in writes:
            t.last_w = (eng, o.idx)
            t.readers = {}
        self.ops[eng].append(o)
        return o

    def dma(self, eng, out_ap, in_ap, tile, is_load, group=None, also_read=()):
        t = tile
        sem = self._dsem(t)

        def fn(e, out_ap=out_ap, in_ap=in_ap, sem=sem):
            return e.dma_start(out=out_ap, in_=in_ap).then_inc(sem, 16)

        o = Op(eng, fn)
        o.is_dma = True
        if t.last_w is not None:
            self._need(o, *t.last_w)
        if is_load:
            for re_, ri in t.readers.items():
                self._need(o, re_, ri)
        if not (group is not None and t.group == group):
            self._need_dma(o, t)
        for t2 in also_read:
            if t2.last_w is not None:
                self._need(o, *t2.last_w)
        t.group = group
        t.dma_cnt += 1
        if is_load:
            t.dma_wcnt = t.dma_cnt
        for t2 in also_read:
            t2.dma_sem = sem
            t2.dma_cnt = t.dma_cnt
        if is_load:
            t.last_w = None
            t.readers = {}
        self.ops[eng].append(o)
        return o

    def barrier(self):
        snap = {e: len(self.cops[e]) for e in ENGS}
        for e in ENGS:
            o = Op(e, None)
            self.cops[e].append(o)
            o.idx = len(self.cops[e])
            for src in ENGS:
                if src != e:
                    self._need(o, src, snap[src])
            for t in self.tiles:
                self._need_dma(o, t)
            self.ops[e].append(o)
        self.tiles = [t for t in self.tiles if not getattr(t, "dead", False)]

    def emit(self):
        nc = self.nc
        em = self
        pref = {}
        for e in ENGS:
            c = 0
            p = [0]
            for o in em.cops[e]:
                if o.inc:
                    c += 1
                p.append(c)
            pref[e] = p
        with nc.Block() as block:
            def run(ename, eh):
                for o in em.ops[ename]:
                    for kind, src, val in o.waits:
                        if kind == "eng":
                            eh.wait_ge(em.sems[src], pref[src][val])
                        else:
                            eh.wait_ge(src.dma_sem, val)
                    ins = None
                    if o.fn is not None:
                        ins = o.fn(eh)
                    if o.is_dma:
                        continue
                    if o.inc:
                        if ins is None:
                            ins = eh.nop()
                        ins.then_inc(em.sems[ename], 1)

            @block.tensor
            def _(e):
                run("pe", e)

            @block.scalar
            def _(e):
                run("act", e)

            @block.vector
            def _(e):
                run("dve", e)

            @block.gpsimd
            def _(e):
                run("pool", e)

            @block.sync
            def _(e):
                run("sp", e)

    def mm(self, out_t, out_ap, lhsT, rhs, start, stop, reads, skip=False):
        return self.op("pe", lambda e: e.matmul(out_ap, lhsT=lhsT, rhs=rhs, start=start, stop=stop,
                                                skip_group_check=skip),
                       reads=reads, writes=[out_t])

    def act(self, out_t, out_ap, in_ap, func, reads, bias=None, scale=None):
        kw = {}
        if bias is not None:
            kw["bias"] = bias
        if scale is not None:
            kw["scale"] = scale
        return self.op("act", lambda e: e.activation(out=out_ap, in_=in_ap, func=func, **kw),
                       reads=reads, writes=[out_t])

    def v(self, eng, name, writes, reads, **kw):
        return self.op(eng, lambda e: getattr(e, name)(**kw), reads=reads, writes=writes)


class Arena:
    def __init__(self, em, nbytes):
        self.em = em
        self.n = nbytes
        self.h = em.stack.enter_context(em.nc.sbuf_tensor("arena", [128, nbytes // 2], BF16))
        self.lo = 0
        self.hi = nbytes

    def alloc(self, name, shape, dt, top=False, at=None):
        esz = 4 if dt == F32 else 2
        n = int(np.prod(shape[1:])) * esz
        n = (n + 63) // 64 * 64
        if at is not None:
            off = at
        elif top:
            self.hi -= n
            off = self.hi
        else:
            off = self.lo
            self.lo += n
        assert self.lo <= self.hi, "arena overflow at %s: lo=%d hi=%d" % (name, self.lo, self.hi)
        nel = int(np.prod(shape[1:]))
        a = self.h[:, off // 2: off // 2 + nel * esz // 2]
        if dt == F32:
            a = a.bitcast(F32)
        if len(shape) == 3:
            a = a.rearrange("p (a b) -> p a b", a=shape[1])
        elif len(shape) == 4:
            a = a.rearrange("p (a b c) -> p a b c", a=shape[1], b=shape[2])
        if shape[0] < 128:
            a = a[0:shape[0]]
        t = self.em.tile(name, a)
        t.off = off
        t.nbytes = n
        return t


def slopes8():
    return [2.0 ** (-(h + 1)) for h in range(8)]


def biasA_layout():
    cols = {}
    n = 0
    for r in range(5):
        njs = 16 if r == 0 else 16 + 4 * r
        for j in range(njs):
            for h in range(8):
                ns = 1 if (r == 0 or h >= 2) else (2 if h == 1 else 4)
                for s in range(ns):
                    cols[(r, j, h, s)] = n
                    n += 1
    return cols, n


BIAS_COLS, NBIAS = biasA_layout()
CST = {}
_o = 0
for _name, _n in (("bg", 16), ("lam4", 256), ("subln", 128), ("wc", 132), ("bc", 44), ("flag", 1),
                  ("biasA", NBIAS)):
    CST[_name] = _o
    _o += _n
NCST = _o


def nsub(r, h):
    return 1 if (r == 0 or h >= 2) else (2 if h == 1 else 4)


def build_biasA(half):
    m = slopes8()
    out = np.zeros((128, NBIAS), np.float32)
    p = np.arange(128, dtype=np.float64)
    for (r, j, h, s), c in BIAS_COLS.items():
        if r == 0:
            qc = 2047.0
        else:
            ws = 512 // nsub(r, h)
            qc = 2048 + 512 * (r - 1) + ws * s + ws / 2
        if half == 0 and r >= 1 and j < 16:
            out[:, c] = NEG
        else:
            out[:, c] = m[h] * (128 * j + p - qc)
    return out


def build_tabB(half):
    m = slopes8()
    kj = np.arange(128)[:, None].astype(np.float64)
    qi = np.arange(128)[None, :].astype(np.float64)
    tabs = np.zeros((128, NTAB, 128), np.float32)
    for g, dil in enumerate(GROUP_DIL):
        for h in range(8):
            cur = np.where(qi - kj >= 0, -m[h] * dil * (qi - kj), NEG)
            st = qi + 128 - kj
            prev = np.where(st <= 128, -m[h] * dil * st, NEG)
            prevx = prev if half == 1 else np.full_like(prev, NEG)
            b = (g * 8 + h) * 3
            tabs[:, b + 0] = cur
            tabs[:, b + 1] = prev
            tabs[:, b + 2] = prevx
    tabs[:, 72] = np.where(kj <= qi, 0.0, NEG)
    return tabs.reshape(128, NTAB * 128)


def prep_inputs(x, w_in, b_gate, lambda_q1, lambda_k1, lambda_q2, lambda_k2, subln_w,
                w_pa, w_pb, w_o, ln1_g, ln1_b, w_up, w_conv, b_conv, w_down, ln2_g, ln2_b):
    f = np.float32
    rep = lambda v: np.ascontiguousarray(np.broadcast_to(np.asarray(v, f).reshape(1, -1), (128, np.asarray(v).size)))
    shared = {
        "w_in": np.ascontiguousarray(w_in[0], f), "w_pa": np.ascontiguousarray(w_pa[0], f),
        "w_pb": np.ascontiguousarray(w_pb[0], f), "w_o": np.ascontiguousarray(w_o[0], f),
        "w_up": np.ascontiguousarray(w_up[0], f), "w_down": np.ascontiguousarray(w_down[0], f),
        "lnp": np.ascontiguousarray(np.concatenate([rep(ln1_g[0]), rep(ln1_b[0]), rep(ln2_g[0]), rep(ln2_b[0])], 1)),
    }
    cst_common = np.zeros((128, NCST), f)
    cst_common[:, CST["bg"]:CST["bg"] + 16] = np.asarray(b_gate[0], f).reshape(16, 128).T
    cst_common[:, CST["lam4"]:CST["lam4"] + 256] = np.concatenate(
        [rep(lambda_q1[0]), rep(lambda_k1[0]), rep(lambda_q2[0]), rep(lambda_k2[0])], 1)
    cst_common[:, CST["subln"]:CST["subln"] + 128] = rep(subln_w[0])
    wc = np.asarray(w_conv[0], f).reshape(3, 44, 128)
    cst_common[:, CST["wc"]:CST["wc"] + 132] = wc.transpose(2, 0, 1).reshape(128, 132)
    cst_common[:, CST["bc"]:CST["bc"] + 44] = np.asarray(b_conv[0], f).reshape(44, 128).T
    tabs = [build_tabB(0), build_tabB(1)]
    biases = [build_biasA(0), build_biasA(1)]
    in_maps = []
    for c in range(8):
        b, half = c // 2, c % 2
        xb = np.asarray(x[b], f)
        if half == 1:
            xT = np.ascontiguousarray(xb.T)
            xown = np.ascontiguousarray(xb[2046:4096])
        else:
            xT = np.zeros((D, S), f)
            xT[:, 2048:] = xb[:2048].T
            xown = np.zeros((NTO, D), f)
            xown[2:] = xb[:2048]
        cst = cst_common.copy()
        cst[:, CST["flag"]] = float(half)
        cst[:, CST["biasA"]:CST["biasA"] + NBIAS] = biases[half]
        m = dict(shared)
        m.update({"xT": xT, "xown": xown, "cst": cst, "tabB": tabs[half]})
        in_maps.append(m)
    return in_maps


def build(stop_after=None, dbg=()):
    nc = bass.Bass("TRN2", target_bir_lowering=False)
    dr = {}
    for name, shape in (("xT", [D, S]), ("xown", [NTO, D]), ("w_in", [D, IN_COLS]), ("w_pa", [D, D]),
                        ("w_pb", [512, D]), ("w_o", [D, D]), ("w_up", [D, 2 * DFF]), ("w_down", [DFF, D]),
                        ("lnp", [128, 4 * D]), ("cst", [128, NCST]), ("tabB", [128, NTAB * 128])):
        dr[name] = nc.dram_tensor(name, shape, F32, kind="ExternalInput").ap()
    y_out = nc.dram_tensor("y", [NT, D], F32, kind="ExternalOutput").ap()
    x1s = nc.dram_tensor("x1s", [NT, D], F32).ap()
    dbg_out = {}
    for name, shape in dbg:
        dbg_out[name] = nc.dram_tensor(name, shape, F32, kind="ExternalOutput").ap()

    w_in_r = dr["w_in"].rearrange("(kc p) n -> p kc n", p=128)
    xT_r = dr["xT"].rearrange("(kc p) t -> p kc t", p=128)
    m_sl = slopes8()

    with ExitStack() as st:
        em = Emitter(nc, st)
        ar = Arena(em, 212000)
        ps_all = st.enter_context(nc.psum_tensor("ps_all", [128, 4096], F32))
        banks = [em.tile("bank%d" % i, ps_all[:, 512 * i:512 * (i + 1)]) for i in range(8)]

        def done():
            em.barrier()
            em.emit()
            return nc

        def dump(name, t, ap):
            if name not in dbg_out:
                return
            em.dma("sp", dbg_out[name], ap, t, False)

        cst = ar.alloc("cst", [128, NCST], F32)
        ident = ar.alloc("ident", [128, 128], BF16)
        i32 = ar.alloc("i32", [128, 128], F32)
        small = ar.alloc("small", [128, 64], F32)
        neghalf = ar.alloc("neghalf", [128, 16], F32)
        sublnc = ar.alloc("sublnc", [128, 128], F32)
        ones32 = ar.alloc("ones32", [128, 64], F32)
        em.dma("sp", cst[:], dr["cst"], cst, True)
        em.v("pool", "memset", [i32], [], ap=i32[:], constant=0.0)
        em.op("pool", lambda e: e.affine_select(out=i32[:], in_=i32[:], pattern=[[-1, 128]],
                                                 compare_op=ALU.not_equal, fill=1.0, base=0,
                                                 channel_multiplier=1), reads=[i32], writes=[i32])
        em.v("dve", "tensor_copy", [ident], [i32], out=ident[:], in_=i32[:])
        em.v("pool", "memset", [neghalf], [], ap=neghalf[:], constant=-0.5)
        em.v("pool", "memset", [ones32], [], ap=ones32[:], constant=1.0)
        L0 = CST["lam4"]
        lamt = ar.alloc("lamt", [128, 128], F32)
        em.v("dve", "tensor_tensor", [lamt], [cst], out=lamt[:, 0:64], in0=cst[:, L0:L0 + 64],
             in1=cst[:, L0 + 64:L0 + 128], op=ALU.mult)
        em.v("dve", "tensor_tensor", [lamt], [cst, lamt], out=lamt[:, 64:128], in0=cst[:, L0 + 128:L0 + 192],
             in1=cst[:, L0 + 192:L0 + 256], op=ALU.mult)
        em.v("dve", "reduce_sum", [small], [lamt], out=small[:, 1:2], in_=lamt[:, 0:64], axis=AX.X)
        em.v("dve", "reduce_sum", [small], [lamt, small], out=small[:, 2:3], in_=lamt[:, 64:128], axis=AX.X)
        em.act(small, small[:, 3:5], small[:, 1:3], AF.Exp, [small])
        em.v("dve", "tensor_tensor", [small], [small], out=small[:, 5:6], in0=small[:, 4:5], in1=small[:, 3:4],
             op=ALU.subtract)
        em.v("dve", "tensor_scalar", [small], [small], out=small[:, 0:1], in0=small[:, 5:6],
             scalar1=-LAMBDA_INIT, scalar2=None, op0=ALU.add)
        em.v("dve", "tensor_scalar", [sublnc], [cst], out=sublnc[:], in0=cst[:, CST["subln"]:CST["subln"] + 128],
             scalar1=C_SUBLN, scalar2=None, op0=ALU.mult)
        neglam = small[:, 0:1]
        flag_ap = cst[:, CST["flag"]:CST["flag"] + 1]
        BA = CST["biasA"]

        M0 = (ar.lo, ar.hi)
        xT_all = ar.alloc("xT_bf", [128, 8, S], BF16)
        xTv = []
        for i in range(8):
            v_ = em.tile("xTv%d" % i, xT_all.ap)
            em.dma("pool", v_[:, :, i * 512:(i + 1) * 512], xT_r[:, :, i * 512:(i + 1) * 512], v_, True)
            xTv.append(v_)
        xT = xT_all.ap

        def xview(ctx0):
            return xTv[ctx0 // 512]

        oaT = ar.alloc("oaT", [128, 8, NTO], BF16)
        obT = ar.alloc("obT", [128, 4, NTO], BF16)
        M1 = (ar.lo, ar.hi)
        maskAt = ar.alloc("maskA", [128, 128], BF16)
        em.dma("pool", maskAt[:], dr["tabB"][:, 72 * 128:73 * 128], maskAt, True)
        maskA = maskAt.ap
        Qp = [ar.alloc("Qp%d" % m, [128, NTO], BF16) for m in range(2)]
        KT = ar.alloc("KT", [128, S], BF16)
        VV = ar.alloc("VV", [128, 32 * 4 * 129 + 128], BF16)
        V2 = VV.ap[:, 0:32 * 2 * 129].rearrange("p (b h e) -> p b h e", b=32, h=2)
        em.v("pool", "memset", [VV], [], ap=V2[:, :, :, 128:129], constant=1.0)
        v2end = VV.off + 32 * 2 * 129 * 2
        KTb = ar.alloc("KTb", [128, S], BF16, at=v2end)
        Qpb = [ar.alloc("Qpb%d" % m, [128, NTO], BF16, at=v2end + 8192 + m * 4104) for m in range(2)]
        assert v2end + 8192 + 2 * 4104 <= VV.off + VV.nbytes
        Wq = ar.alloc("Wq", [128, 8, 128], BF16)
        Wk = ar.alloc("Wk", [128, 8, 128], BF16)
        Wv = ar.alloc("Wv", [128, 8, 512], BF16)
        Wv2t = ar.alloc("Wv2", [128, 8, 256], BF16, at=Wv.off)
        Wv2 = Wv2t.ap
        Wqb_ = ar.alloc("WqB", [128, 8, 128], BF16, at=Wv.off + 4096)
        Wkb_ = ar.alloc("WkB", [128, 8, 128], BF16, at=Wv.off + 6144)
        KTs = [KT, KTb]
        Qps = [Qp, Qpb]
        Wqs = [Wq, Wqb_]
        Wks = [Wk, Wkb_]
        for qq in Qps:
            em.v("pool", "memset", [qq[0]], [], ap=qq[0][64:128, :], constant=0.0)
            em.v("pool", "memset", [qq[1]], [], ap=qq[1][0:64, :], constant=0.0)
        QT = [(0, 2, 2046)] + [(2 + 512 * i, 512, 2048 + 512 * i) for i in range(4)]

        mA = (ar.lo, ar.hi)
        P = [ar.alloc("P%d" % b, [128, 2, 512], BF16) for b in range(2)]
        tA = ar.alloc("tA", [128, 4, 128], F32)
        tO = ar.alloc("tO", [128, 4, 128], F32)
        oab = ar.alloc("oab", [128, 4, 128], BF16)
        rc = ar.alloc("rc", [128, 24], F32)
        Uc = ar.alloc("Uc", [128, 8, 129], F32)
        Sb = [[banks[0], banks[2]], [banks[1], banks[3]]]
        Spair = [ps_all[:, 1024 * b:1024 * (b + 1)].rearrange("p (m c) -> p m c", m=2) for b in range(2)]
        Ub = [banks[4], banks[5], banks[6]]
        X = banks[7]
        Uv = []
        Ut = []
        for u in range(8):
            bk, slot = Ub[u // 3], u % 3
            Uv.append(bk.ap[:, slot * 129:(slot + 1) * 129])
            Ut.append(bk)
        Xbf = X.ap.bitcast(BF16)
        pool5 = [banks[0], banks[1], banks[2], banks[3], banks[7]]
        rot = [0]

        def nextbank():
            b = pool5[rot[0] % 5]
            rot[0] += 1
            return b

        def kq_units(h, bankfn):
            pb = h % 2
            kt, qp, wq, wk = KTs[pb], Qps[pb], Wqs[pb], Wks[pb]
            units = []

            def wload():
                em.dma("pool", wq[:], w_in_r[:, :, h * 128:(h + 1) * 128], wq, True)
                em.dma("pool", wk[:], w_in_r[:, :, 1024 + h * 128:1024 + (h + 1) * 128], wk, True)
            units.append([wload])
            for tt in range(8):
                cell = {}
                mic = []
                for kc in range(8):
                    def kmm(kc=kc, tt=tt, cell=cell):
                        if kc == 0:
                            cell["bk"] = bankfn()
                        bk = cell["bk"]
                        em.mm(bk, bk[:, 0:512], wk[:, kc, :], xT[:, kc, tt * 512:(tt + 1) * 512], kc == 0, kc == 7,
                              [wk, xTv[tt]])
                    mic.append(kmm)

                def kev(tt=tt, cell=cell):
                    bk = cell["bk"]
                    em.v("dve", "tensor_copy", [kt], [bk], out=kt[:, tt * 512:(tt + 1) * 512], in_=bk[:, 0:512])
                mic.append(kev)
                units.append(mic)
            for (t0, w, c0) in QT:
                cell = {}
                mic = []
                for kc in range(8):
                    def qmm(kc=kc, t0=t0, w=w, c0=c0, cell=cell):
                        if kc == 0:
                            cell["bk"] = bankfn()
                        bk = cell["bk"]
                        em.mm(bk, bk[:, 0:w], wq[:, kc, :], xT[:, kc, c0:c0 + w], kc == 0, kc == 7, [wq, xview(c0)])
                    mic.append(qmm)

                def qev(t0=t0, w=w, cell=cell):
                    bk = cell["bk"]
                    em.v("dve", "tensor_scalar", [qp[0]], [bk], out=qp[0][0:64, t0:t0 + w], in0=bk[0:64, 0:w],
                         scalar1=0.125, scalar2=None, op0=ALU.mult)
                    em.v("dve", "tensor_scalar", [qp[1]], [bk], out=qp[1][64:128, t0:t0 + w], in0=bk[64:128, 0:w],
                         scalar1=0.125, scalar2=None, op0=ALU.mult)
                mic.append(qev)
                units.append(mic)
            return units

        for u_ in kq_units(0, nextbank):
            for mi_ in u_:
                mi_()
        deferred = []
        for h in range(8):
            hh = h % 2
            KTh, Qph = KTs[h % 2], Qps[h % 2]
            if hh == 0:
                em.dma("pool", Wv2, w_in_r[:, :, 2048 + h * 128:2048 + (h + 2) * 128], Wv2t, True)
                for blk in range(32):
                    bk = nextbank()
                    for kc in range(8):
                        em.mm(bk, bk[:, 0:256], xT[:, kc, blk * 128:(blk + 1) * 128], Wv2[:, kc, :], kc == 0, kc == 7,
                              [Wv2t, xview(blk * 128)])
                    em.v("dve", "tensor_copy", [VV], [bk], out=V2[:, blk, :, 0:128],
                         in_=bk[:, 0:256].rearrange("p (h e) -> p h e", h=2))
            pend_units = kq_units(h + 1, lambda: X) if h < 7 else []
            pending = []

            def pump(n):
                while n > 0 and (pending or pend_units):
                    if not pending:
                        pending.extend(pend_units.pop(0))
                    pending.pop(0)()
                    n -= 1

            def flush_unit():
                while pending:
                    pending.pop(0)()
            plan = []
            for r in range(5):
                t0, w, _c = QT[r]
                if r == 0:
                    steps = [(j, 0, False) for j in range(15)] + [(15, 0, True)]
                    nfull = 15
                else:
                    nfull = 16 + 4 * (r - 1)
                    steps = [(j, 0, False) for j in range(nfull)] + [(nfull + d, 128 * d, True) for d in range(4)]
                ns = nsub(r, h)
                ws = w // ns
                qc0 = 2047.0 if r == 0 else 2048 + 512 * (r - 1) + ws / 2.0
                keep = [st_ for st_ in steps if st_[2] or m_sl[h] * (qc0 - (128 * st_[0] + 127)) <= SKIP_MARGIN]
                if not any(not st_[2] for st_ in keep):
                    keep = [st_ for st_ in steps if not st_[2]][-1:] + keep
                plan.append((keep, nfull, ns, ws))
            total_steps = sum(len(p_[0]) for p_ in plan)
            n_micro = sum(len(u_) for u_ in pend_units)
            rate = n_micro / float(max(total_steps, 1))
            credit = 0.0
            for r in range(5):
                t0, w, _c = QT[r]
                steps, nfull, ns, ws = plan[r]
                if r == 0:
                    dw, mc0, nblk, nq = 2, 126, 1, 2
                else:
                    dw, mc0, nblk, nq = 128, 0, 4, 128

                def QK(si):
                    j, c0, diag = steps[si]
                    buf = si % 2
                    kblk = KTh[:, j * 128:(j + 1) * 128]
                    for m in range(2):
                        sb = Sb[m][buf]
                        if not diag:
                            em.mm(sb, sb[:, 0:w], kblk, Qph[m][:, t0:t0 + w], True, True, [KTh, Qph[m]])
                        else:
                            em.mm(sb, sb[:, c0:c0 + dw], kblk, Qph[m][:, t0 + c0:t0 + c0 + dw], True, False, [KTh, Qph[m]])
                            em.mm(sb, sb[:, c0:c0 + dw], ident[:], maskA[:, mc0:mc0 + dw], False, True, [ident, maskAt])
                            if c0 + dw < w:
                                em.mm(sb, sb[:, c0 + dw:w], kblk, Qph[m][:, t0 + c0 + dw:t0 + w], True, True, [KTh, Qph[m]])

                def EXP(si):
                    j, c0, diag = steps[si]
                    buf = si % 2
                    for s_ in range(ns):
                        a, b_ = max(s_ * ws, c0), (s_ + 1) * ws
                        if b_ <= a:
                            continue
                        col = BA + BIAS_COLS[(r, j, h, s_)]
                        em.act(P[buf], P[buf][:, :, a:b_], Spair[buf][:, :, a:b_], AF.Exp,
                               [Sb[0][buf], Sb[1][buf], cst], bias=cst[:, col:col + 1])

                def PV(si):
                    j, c0, diag = steps[si]
                    buf = si % 2
                    seen = set()
                    for m in range(2):
                        for b_ in range(nblk):
                            if 128 * b_ < c0:
                                continue
                            u = m * 4 + b_
                            last = nfull + b_
                            first = (si == 0 and (u // 3) not in seen)
                            seen.add(u // 3)
                            em.mm(Ut[u], Uv[u][0:nq, 0:129], P[buf][:, m, 128 * b_:128 * b_ + nq],
                                  V2[:, j, hh, :], first, j == last, [P[buf], VV], skip=True)

                QK(0)
                if len(steps) > 1:
                    QK(1)
                for si in range(len(steps)):
                    EXP(si)
                    if si + 2 < len(steps):
                        QK(si + 2)
                    PV(si)
                    credit += rate
                    k_ = int(credit)
                    credit -= k_
                    pump(k_)
                if deferred:
                    flush_unit()
                for d_ in deferred:
                    d_()
                deferred = []
                nb = nblk
                Ucv = Uc.ap
                if nb == 4:
                    for bkI, (ua, ub) in enumerate(((0, 3), (3, 6), (6, 8))):
                        em.v("dve", "tensor_copy", [Uc], [Ub[bkI]],
                             out=Ucv[:, ua:ub, :].rearrange("p a b -> p (a b)"), in_=Ub[bkI][:, 0:(ub - ua) * 129])
                else:
                    em.v("dve", "tensor_copy", [Uc], [Ut[0]], out=Ucv[0:nq, 0, :], in_=Uv[0][0:nq, :])
                    em.v("dve", "tensor_copy", [Uc], [Ut[4]], out=Ucv[0:nq, 4, :], in_=Uv[4][0:nq, :])
                r0 = rc[0:nq, 0:nb]
                r1 = rc[0:nq, 4:4 + nb]
                em.v("dve", "reciprocal", [rc], [Uc], out=r0, in_=Ucv[0:nq, 0:nb, 128])
                em.v("dve", "reciprocal", [rc], [Uc, rc], out=r1, in_=Ucv[0:nq, 4:4 + nb, 128])
                a4 = tA[0:nq, 0:nb, :]
                o4 = tO[0:nq, 0:nb, :]
                bc = lambda ap2, nq=nq, nb=nb: ap2.unsqueeze(2).to_broadcast([nq, nb, 128])
                em.v("dve", "scalar_tensor_tensor", [tA], [Uc, rc, small], out=a4, in0=Ucv[0:nq, 4:4 + nb, 0:128],
                     scalar=neglam[0:nq, :], in1=bc(r1), op0=ALU.mult, op1=ALU.mult)
                em.v("dve", "tensor_tensor", [tO], [Uc, rc], out=o4, in0=Ucv[0:nq, 0:nb, 0:128], in1=bc(r0), op=ALU.mult)
                em.v("dve", "tensor_tensor", [tO], [tO, tA], out=o4, in0=o4, in1=a4, op=ALU.add)
                em.v("dve", "tensor_tensor", [tA], [tO], out=a4, in0=o4, in1=o4, op=ALU.mult)
                em.v("dve", "reduce_sum", [rc], [tA], out=rc[0:nq, 8:8 + nb], in_=a4, axis=AX.X)
                em.v("dve", "tensor_scalar", [rc], [rc], out=rc[0:nq, 12:12 + nb], in0=rc[0:nq, 8:8 + nb],
                     scalar1=1.0 / 128, scalar2=EPS, op0=ALU.mult, op1=ALU.add)
                em.v("pool", "tensor_tensor", [rc], [rc, neghalf], out=rc[0:nq, 16:16 + nb], in0=rc[0:nq, 12:12 + nb],
                     in1=neghalf[0:nq, 0:nb], op=ALU.pow)
                em.v("dve", "tensor_tensor", [tO], [tO, rc], out=o4, in0=o4, in1=bc(rc[0:nq, 16:16 + nb]), op=ALU.mult)
                em.v("dve", "tensor_tensor", [oab], [tO, sublnc], out=oab[0:nq, 0:nb, :], in0=o4,
                     in1=sublnc[0:nq, :].unsqueeze(1).to_broadcast([nq, nb, 128]), op=ALU.mult)

                def part2(nq=nq, nb=nb, t0=t0, h=h):
                    for b_ in range(nb):
                        em.op("pe", lambda e, b_=b_: e.transpose(Xbf[:, b_ * nq:(b_ + 1) * nq], oab[0:nq, b_, :],
                                                                 ident[0:nq, 0:nq]),
                              reads=[oab, ident], writes=[X])
                    em.v("dve", "tensor_copy", [oaT], [X], out=oaT[:, h, t0:t0 + nb * nq], in_=Xbf[:, 0:nb * nq])
                deferred.append(part2)
            pump(1 << 30)
        for d_ in deferred:
            d_()
        def dump3(name, src, nchunk):
            if name not in dbg_out:
                return
            em.barrier()
            stg = VV.ap[:, 0:2 * NTO].bitcast(F32)
            for c_ in range(nchunk):
                em.v("dve", "tensor_copy", [VV], [src], out=stg, in_=src[:, c_, :])
                em.dma("sp", dbg_out[name][:, c_, :], stg, VV, False)
            em.barrier()

        dump3("oaT", oaT, 8)
        if stop_after == "A":
            return done()
        em.barrier()
        ar.lo, ar.hi = mA

        tabB = ar.alloc("tabB", [128, NTAB - 1, 128], BF16)
        em.dma("pool", tabB[:], dr["tabB"][:, 0:72 * 128].rearrange("p (t q) -> p t q", t=NTAB - 1), tabB, True)
        Pb = [ar.alloc("Pb%d" % b, [128, 2, 128], BF16) for b in range(2)]
        shiftI = ar.alloc("shiftI", [128, 128], BF16)
        em.v("pool", "memset", [shiftI], [], ap=shiftI[:], constant=0.0)
        em.v("dve", "tensor_copy", [shiftI], [ident], out=shiftI[0:64, 64:128], in_=ident[0:64, 0:64])
        vtail = VV.off + 8320
        acc = [ar.alloc("acc%d" % i, [128, NTO], F32, at=vtail + i * 8200) for i in range(2)]
        rE = ar.alloc("rE", [128, NTO], F32, at=vtail + 2 * 8200)
        assert vtail + 3 * 8200 <= VV.off + VV.nbytes
        obtmp = ar.alloc("obtmp", [128, NTO], BF16, at=Wv.off)
        Wvb = ar.alloc("Wvb", [128, 8, 128], BF16, at=Wv.off + 4160)
        assert 4160 + 2048 <= Wv.nbytes
        Vp = VV.ap[:, 0:32 * 2 * 65].rearrange("p (b h e) -> p b h e", b=32, h=2)
        Wqb, Wkb = Wq, Wk
        SbB = [banks[0], banks[1]]
        ObB = [banks[2], banks[3]]
        poolB = [banks[4], banks[5], banks[6], banks[7]]
        rotB = [0]

        def nextbankB():
            b = poolB[rotB[0] % 4]
            rotB[0] += 1
            return b

        def chunk_tokens(g, c, r_):
            dil = GROUP_DIL[g]
            s0 = 128 * dil * c + r_
            return s0, dil

        for hp in range(4):
            for g in range(3):
                dil = GROUP_DIL[g]
                base = 3072 + g * 512 + hp * 128
                em.dma("pool", Wqb[:], w_in_r[:, :, base:base + 128], Wqb, True)
                em.dma("pool", Wkb[:], w_in_r[:, :, base + 1536:base + 1536 + 128], Wkb, True)
                em.dma("pool", Wvb[:], w_in_r[:, :, base + 3072:base + 3072 + 128], Wvb, True)
                first_tok = {0: 14 * 128, 1: 2 * 512, 2: 0}[g]
                for tt in range(first_tok // 512, 8):
                    bk = nextbankB()
                    for kc in range(8):
                        em.mm(bk, bk[:, 0:512], Wkb[:, kc, :], xT[:, kc, tt * 512:(tt + 1) * 512], kc == 0, kc == 7,
                              [Wkb, xTv[tt]])
                    em.v("dve", "tensor_copy", [KT], [bk], out=KT[:, tt * 512:(tt + 1) * 512], in_=bk[:, 0:512])
                for (t0, w, c0) in QT:
                    bk = nextbankB()
                    for kc in range(8):
                        em.mm(bk, bk[:, 0:w], Wqb[:, kc, :], xT[:, kc, c0:c0 + w], kc == 0, kc == 7, [Wqb, xview(c0)])
                    em.v("dve", "tensor_scalar", [Qp[0]], [bk], out=Qp[0][0:64, t0:t0 + w], in0=bk[0:64, 0:w],
                         scalar1=0.125, scalar2=None, op0=ALU.mult)
                    em.v("dve", "tensor_scalar", [Qp[1]], [bk], out=Qp[1][64:128, t0:t0 + w], in0=bk[64:128, 0:w],
                         scalar1=0.125, scalar2=None, op0=ALU.mult)
                if g == 0 and hp == 0:
                    pass
                for cid in range(32):
                    c_, r_ = cid // dil, cid % dil
                    s0 = 128 * dil * c_ + r_
                    if 128 * dil * c_ < first_tok:
                        continue
                    bk = nextbankB()
                    for kc in range(8):
                        em.mm(bk, bk[:, 0:128], xT[:, kc, s0:s0 + 127 * dil + 1:dil], Wvb[:, kc, :], kc == 0, kc == 7,
                              [Wvb] + xTv)
                    em.v("dve", "tensor_copy", [VV], [bk], out=Vp[:, cid, :, 0:64],
                         in_=bk[:, 0:128].rearrange("p (h e) -> p h e", h=2))
                if g == 0 and hp == 0:
                    em.v("pool", "memset", [VV], [], ap=Vp[:, :, :, 64:65], constant=1.0)
                items = []
                if g == 0:
                    items.append((15, 126, 128, 0, 1, 1))
                    for c_ in range(16, 32):
                        items.append((c_, 0, 128, 2 + 128 * (c_ - 16), 1, 2 if c_ == 16 else 1))
                elif g == 1:
                    items.append((3 * 4 + 2, 127, 128, 0, 1, 1))
                    items.append((3 * 4 + 3, 127, 128, 1, 1, 1))
                    for c_ in range(4, 8):
                        for r_ in range(4):
                            items.append((c_ * 4 + r_, 0, 128, 2 + 512 * (c_ - 4) + r_, 4, 2 if c_ == 4 else 1))
                else:
                    items.append((14, 127, 128, 0, 1, None))
                    items.append((15, 127, 128, 1, 1, None))
                    for r_ in range(16):
                        items.append((16 + r_, 0, 128, 2 + r_, 16, 2))
                for hd in range(2):
                    head = 2 * hp + hd
                    tb = (g * 8 + head) * 3

                    def kchunk(cid):
                        c_, r_ = cid // dil, cid % dil
                        s0 = 128 * dil * c_ + r_
                        return KT[:, s0:s0 + 127 * dil + 1:dil]

                    def QKb(ii):
                        cid, i0, i1, tq, tstep, pk = items[ii]
                        n = i1 - i0
                        sb = SbB[ii % 2]
                        qa = Qp[hd][:, tq:tq + (n - 1) * tstep + 1:tstep]
                        if pk is not None:
                            em.mm(sb, sb[:, 0:n], kchunk(cid - dil), qa, True, False, [KT, Qp[hd]])
                            em.mm(sb, sb[:, 0:n], ident[:], tabB[:, tb + pk, i0:i1], False, True, [ident, tabB])
                        em.mm(sb, sb[:, 128:128 + n], kchunk(cid), qa, True, False, [KT, Qp[hd]])
                        em.mm(sb, sb[:, 128:128 + n], ident[:], tabB[:, tb + 0, i0:i1], False, True, [ident, tabB])

                    def EXPb(ii):
                        cid, i0, i1, tq, tstep, pk = items[ii]
                        n = i1 - i0
                        sb = SbB[ii % 2]
                        pb = Pb[ii % 2]
                        if pk is not None and n == 128:
                            em.act(pb, pb[:].rearrange("p a b -> p (a b)"), sb[:, 0:256], AF.Exp, [sb])
                        else:
                            if pk is not None:
                                em.act(pb, pb[:, 0, 0:n], sb[:, 0:n], AF.Exp, [sb])
                            em.act(pb, pb[:, 1, 0:n], sb[:, 128:128 + n], AF.Exp, [sb])

                    def PVb(ii):
                        cid, i0, i1, tq, tstep, pk = items[ii]
                        n = i1 - i0
                        pb = Pb[ii % 2]
                        ob = ObB[ii % 2]
                        if pk is not None:
                            em.mm(ob, ob[0:65, 0:n], Vp[:, cid - dil, hd, :], pb[:, 0, 0:n], True, False, [VV, pb])
                        em.mm(ob, ob[0:65, 0:n], Vp[:, cid, hd, :], pb[:, 1, 0:n], pk is None, True, [VV, pb])
                        dst = acc[hd][0:65, tq:tq + (n - 1) * tstep + 1:tstep]
                        if g == 0:
                            em.v("dve", "tensor_copy", [acc[hd]], [ob], out=dst, in_=ob[0:65, 0:n])
                        else:
                            em.v("dve", "tensor_tensor", [acc[hd]], [ob, acc[hd]], out=dst, in0=dst, in1=ob[0:65, 0:n],
                                 op=ALU.add)

                    QKb(0)
                    for ii in range(len(items)):
                        EXPb(ii)
                        if ii + 1 < len(items):
                            QKb(ii + 1)
                        PVb(ii)
            for hd in range(2):
                em.act(rE, rE[64:65, :], acc[hd][64:65, :], AF.Ln, [acc[hd]])
                em.act(rE, rE[64:65, :], rE[64:65, :], AF.Exp, [rE], scale=-1.0)
                for c0 in range(0, NTO, 512):
                    w = min(512, NTO - c0)
                    bk = nextbankB()
                    em.mm(bk, bk[0:64, 0:w], ones32[64:65, :], rE[64:65, c0:c0 + w], True, True, [ones32, rE])
                    if hd == 0:
                        em.v("dve", "tensor_tensor", [obT], [bk, acc[hd]], out=obT[0:64, hp, c0:c0 + w],
                             in0=acc[hd][0:64, c0:c0 + w], in1=bk[0:64, 0:w], op=ALU.mult)
                    else:
                        em.v("dve", "tensor_tensor", [obtmp], [bk, acc[hd]], out=obtmp[0:64, c0:c0 + w],
                             in0=acc[hd][0:64, c0:c0 + w], in1=bk[0:64, 0:w], op=ALU.mult)
                if hd == 1:
                    for c0 in range(0, NTO, 512):
                        w = min(512, NTO - c0)
                        bk = nextbankB()
                        em.mm(bk, bk[:, 0:w], shiftI[0:64, :], obtmp[0:64, c0:c0 + w], True, True, [shiftI, obtmp])
                        em.v("dve", "tensor_copy", [obT], [bk], out=obT[64:128, hp, c0:c0 + w], in_=bk[64:128, 0:w])
        dump3("obT", obT, 4)
        if stop_after == "B":
            return done()
        em.barrier()
        ar.lo, ar.hi = M1

        w_pa_r = dr["w_pa"].rearrange("(kc p) n -> p kc n", p=128)
        w_pb_r = dr["w_pb"].rearrange("(kc p) n -> p kc n", p=128)
        w_o_r = dr["w_o"].rearrange("(kc p) n -> p kc n", p=128)
        w_up_r = dr["w_up"].rearrange("(kc p) n -> p kc n", p=128)
        w_down_r = dr["w_down"].rearrange("(c p) n -> p c n", p=128)
        yT = ar.alloc("yT", [128, 8, NTO], BF16, top=True)
        WgA = [ar.alloc("WgA%d" % i, [128, 8, 128], BF16) for i in range(2)]
        WgB = [ar.alloc("WgB%d" % i, [128, 8, 128], BF16) for i in range(2)]
        Wpa = [ar.alloc("Wpa%d" % i, [128, 8, 128], BF16) for i in range(2)]
        Wpb = [ar.alloc("Wpb%d" % i, [128, 4, 128], BF16) for i in range(2)]
        g0s = [ar.alloc("g0s%d" % i, [128, 512], F32) for i in range(2)]
        g1s = [ar.alloc("g1s%d" % i, [128, 512], F32) for i in range(2)]
        tt = ar.alloc("tt", [128, 512], F32)
        tu = ar.alloc("tu", [128, 512], F32)
        BG = CST["bg"]
        it = 0
        for c in range(8):
            bi = c % 2
            em.dma("pool", WgA[bi][:], w_in_r[:, :, 7680 + c * 128:7680 + (c + 1) * 128], WgA[bi], True)
            em.dma("pool", WgB[bi][:], w_in_r[:, :, 8704 + c * 128:8704 + (c + 1) * 128], WgB[bi], True)
            em.dma("pool", Wpa[bi][:], w_pa_r[:, :, c * 128:(c + 1) * 128], Wpa[bi], True)
            em.dma("pool", Wpb[bi][:], w_pb_r[:, :, c * 128:(c + 1) * 128], Wpb[bi], True)
            for (t0, w, c0) in QT:
                st_ = it % 2
                it += 1
                G0, G1, PA, PB = banks[st_ * 4:st_ * 4 + 4]
                for kc in range(8):
                    em.mm(G0, G0[:, 0:w], WgA[bi][:, kc, :], xT[:, kc, c0:c0 + w], kc == 0, kc == 7, [WgA[bi], xview(c0)])
                for kc in range(8):
                    em.mm(G1, G1[:, 0:w], WgB[bi][:, kc, :], xT[:, kc, c0:c0 + w], kc == 0, kc == 7, [WgB[bi], xview(c0)])
                for fc in range(8):
                    em.mm(PA, PA[:, 0:w], Wpa[bi][:, fc, :], oaT[:, fc, t0:t0 + w], fc == 0, fc == 7, [Wpa[bi], oaT])
                for fc in range(4):
                    em.mm(PB, PB[:, 0:w], Wpb[bi][:, fc, :], obT[:, fc, t0:t0 + w], fc == 0, fc == 3, [Wpb[bi], obT])
                em.act(g0s[st_], g0s[st_][:, 0:w], G0[:, 0:w], AF.Sigmoid, [G0, cst], bias=cst[:, BG + c:BG + c + 1])
                em.act(g1s[st_], g1s[st_][:, 0:w], G1[:, 0:w], AF.Sigmoid, [G1, cst], bias=cst[:, BG + 8 + c:BG + 9 + c])
                em.v("dve", "tensor_tensor", [tt], [g0s[st_], PA], out=tt[:, 0:w], in0=g0s[st_][:, 0:w], in1=PA[:, 0:w],
                     op=ALU.mult)
                em.v("dve", "tensor_tensor", [tu], [g1s[st_], PB], out=tu[:, 0:w], in0=g1s[st_][:, 0:w], in1=PB[:, 0:w],
                     op=ALU.mult)
                em.v("dve", "tensor_tensor", [yT], [tt, tu], out=yT[:, c, t0:t0 + w], in0=tt[:, 0:w], in1=tu[:, 0:w],
                     op=ALU.add)
        dump3("yT", yT, 8)
        if stop_after == "C1":
            return done()
        em.barrier()
        ar.lo = M0[0]

        x1T = ar.alloc("x1T", [128, 8, NTO], BF16)
        M2 = ar.lo
        w_o = ar.alloc("w_o", [128, 8, D], BF16)
        lnp = ar.alloc("lnp", [128, 2, D], F32)
        xo = [ar.alloc("xo%d" % i, [128, D], F32) for i in range(2)]
        rrs = [ar.alloc("rr%d" % i, [128, D], F32) for i in range(2)]
        xns = [ar.alloc("xn%d" % i, [128, D], F32) for i in range(2)]
        stts = [ar.alloc("stt%d" % i, [128, 2, 6], F32) for i in range(2)]
        mvs = [ar.alloc("mv%d" % i, [128, 8], F32) for i in range(2)]
        x1t = [ar.alloc("x1t%d" % i, [128, D], F32) for i in range(2)]
        x1bs = [ar.alloc("x1b%d" % i, [128, D], BF16) for i in range(2)]
        stt = ar.alloc("stt", [128, 2, 6], F32)
        mv = ar.alloc("mv", [128, 8], F32)
        for hf in range(2):
            em.dma("pool", w_o[:, :, hf * 512:(hf + 1) * 512], w_o_r[:, :, hf * 512:(hf + 1) * 512], w_o, True, group="wo")
        em.dma("sp", lnp[:].rearrange("p a b -> p (a b)"), dr["lnp"][:, 0:2 * D], lnp, True)

        def layer_norm(src, xn, dst, nq, lnt, stt_, mv_):
            for c_ in range(2):
                em.v("dve", "bn_stats", [stt_], [src], out=stt_[0:nq, c_, :], in_=src[0:nq, c_ * 512:(c_ + 1) * 512])
            em.v("dve", "bn_aggr", [mv_], [stt_], out=mv_[0:nq, 0:2], in_=stt_[0:nq].rearrange("p c s -> p (c s)"))
            em.v("dve", "tensor_scalar", [mv_], [mv_], out=mv_[0:nq, 2:3], in0=mv_[0:nq, 1:2], scalar1=EPS, scalar2=None,
                 op0=ALU.add)
            em.v("pool", "tensor_tensor", [mv_], [mv_, neghalf], out=mv_[0:nq, 3:4], in0=mv_[0:nq, 2:3],
                 in1=neghalf[0:nq, 0:1], op=ALU.pow)
            em.v("dve", "tensor_scalar", [mv_], [mv_], out=mv_[0:nq, 4:5], in0=mv_[0:nq, 0:1], scalar1=mv_[0:nq, 3:4],
                 scalar2=-1.0, op0=ALU.mult, op1=ALU.mult)
            em.act(xn, xn[0:nq, :], src[0:nq, :], AF.Identity, [src, mv_], bias=mv_[0:nq, 4:5], scale=mv_[0:nq, 3:4])
            em.v("dve", "tensor_tensor", [xn], [xn, lnt], out=xn[0:nq, :], in0=xn[0:nq, :], in1=lnt[0:nq, 0, :],
                 op=ALU.mult)
            em.v("pool", "tensor_tensor", [dst], [xn, lnt], out=dst[0:nq, :], in0=xn[0:nq, :], in1=lnt[0:nq, 1, :],
                 op=ALU.add)

        BLK = [(0, 2)] + [(2 + 128 * i, 128) for i in range(16)]
        def c2_z(bi):
            t0, nq = BLK[bi]
            s_ = bi % 2
            Z = [banks[s_ * 2], banks[s_ * 2 + 1]]
            em.dma("sp", xo[s_][0:nq, :], dr["xown"][t0:t0 + nq, :], xo[s_], True)
            for n in range(2):
                for cc in range(8):
                    em.mm(Z[n], Z[n][0:nq, 0:512], yT[:, cc, t0:t0 + nq], w_o[:, cc, n * 512:(n + 1) * 512], cc == 0,
                          cc == 7, [yT, w_o])

        def c2_ln(bi):
            t0, nq = BLK[bi]
            s_ = bi % 2
            Z = [banks[s_ * 2], banks[s_ * 2 + 1]]
            for n in range(2):
                em.v("dve", "scalar_tensor_tensor", [rrs[s_]], [xo[s_], Z[n]], out=rrs[s_][0:nq, n * 512:(n + 1) * 512],
                     in0=xo[s_][0:nq, n * 512:(n + 1) * 512], scalar=ALPHA, in1=Z[n][0:nq, 0:512], op0=ALU.mult,
                     op1=ALU.add)
            layer_norm(rrs[s_], xns[s_], x1t[s_], nq, lnp, stts[s_], mvs[s_])
            if nq == 128:
                em.dma("sp", x1s[t0 - 2:t0 - 2 + 128, :], x1t[s_][:], x1t[s_], False)
            em.act(x1bs[s_], x1bs[s_][0:nq, :], x1t[s_][0:nq, :], AF.Copy, [x1t[s_]])

        def c2_tr(bi):
            t0, nq = BLK[bi]
            s_ = bi % 2
            TR = banks[4 + s_]
            TRbf = TR.ap.bitcast(BF16)
            x1b = x1bs[s_]
            for cc in range(8):
                em.op("pe", lambda e, cc=cc, nq=nq, TRbf=TRbf, x1b=x1b: e.transpose(TRbf[:, cc * nq:(cc + 1) * nq],
                                                                                     x1b[0:nq, cc * 128:(cc + 1) * 128],
                                                                                     ident[0:nq, 0:nq]),
                      reads=[x1b, ident], writes=[TR])
            em.v("dve", "tensor_copy", [x1T], [TR], out=x1T[:, :, t0:t0 + nq],
                 in_=TRbf[:, 0:8 * nq].rearrange("p (c q) -> p c q", c=8))

        c2_z(0)
        for bi in range(len(BLK)):
            if bi + 1 < len(BLK):
                c2_z(bi + 1)
            c2_ln(bi)
            if bi >= 1:
                c2_tr(bi - 1)
        c2_tr(len(BLK) - 1)
        dump3("x1T", x1T, 8)
        if stop_after == "C2":
            return done()
        em.barrier()
        ar.lo = M2
        ar.hi = M0[1]

        fT = ar.alloc("fT", [128, NCP, NT], BF16)
        M3 = ar.lo
        hA = [ar.alloc("hA%d" % i, [128, NTO], F32) for i in range(2)]
        hG = [ar.alloc("hG%d" % i, [128, NTO], F32) for i in range(2)]
        ca = ar.alloc("ca", [128, NT], F32)
        cg = ar.alloc("cg", [128, NT], F32)
        WuA = [ar.alloc("WuA%d" % i, [128, 8, 128], BF16) for i in range(2)]
        WuG = [ar.alloc("WuG%d" % i, [128, 8, 128], BF16) for i in range(2)]
        UT = [(0, 512), (512, 512), (1024, 512), (1536, 512), (2048, 2)]
        WC, BC = CST["wc"], CST["bc"]
        it = 0
        for cp in range(NCP):
            s_ = cp % 2
            em.dma("pool", WuA[s_][:], w_up_r[:, :, cp * 128:(cp + 1) * 128], WuA[s_], True)
            em.dma("pool", WuG[s_][:], w_up_r[:, :, DFF + cp * 128:DFF + (cp + 1) * 128], WuG[s_], True)
            for (u0, w) in UT:
                bs = it % 4
                it += 1
                A_, G_ = banks[2 * bs], banks[2 * bs + 1]
                for kc in range(8):
                    em.mm(A_, A_[:, 0:w], WuA[s_][:, kc, :], x1T[:, kc, u0:u0 + w], kc == 0, kc == 7, [WuA[s_], x1T])
                for kc in range(8):
                    em.mm(G_, G_[:, 0:w], WuG[s_][:, kc, :], x1T[:, kc, u0:u0 + w], kc == 0, kc == 7, [WuG[s_], x1T])
                em.act(hA[s_], hA[s_][:, u0:u0 + w], A_[:, 0:w], AF.Copy, [A_])
                em.act(hG[s_], hG[s_][:, u0:u0 + w], G_[:, 0:w], AF.Copy, [G_])
            for (hb, cc_, col) in ((hA[s_], ca, cp), (hG[s_], cg, NCP + cp)):
                em.v("dve", "tensor_scalar", [hb], [hb, cst], out=hb[:, 0:2], in0=hb[:, 0:2], scalar1=flag_ap,
                     scalar2=None, op0=ALU.mult)
                em.act(cc_, cc_[:], hb[:, 2:NTO], AF.Identity, [hb, cst], bias=cst[:, BC + col:BC + col + 1],
                       scale=cst[:, WC + 88 + col:WC + 88 + col + 1])
                em.v("dve", "scalar_tensor_tensor", [cc_], [hb, cst, cc_], out=cc_[:], in0=hb[:, 1:NTO - 1],
                     scalar=cst[:, WC + 44 + col:WC + 44 + col + 1], in1=cc_[:], op0=ALU.mult, op1=ALU.add)
                em.v("dve", "scalar_tensor_tensor", [cc_], [hb, cst, cc_], out=cc_[:], in0=hb[:, 0:NT],
                     scalar=cst[:, WC + col:WC + col + 1], in1=cc_[:], op0=ALU.mult, op1=ALU.add)
            em.act(ca, ca[:], ca[:], AF.Gelu, [ca])
            em.v("dve", "tensor_tensor", [fT], [ca, cg], out=fT[:, cp, :], in0=ca[:], in1=cg[:], op=ALU.mult)
        if stop_after == "D1":
            return done()
        em.barrier()
        ar.lo = M3

        w_dn = ar.alloc("w_dn", [128, NCP, D], BF16)
        lnp2 = ar.alloc("lnp2", [128, 2, D], F32)
        xb_ = [ar.alloc("xb%d" % i, [128, D], F32) for i in range(2)]
        r2s = [ar.alloc("r2_%d" % i, [128, D], F32, at=x1T.off + i * 4096) for i in range(2)]
        xn2 = [ar.alloc("xn2_%d" % i, [128, D], F32, at=x1T.off + 8192 + i * 4096) for i in range(2)]
        stt2 = [ar.alloc("stt2_%d" % i, [128, 2, 6], F32, at=x1T.off + 16384 + i * 64) for i in range(2)]
        mv2 = [ar.alloc("mv2_%d" % i, [128, 8], F32, at=x1T.off + 16640 + i * 64) for i in range(2)]
        ot = [ar.alloc("ot%d" % i, [128, D], F32) for i in range(2)]
        stt = ar.alloc("stt2", [128, 2, 6], F32)
        mv = ar.alloc("mv2", [128, 8], F32)
        for hf in range(2):
            em.dma("pool", w_dn[:, hf * 11:(hf + 1) * 11, :], w_down_r[:, hf * 11:(hf + 1) * 11, :], w_dn, True, group="wd")
        em.dma("sp", lnp2[:].rearrange("p a b -> p (a b)"), dr["lnp"][:, 2 * D:4 * D], lnp2, True)
        em.dma("sp", xb_[0][:], x1s[0:128, :], xb_[0], True)
        for blk in range(16):
            s_ = blk % 2
            Z = [banks[s_ * 2], banks[s_ * 2 + 1]]
            if blk + 1 < 16:
                em.dma("sp", xb_[1 - s_][:], x1s[(blk + 1) * 128:(blk + 2) * 128, :], xb_[1 - s_], True)
            for n in range(2):
                for cp in range(NCP):
                    em.mm(Z[n], Z[n][:, 0:512], fT[:, cp, blk * 128:(blk + 1) * 128], w_dn[:, cp, n * 512:(n + 1) * 512],
                          cp == 0, cp == NCP - 1, [fT, w_dn])
                em.v("dve", "scalar_tensor_tensor", [r2s[s_]], [xb_[s_], Z[n]], out=r2s[s_][:, n * 512:(n + 1) * 512],
                     in0=xb_[s_][:, n * 512:(n + 1) * 512], scalar=ALPHA, in1=Z[n][:, 0:512], op0=ALU.mult, op1=ALU.add)
            layer_norm(r2s[s_], xn2[s_], ot[s_], 128, lnp2, stt2[s_], mv2[s_])
            em.dma("sp", y_out[blk * 128:(blk + 1) * 128, :], ot[s_][:], ot[s_], False)
        return done()


def kernel(**inputs):
    in_maps = prep_inputs(**{k: np.asarray(v) for k, v in inputs.items()})
    nc = build()
    res = run_bass_kernel_spmd(nc, in_maps, core_ids=list(range(8)))
    out = np.zeros((4, S, D), np.float32)
    for c in range(8):
        b, half = c // 2, c % 2
        out[b, half * NT:(half + 1) * NT] = res.results[c]["y"]
    return out
```

```python
import math
from contextlib import ExitStack

import numpy as np
import concourse.bass as bass
import concourse.mybir as mybir
from concourse.bass_utils import run_bass_kernel_spmd

F32 = mybir.dt.float32
BF16 = mybir.dt.bfloat16
AF = mybir.ActivationFunctionType
ALU = mybir.AluOpType
AX = mybir.AxisListType

ENGS = ("pe", "act", "dve", "pool", "sp")

D = 1024
S = 4096
NT = 2048
HALO = 2
NTO = NT + HALO
DFF = 2816
NCP = 22
IN_COLS = 9728
ALPHA = 2.0 ** 0.25
LAMBDA_INIT = 0.8 - 0.6 * math.exp(0.0)
C_SUBLN = 1.0 - LAMBDA_INIT
EPS = 1e-5
NEG = -30000.0
GROUP_DIL = (1, 4, 16)
NTAB = 73
SKIP_MARGIN = 150.0


class T:
    def __init__(self, name, ap):
        self.name = name
        self.ap = ap
        self.last_w = None
        self.readers = {}
        self.dma_sem = None
        self.dma_cnt = 0
        self.dma_wcnt = 0
        self.group = None

    def __getitem__(self, k):
        return self.ap[k]


class Op:
    __slots__ = ("eng", "fn", "waits", "inc", "idx", "is_dma")

    def __init__(self, eng, fn):
        self.eng = eng
        self.fn = fn
        self.waits = []
        self.inc = False
        self.idx = 0
        self.is_dma = False


class Emitter:
    def __init__(self, nc, stack):
        self.nc = nc
        self.stack = stack
        self.ops = {e: [] for e in ENGS}
        self.cops = {e: [] for e in ENGS}
        self.sems = {e: stack.enter_context(nc.semaphore("s_" + e)) for e in ENGS}
        self.waited = {e: {} for e in ENGS}
        self.tiles = []
        self.free_dsems = []
        self.ndsem = 0

    def tile(self, name, ap):
        t = T(name, ap)
        self.tiles.append(t)
        return t

    def _dsem(self, t):
        if t.dma_sem is None:
            t.dma_sem = self.stack.enter_context(self.nc.semaphore("d%d" % self.ndsem))
            self.ndsem += 1
        return t.dma_sem

    def _need(self, op, src_eng, idx):
        if idx is None or idx <= 0:
            return
        e = op.eng
        if src_eng == e and e == "pe":
            return
        w = self.waited[e]
        if w.get(src_eng, 0) >= idx:
            return
        w[src_eng] = idx
        op.waits.append(("eng", src_eng, idx))
        self.cops[src_eng][idx - 1].inc = True

    def _need_dma(self, op, t, writes_only=False):
        cnt = t.dma_wcnt if writes_only else t.dma_cnt
        if t.dma_sem is None or cnt == 0:
            return
        key = ("dma", id(t))
        w = self.waited[op.eng]
        if w.get(key, 0) >= cnt:
            return
        w[key] = cnt
        op.waits.append(("dma", t, cnt * 16))

    def op(self, eng, fn, reads=(), writes=()):
        o = Op(eng, fn)
        self.cops[eng].append(o)
        o.idx = len(self.cops[eng])
        for t in reads:
            if t.last_w is not None:
                self._need(o, *t.last_w)
            self._need_dma(o, t, writes_only=True)
        for t in writes:
            if t.last_w is not None:
                self._need(o, *t.last_w)
            for re_, ri in t.readers.items():
                self._need(o, re_, ri)
            self._need_dma(o, t)
        for t in reads:
            if t.readers.get(eng, 0) < o.idx:
                t.readers[eng] = o.idx
        for t in writes:
            t.last_w = (eng, o.idx)
            t.readers = {}
        self.ops[eng].append(o)
        return o

    def dma(self, eng, out_ap, in_ap, tile, is_load, group=None, also_read=()):
        t = tile
        sem = self._dsem(t)

        def fn(e, out_ap=out_ap, in_ap=in_ap, sem=sem):
            return e.dma_start(out=out_ap, in_=in_ap).then_inc(sem, 16)

        o = Op(eng, fn)
        o.is_dma = True
        if t.last_w is not None:
            self._need(o, *t.last_w)
        if is_load:
            for re_, ri in t.readers.items():
                self._need(o, re_, ri)
        if not (group is not None and t.group == group):
            self._need_dma(o, t)
        for t2 in also_read:
            if t2.last_w is not None:
                self._need(o, *t2.last_w)
        t.group = group
        t.dma_cnt += 1
        if is_load:
            t.dma_wcnt = t.dma_cnt
        for t2 in also_read:
            t2.dma_sem = sem
            t2.dma_cnt = t.dma_cnt
        if is_load:
            t.last_w = None
            t.readers = {}
        self.ops[eng].append(o)
        return o

    def barrier(self):
        snap = {e: len(self.cops[e]) for e in ENGS}
        for e in ENGS:
            o = Op(e, None)
            self.cops[e].append(o)
            o.idx = len(self.cops[e])
            for src in ENGS:
                if src != e:
                    self._need(o, src, snap[src])
            for t in self.tiles:
                self._need_dma(o, t)
            self.ops[e].append(o)
        self.tiles = [t for t in self.tiles if not getattr(t, "dead", False)]

    def emit(self):
        nc = self.nc
        em = self
        pref = {}
        for e in ENGS:
            c = 0
            p = [0]
            for o in em.cops[e]:
                if o.inc:
                    c += 1
                p.append(c)
            pref[e] = p
        with nc.Block() as block:
            def run(ename, eh):
                for o in em.ops[ename]:
                    for kind, src, val in o.waits:
                        if kind == "eng":
                            eh.wait_ge(em.sems[src], pref[src][val])
                        else:
                            eh.wait_ge(src.dma_sem, val)
                    ins = None
                    if o.fn is not None:
                        ins = o.fn(eh)
                    if o.is_dma:
                        continue
                    if o.inc:
                        if ins is None:
                            ins = eh.nop()
                        ins.then_inc(em.sems[ename], 1)

            @block.tensor
            def _(e):
                run("pe", e)

            @block.scalar
            def _(e):
                run("act", e)

            @block.vector
            def _(e):
                run("dve", e)

            @block.gpsimd
            def _(e):
                run("pool", e)

            @block.sync
            def _(e):
                run("sp", e)

    def mm(self, out_t, out_ap, lhsT, rhs, start, stop, reads, skip=False):
        return self.op("pe", lambda e: e.matmul(out_ap, lhsT=lhsT, rhs=rhs, start=start, stop=stop,
                                                skip_group_check=skip),
                       reads=reads, writes=[out_t])

    def act(self, out_t, out_ap, in_ap, func, reads, bias=None, scale=None):
        kw = {}
        if bias is not None:
            kw["bias"] = bias
        if scale is not None:
            kw["scale"] = scale
        return self.op("act", lambda e: e.activation(out=out_ap, in_=in_ap, func=func, **kw),
                       reads=reads, writes=[out_t])

    def v(self, eng, name, writes, reads, **kw):
        return self.op(eng, lambda e: getattr(e, name)(**kw), reads=reads, writes=writes)


class Arena:
    def __init__(self, em, nbytes):
        self.em = em
        self.n = nbytes
        self.h = em.stack.enter_context(em.nc.sbuf_tensor("arena", [128, nbytes // 2], BF16))
        self.lo = 0
        self.hi = nbytes

    def alloc(self, name, shape, dt, top=False, at=None):
        esz = 4 if dt == F32 else 2
        n = int(np.prod(shape[1:])) * esz
        n = (n + 63) // 64 * 64
        if at is not None:
            off = at
        elif top:
            self.hi -= n
            off = self.hi
        else:
            off = self.lo
            self.lo += n
        assert self.lo <= self.hi, "arena overflow at %s: lo=%d hi=%d" % (name, self.lo, self.hi)
        nel = int(np.prod(shape[1:]))
        a = self.h[:, off // 2: off // 2 + nel * esz // 2]
        if dt == F32:
            a = a.bitcast(F32)
        if len(shape) == 3:
            a = a.rearrange("p (a b) -> p a b", a=shape[1])
        elif len(shape) == 4:
            a = a.rearrange("p (a b c) -> p a b c", a=shape[1], b=shape[2])
        if shape[0] < 128:
            a = a[0:shape[0]]
        t = self.em.tile(name, a)
        t.off = off
        t.nbytes = n
        return t


def slopes8():
    return [2.0 ** (-(h + 1)) for h in range(8)]


def biasA_layout():
    cols = {}
    n = 0
    for r in range(5):
        njs = 16 if r == 0 else 16 + 4 * r
        for j in range(njs):
            for h in range(8):
                ns = 1 if (r == 0 or h >= 2) else (2 if h == 1 else 4)
                for s in range(ns):
                    cols[(r, j, h, s)] = n
                    n += 1
    return cols, n


BIAS_COLS, NBIAS = biasA_layout()
CST = {}
_o = 0
for _name, _n in (("bg", 16), ("lam4", 256), ("subln", 128), ("wc", 132), ("bc", 44), ("flag", 1),
                  ("biasA", NBIAS)):
    CST[_name] = _o
    _o += _n
NCST = _o


def nsub(r, h):
    return 1 if (r == 0 or h >= 2) else (2 if h == 1 else 4)


def build_biasA(half):
    m = slopes8()
    out = np.zeros((128, NBIAS), np.float32)
    p = np.arange(128, dtype=np.float64)
    for (r, j, h, s), c in BIAS_COLS.items():
        if r == 0:
            qc = 2047.0
        else:
            ws = 512 // nsub(r, h)
            qc = 2048 + 512 * (r - 1) + ws * s + ws / 2
        if half == 0 and r >= 1 and j < 16:
            out[:, c] = NEG
        else:
            out[:, c] = m[h] * (128 * j + p - qc)
    return out


def build_tabB(half):
    m = slopes8()
    kj = np.arange(128)[:, None].astype(np.float64)
    qi = np.arange(128)[None, :].astype(np.float64)
    tabs = np.zeros((128, NTAB, 128), np.float32)
    for g, dil in enumerate(GROUP_DIL):
        for h in range(8):
            cur = np.where(qi - kj >= 0, -m[h] * dil * (qi - kj), NEG)
            st = qi + 128 - kj
            prev = np.where(st <= 128, -m[h] * dil * st, NEG)
            prevx = prev if half == 1 else np.full_like(prev, NEG)
            b = (g * 8 + h) * 3
            tabs[:, b + 0] = cur
            tabs[:, b + 1] = prev
            tabs[:, b + 2] = prevx
    tabs[:, 72] = np.where(kj <= qi, 0.0, NEG)
    return tabs.reshape(128, NTAB * 128)


def prep_inputs(x, w_in, b_gate, lambda_q1, lambda_k1, lambda_q2, lambda_k2, subln_w,
                w_pa, w_pb, w_o, ln1_g, ln1_b, w_up, w_conv, b_conv, w_down, ln2_g, ln2_b):
    f = np.float32
    rep = lambda v: np.ascontiguousarray(np.broadcast_to(np.asarray(v, f).reshape(1, -1), (128, np.asarray(v).size)))
    shared = {
        "w_in": np.ascontiguousarray(w_in[0], f), "w_pa": np.ascontiguousarray(w_pa[0], f),
        "w_pb": np.ascontiguousarray(w_pb[0], f), "w_o": np.ascontiguousarray(w_o[0], f),
        "w_up": np.ascontiguousarray(w_up[0], f), "w_down": np.ascontiguousarray(w_down[0], f),
        "lnp": np.ascontiguousarray(np.concatenate([rep(ln1_g[0]), rep(ln1_b[0]), rep(ln2_g[0]), rep(ln2_b[0])], 1)),
    }
    cst_common = np.zeros((128, NCST), f)
    cst_common[:, CST["bg"]:CST["bg"] + 16] = np.asarray(b_gate[0], f).reshape(16, 128).T
    cst_common[:, CST["lam4"]:CST["lam4"] + 256] = np.concatenate(
        [rep(lambda_q1[0]), rep(lambda_k1[0]), rep(lambda_q2[0]), rep(lambda_k2[0])], 1)
    cst_common[:, CST["subln"]:CST["subln"] + 128] = rep(subln_w[0])
    wc = np.asarray(w_conv[0], f).reshape(3, 44, 128)
    cst_common[:, CST["wc"]:CST["wc"] + 132] = wc.transpose(2, 0, 1).reshape(128, 132)
    cst_common[:, CST["bc"]:CST["bc"] + 44] = np.asarray(b_conv[0], f).reshape(44, 128).T
    tabs = [build_tabB(0), build_tabB(1)]
    biases = [build_biasA(0), build_biasA(1)]
    in_maps = []
    for c in range(8):
        b, half = c // 2, c % 2
        xb = np.asarray(x[b], f)
        if half == 1:
            xT = np.ascontiguousarray(xb.T)
            xown = np.ascontiguousarray(xb[2046:4096])
        else:
            xT = np.zeros((D, S), f)
            xT[:, 2048:] = xb[:2048].T
            xown = np.zeros((NTO, D), f)
            xown[2:] = xb[:2048]
        cst = cst_common.copy()
        cst[:, CST["flag"]] = float(half)
        cst[:, CST["biasA"]:CST["biasA"] + NBIAS] = biases[half]
        m = dict(shared)
        m.update({"xT": xT, "xown": xown, "cst": cst, "tabB": tabs[half]})
        in_maps.append(m)
    return in_maps


def build(stop_after=None, dbg=()):
    nc = bass.Bass("TRN2", target_bir_lowering=False)
    dr = {}
    for name, shape in (("xT", [D, S]), ("xown", [NTO, D]), ("w_in", [D, IN_COLS]), ("w_pa", [D, D]),
                        ("w_pb", [512, D]), ("w_o", [D, D]), ("w_up", [D, 2 * DFF]), ("w_down", [DFF, D]),
                        ("lnp", [128, 4 * D]), ("cst", [128, NCST]), ("tabB", [128, NTAB * 128])):
        dr[name] = nc.dram_tensor(name, shape, F32, kind="ExternalInput").ap()
    y_out = nc.dram_tensor("y", [NT, D], F32, kind="ExternalOutput").ap()
    x1s = nc.dram_tensor("x1s", [NT, D], F32).ap()
    dbg_out = {}
    for name, shape in dbg:
        dbg_out[name] = nc.dram_tensor(name, shape, F32, kind="ExternalOutput").ap()

    w_in_r = dr["w_in"].rearrange("(kc p) n -> p kc n", p=128)
    xT_r = dr["xT"].rearrange("(kc p) t -> p kc t", p=128)
    m_sl = slopes8()

    with ExitStack() as st:
        em = Emitter(nc, st)
        ar = Arena(em, 212000)
        ps_all = st.enter_context(nc.psum_tensor("ps_all", [128, 4096], F32))
        banks = [em.tile("bank%d" % i, ps_all[:, 512 * i:512 * (i + 1)]) for i in range(8)]

        def done():
            em.barrier()
            em.emit()
            return nc

        def dump(name, t, ap):
            if name not in dbg_out:
                return
            em.dma("sp", dbg_out[name], ap, t, False)

        cst = ar.alloc("cst", [128, NCST], F32)
        ident = ar.alloc("ident", [128, 128], BF16)
        i32 = ar.alloc("i32", [128, 128], F32)
        small = ar.alloc("small", [128, 64], F32)
        neghalf = ar.alloc("neghalf", [128, 16], F32)
        sublnc = ar.alloc("sublnc", [128, 128], F32)
        ones32 = ar.alloc("ones32", [128, 64], F32)
        em.dma("sp", cst[:], dr["cst"], cst, True)
        em.v("pool", "memset", [i32], [], ap=i32[:], constant=0.0)
        em.op("pool", lambda e: e.affine_select(out=i32[:], in_=i32[:], pattern=[[-1, 128]],
                                                 compare_op=ALU.not_equal, fill=1.0, base=0,
                                                 channel_multiplier=1), reads=[i32], writes=[i32])
        em.v("dve", "tensor_copy", [ident], [i32], out=ident[:], in_=i32[:])
        em.v("pool", "memset", [neghalf], [], ap=neghalf[:], constant=-0.5)
        em.v("pool", "memset", [ones32], [], ap=ones32[:], constant=1.0)
        L0 = CST["lam4"]
        lamt = ar.alloc("lamt", [128, 128], F32)
        em.v("dve", "tensor_tensor", [lamt], [cst], out=lamt[:, 0:64], in0=cst[:, L0:L0 + 64],
             in1=cst[:, L0 + 64:L0 + 128], op=ALU.mult)
        em.v("dve", "tensor_tensor", [lamt], [cst, lamt], out=lamt[:, 64:128], in0=cst[:, L0 + 128:L0 + 192],
             in1=cst[:, L0 + 192:L0 + 256], op=ALU.mult)
        em.v("dve", "reduce_sum", [small], [lamt], out=small[:, 1:2], in_=lamt[:, 0:64], axis=AX.X)
        em.v("dve", "reduce_sum", [small], [lamt, small], out=small[:, 2:3], in_=lamt[:, 64:128], axis=AX.X)
        em.act(small, small[:, 3:5], small[:, 1:3], AF.Exp, [small])
        em.v("dve", "tensor_tensor", [small], [small], out=small[:, 5:6], in0=small[:, 4:5], in1=small[:, 3:4],
             op=ALU.subtract)
        em.v("dve", "tensor_scalar", [small], [small], out=small[:, 0:1], in0=small[:, 5:6],
             scalar1=-LAMBDA_INIT, scalar2=None, op0=ALU.add)
        em.v("dve", "tensor_scalar", [sublnc], [cst], out=sublnc[:], in0=cst[:, CST["subln"]:CST["subln"] + 128],
             scalar1=C_SUBLN, scalar2=None, op0=ALU.mult)
        neglam = small[:, 0:1]
        flag_ap = cst[:, CST["flag"]:CST["flag"] + 1]
        BA = CST["biasA"]

        M0 = (ar.lo, ar.hi)
        xT_all = ar.alloc("xT_bf", [128, 8, S], BF16)
        xTv = []
        for i in range(8):
            v_ = em.tile("xTv%d" % i, xT_all.ap)
            em.dma("pool", v_[:, :, i * 512:(i + 1) * 512], xT_r[:, :, i * 512:(i + 1) * 512], v_, True)
            xTv.append(v_)
        xT = xT_all.ap

        def xview(ctx0):
            return xTv[ctx0 // 512]

        oaT = ar.alloc("oaT", [128, 8, NTO], BF16)
        obT = ar.alloc("obT", [128, 4, NTO], BF16)
        M1 = (ar.lo, ar.hi)
        maskAt = ar.alloc("maskA", [128, 128], BF16)
        em.dma("pool", maskAt[:], dr["tabB"][:, 72 * 128:73 * 128], maskAt, True)
        maskA = maskAt.ap
        Qp = [ar.alloc("Qp%d" % m, [128, NTO], BF16) for m in range(2)]
        KT = ar.alloc("KT", [128, S], BF16)
        VV = ar.alloc("VV", [128, 32 * 4 * 129 + 128], BF16)
        V2 = VV.ap[:, 0:32 * 2 * 129].rearrange("p (b h e) -> p b h e", b=32, h=2)
        em.v("pool", "memset", [VV], [], ap=V2[:, :, :, 128:129], constant=1.0)
        v2end = VV.off + 32 * 2 * 129 * 2
        KTb = ar.alloc("KTb", [128, S], BF16, at=v2end)
        Qpb = [ar.alloc("Qpb%d" % m, [128, NTO], BF16, at=v2end + 8192 + m * 4104) for m in range(2)]
        assert v2end + 8192 + 2 * 4104 <= VV.off + VV.nbytes
        Wq = ar.alloc("Wq", [128, 8, 128], BF16)
        Wk = ar.alloc("Wk", [128, 8, 128], BF16)
        Wv = ar.alloc("Wv", [128, 8, 512], BF16)
        Wv2t = ar.alloc("Wv2", [128, 8, 256], BF16, at=Wv.off)
        Wv2 = Wv2t.ap
        Wqb_ = ar.alloc("WqB", [128, 8, 128], BF16, at=Wv.off + 4096)
        Wkb_ = ar.alloc("WkB", [128, 8, 128], BF16, at=Wv.off + 6144)
        KTs = [KT, KTb]
        Qps = [Qp, Qpb]
        Wqs = [Wq, Wqb_]
        Wks = [Wk, Wkb_]
        for qq in Qps:
            em.v("pool", "memset", [qq[0]], [], ap=qq[0][64:128, :], constant=0.0)
            em.v("pool", "memset", [qq[1]], [], ap=qq[1][0:64, :], constant=0.0)
        QT = [(0, 2, 2046)] + [(2 + 512 * i, 512, 2048 + 512 * i) for i in range(4)]

        mA = (ar.lo, ar.hi)
        P = [ar.alloc("P%d" % b, [128, 2, 512], BF16) for b in range(2)]
        tA = ar.alloc("tA", [128, 4, 128], F32)
        tO = ar.alloc("tO", [128, 4, 128], F32)
        oab = ar.alloc("oab", [128, 4, 128], BF16)
        rc = ar.alloc("rc", [128, 24], F32)
        Uc = ar.alloc("Uc", [128, 8, 129], F32)
        Sb = [[banks[0], banks[2]], [banks[1], banks[3]]]
        Spair = [ps_all[:, 1024 * b:1024 * (b + 1)].rearrange("p (m c) -> p m c", m=2) for b in range(2)]
        Ub = [banks[4], banks[5], banks[6]]
        X = banks[7]
        Uv = []
        Ut = []
        for u in range(8):
            bk, slot = Ub[u // 3], u % 3
            Uv.append(bk.ap[:, slot * 129:(slot + 1) * 129])
            Ut.append(bk)
        Xbf = X.ap.bitcast(BF16)
        pool5 = [banks[0], banks[1], banks[2], banks[3], banks[7]]
        rot = [0]

        def nextbank():
            b = pool5[rot[0] % 5]
            rot[0] += 1
            return b

        def kq_units(h, bankfn):
            pb = h % 2
            kt, qp, wq, wk = KTs[pb], Qps[pb], Wqs[pb], Wks[pb]
            units = []

            def wload():
                em.dma("pool", wq[:], w_in_r[:, :, h * 128:(h + 1) * 128], wq, True)
                em.dma("pool", wk[:], w_in_r[:, :, 1024 + h * 128:1024 + (h + 1) * 128], wk, True)
            units.append([wload])
            for tt in range(8):
                cell = {}
                mic = []
                for kc in range(8):
                    def kmm(kc=kc, tt=tt, cell=cell):
                        if kc == 0:
                            cell["bk"] = bankfn()
                        bk = cell["bk"]
                        em.mm(bk, bk[:, 0:512], wk[:, kc, :], xT[:, kc, tt * 512:(tt + 1) * 512], kc == 0, kc == 7,
                              [wk, xTv[tt]])
                    mic.append(kmm)

                def kev(tt=tt, cell=cell):
                    bk = cell["bk"]
                    em.v("dve", "tensor_copy", [kt], [bk], out=kt[:, tt * 512:(tt + 1) * 512], in_=bk[:, 0:512])
                mic.append(kev)
                units.append(mic)
            for (t0, w, c0) in QT:
                cell = {}
                mic = []
                for kc in range(8):
                    def qmm(kc=kc, t0=t0, w=w, c0=c0, cell=cell):
                        if kc == 0:
                            cell["bk"] = bankfn()
                        bk = cell["bk"]
                        em.mm(bk, bk[:, 0:w], wq[:, kc, :], xT[:, kc, c0:c0 + w], kc == 0, kc == 7, [wq, xview(c0)])
                    mic.append(qmm)

                def qev(t0=t0, w=w, cell=cell):
                    bk = cell["bk"]
                    em.v("dve", "tensor_scalar", [qp[0]], [bk], out=qp[0][0:64, t0:t0 + w], in0=bk[0:64, 0:w],
                         scalar1=0.125, scalar2=None, op0=ALU.mult)
                    em.v("dve", "tensor_scalar", [qp[1]], [bk], out=qp[1][64:128, t0:t0 + w], in0=bk[64:128, 0:w],
                         scalar1=0.125, scalar2=None, op0=ALU.mult)
                mic.append(qev)
                units.append(mic)
            return units

        for u_ in kq_units(0, nextbank):
            for mi_ in u_:
                mi_()
        deferred = []
        for h in range(8):
            hh = h % 2
            KTh, Qph = KTs[h % 2], Qps[h % 2]
            if hh == 0:
                em.dma("pool", Wv2, w_in_r[:, :, 2048 + h * 128:2048 + (h + 2) * 128], Wv2t, True)
                for blk in range(32):
                    bk = nextbank()
                    for kc in range(8):
                        em.mm(bk, bk[:, 0:256], xT[:, kc, blk * 128:(blk + 1) * 128], Wv2[:, kc, :], kc == 0, kc == 7,
                              [Wv2t, xview(blk * 128)])
                    em.v("dve", "tensor_copy", [VV], [bk], out=V2[:, blk, :, 0:128],
                         in_=bk[:, 0:256].rearrange("p (h e) -> p h e", h=2))
            pend_units = kq_units(h + 1, lambda: X) if h < 7 else []
            pending = []

            def pump(n):
                while n > 0 and (pending or pend_units):
                    if not pending:
                        pending.extend(pend_units.pop(0))
                    pending.pop(0)()
                    n -= 1

            def flush_unit():
                while pending:
                    pending.pop(0)()
            plan = []
            for r in range(5):
                t0, w, _c = QT[r]
                if r == 0:
                    steps = [(j, 0, False) for j in range(15)] + [(15, 0, True)]
                    nfull = 15
                else:
                    nfull = 16 + 4 * (r - 1)
                    steps = [(j, 0, False) for j in range(nfull)] + [(nfull + d, 128 * d, True) for d in range(4)]
                ns = nsub(r, h)
                ws = w // ns
                qc0 = 2047.0 if r == 0 else 2048 + 512 * (r - 1) + ws / 2.0
                keep = [st_ for st_ in steps if st_[2] or m_sl[h] * (qc0 - (128 * st_[0] + 127)) <= SKIP_MARGIN]
                if not any(not st_[2] for st_ in keep):
                    keep = [st_ for st_ in steps if not st_[2]][-1:] + keep
                plan.append((keep, nfull, ns, ws))
            total_steps = sum(len(p_[0]) for p_ in plan)
            n_micro = sum(len(u_) for u_ in pend_units)
            rate = n_micro / float(max(total_steps, 1))
            credit = 0.0
            for r in range(5):
                t0, w, _c = QT[r]
                steps, nfull, ns, ws = plan[r]
                if r == 0:
                    dw, mc0, nblk, nq = 2, 126, 1, 2
                else:
                    dw, mc0, nblk, nq = 128, 0, 4, 128

                def QK(si):
                    j, c0, diag = steps[si]
                    buf = si % 2
                    kblk = KTh[:, j * 128:(j + 1) * 128]
                    for m in range(2):
                        sb = Sb[m][buf]
                        if not diag:
                            em.mm(sb, sb[:, 0:w], kblk, Qph[m][:, t0:t0 + w], True, True, [KTh, Qph[m]])
                        else:
                            em.mm(sb, sb[:, c0:c0 + dw], kblk, Qph[m][:, t0 + c0:t0 + c0 + dw], True, False, [KTh, Qph[m]])
                            em.mm(sb, sb[:, c0:c0 + dw], ident[:], maskA[:, mc0:mc0 + dw], False, True, [ident, maskAt])
                            if c0 + dw < w:
                                em.mm(sb, sb[:, c0 + dw:w], kblk, Qph[m][:, t0 + c0 + dw:t0 + w], True, True, [KTh, Qph[m]])

                def EXP(si):
                    j, c0, diag = steps[si]
                    buf = si % 2
                    for s_ in range(ns):
                        a, b_ = max(s_ * ws, c0), (s_ + 1) * ws
                        if b_ <= a:
                            continue
                        col = BA + BIAS_COLS[(r, j, h, s_)]
                        em.act(P[buf], P[buf][:, :, a:b_], Spair[buf][:, :, a:b_], AF.Exp,
                               [Sb[0][buf], Sb[1][buf], cst], bias=cst[:, col:col + 1])

                def PV(si):
                    j, c0, diag = steps[si]
                    buf = si % 2
                    seen = set()
                    for m in range(2):
                        for b_ in range(nblk):
                            if 128 * b_ < c0:
                                continue
                            u = m * 4 + b_
                            last = nfull + b_
                            first = (si == 0 and (u // 3) not in seen)
                            seen.add(u // 3)
                            em.mm(Ut[u], Uv[u][0:nq, 0:129], P[buf][:, m, 128 * b_:128 * b_ + nq],
                                  V2[:, j, hh, :], first, j == last, [P[buf], VV], skip=True)

                QK(0)
                if len(steps) > 1:
                    QK(1)
                for si in range(len(steps)):
                    EXP(si)
                    if si + 2 < len(steps):
                        QK(si + 2)
                    PV(si)
                    credit += rate
                    k_ = int(credit)
                    credit -= k_
                    pump(k_)
                if deferred:
                    flush_unit()
                for d_ in deferred:
                    d_()
                deferred = []
                nb = nblk
                Ucv = Uc.ap
                if nb == 4:
                    for bkI, (ua, ub) in enumerate(((0, 3), (3, 6), (6, 8))):
                        em.v("dve", "tensor_copy", [Uc], [Ub[bkI]],
                             out=Ucv[:, ua:ub, :].rearrange("p a b -> p (a b)"), in_=Ub[bkI][:, 0:(ub - ua) * 129])
                else:
                    em.v("dve", "tensor_copy", [Uc], [Ut[0]], out=Ucv[0:nq, 0, :], in_=Uv[0][0:nq, :])
                    em.v("dve", "tensor_copy", [Uc], [Ut[4]], out=Ucv[0:nq, 4, :], in_=Uv[4][0:nq, :])
                r0 = rc[0:nq, 0:nb]
                r1 = rc[0:nq, 4:4 + nb]
                em.v("dve", "reciprocal", [rc], [Uc], out=r0, in_=Ucv[0:nq, 0:nb, 128])
                em.v("dve", "reciprocal", [rc], [Uc, rc], out=r1, in_=Ucv[0:nq, 4:4 + nb, 128])
                a4 = tA[0:nq, 0:nb, :]
                o4 = tO[0:nq, 0:nb, :]
                bc = lambda ap2, nq=nq, nb=nb: ap2.unsqueeze(2).to_broadcast([nq, nb, 128])
                em.v("dve", "scalar_tensor_tensor", [tA], [Uc, rc, small], out=a4, in0=Ucv[0:nq, 4:4 + nb, 0:128],
                     scalar=neglam[0:nq, :], in1=bc(r1), op0=ALU.mult, op1=ALU.mult)
                em.v("dve", "tensor_tensor", [tO], [Uc, rc], out=o4, in0=Ucv[0:nq, 0:nb, 0:128], in1=bc(r0), op=ALU.mult)
                em.v("dve", "tensor_tensor", [tO], [tO, tA], out=o4, in0=o4, in1=a4, op=ALU.add)
                em.v("dve", "tensor_tensor", [tA], [tO], out=a4, in0=o4, in1=o4, op=ALU.mult)
                em.v("dve", "reduce_sum", [rc], [tA], out=rc[0:nq, 8:8 + nb], in_=a4, axis=AX.X)
                em.v("dve", "tensor_scalar", [rc], [rc], out=rc[0:nq, 12:12 + nb], in0=rc[0:nq, 8:8 + nb],
                     scalar1=1.0 / 128, scalar2=EPS, op0=ALU.mult, op1=ALU.add)
                em.v("pool", "tensor_tensor", [rc], [rc, neghalf], out=rc[0:nq, 16:16 + nb], in0=rc[0:nq, 12:12 + nb],
                     in1=neghalf[0:nq, 0:nb], op=ALU.pow)
                em.v("dve", "tensor_tensor", [tO], [tO, rc], out=o4, in0=o4, in1=bc(rc[0:nq, 16:16 + nb]), op=ALU.mult)
                em.v("dve", "tensor_tensor", [oab], [tO, sublnc], out=oab[0:nq, 0:nb, :], in0=o4,
                     in1=sublnc[0:nq, :].unsqueeze(1).to_broadcast([nq, nb, 128]), op=ALU.mult)

                def part2(nq=nq, nb=nb, t0=t0, h=h):
                    for b_ in range(nb):
                        em.op("pe", lambda e, b_=b_: e.transpose(Xbf[:, b_ * nq:(b_ + 1) * nq], oab[0:nq, b_, :],
                                                                 ident[0:nq, 0:nq]),
                              reads=[oab, ident], writes=[X])
                    em.v("dve", "tensor_copy", [oaT], [X], out=oaT[:, h, t0:t0 + nb * nq], in_=Xbf[:, 0:nb * nq])
                deferred.append(part2)
            pump(1 << 30)
        for d_ in deferred:
            d_()
        def dump3(name, src, nchunk):
            if name not in dbg_out:
                return
            em.barrier()
            stg = VV.ap[:, 0:2 * NTO].bitcast(F32)
            for c_ in range(nchunk):
                em.v("dve", "tensor_copy", [VV], [src], out=stg, in_=src[:, c_, :])
                em.dma("sp", dbg_out[name][:, c_, :], stg, VV, False)
            em.barrier()

        dump3("oaT", oaT, 8)
        if stop_after == "A":
            return done()
        em.barrier()
        ar.lo, ar.hi = mA

        tabB = ar.alloc("tabB", [128, NTAB - 1, 128], BF16)
        em.dma("pool", tabB[:], dr["tabB"][:, 0:72 * 128].rearrange("p (t q) -> p t q", t=NTAB - 1), tabB, True)
        Pb = [ar.alloc("Pb%d" % b, [128, 2, 128], BF16) for b in range(2)]
        shiftI = ar.alloc("shiftI", [128, 128], BF16)
        em.v("pool", "memset", [shiftI], [], ap=shiftI[:], constant=0.0)
        em.v("dve", "tensor_copy", [shiftI], [ident], out=shiftI[0:64, 64:128], in_=ident[0:64, 0:64])
        vtail = VV.off + 8320
        acc = [ar.alloc("acc%d" % i, [128, NTO], F32, at=vtail + i * 8200) for i in range(2)]
        rE = ar.alloc("rE", [128, NTO], F32, at=vtail + 2 * 8200)
        assert vtail + 3 * 8200 <= VV.off + VV.nbytes
        obtmp = ar.alloc("obtmp", [128, NTO], BF16, at=Wv.off)
        Wvb = ar.alloc("Wvb", [128, 8, 128], BF16, at=Wv.off + 4160)
        assert 4160 + 2048 <= Wv.nbytes
        Vp = VV.ap[:, 0:32 * 2 * 65].rearrange("p (b h e) -> p b h e", b=32, h=2)
        Wqb, Wkb = Wq, Wk
        SbB = [banks[0], banks[1]]
        ObB = [banks[2], banks[3]]
        poolB = [banks[4], banks[5], banks[6], banks[7]]
        rotB = [0]

        def nextbankB():
            b = poolB[rotB[0] % 4]
            rotB[0] += 1
            return b

        def chunk_tokens(g, c, r_):
            dil = GROUP_DIL[g]
            s0 = 128 * dil * c + r_
            return s0, dil

        for hp in range(4):
            for g in range(3):
                dil = GROUP_DIL[g]
                base = 3072 + g * 512 + hp * 128
                em.dma("pool", Wqb[:], w_in_r[:, :, base:base + 128], Wqb, True)
                em.dma("pool", Wkb[:], w_in_r[:, :, base + 1536:base + 1536 + 128], Wkb, True)
                em.dma("pool", Wvb[:], w_in_r[:, :, base + 3072:base + 3072 + 128], Wvb, True)
                first_tok = {0: 14 * 128, 1: 2 * 512, 2: 0}[g]
                for tt in range(first_tok // 512, 8):
                    bk = nextbankB()
                    for kc in range(8):
                        em.mm(bk, bk[:, 0:512], Wkb[:, kc, :], xT[:, kc, tt * 512:(tt + 1) * 512], kc == 0, kc == 7,
                              [Wkb, xTv[tt]])
                    em.v("dve", "tensor_copy", [KT], [bk], out=KT[:, tt * 512:(tt + 1) * 512], in_=bk[:, 0:512])
                for (t0, w, c0) in QT:
                    bk = nextbankB()
                    for kc in range(8):
                        em.mm(bk, bk[:, 0:w], Wqb[:, kc, :], xT[:, kc, c0:c0 + w], kc == 0, kc == 7, [Wqb, xview(c0)])
                    em.v("dve", "tensor_scalar", [Qp[0]], [bk], out=Qp[0][0:64, t0:t0 + w], in0=bk[0:64, 0:w],
                         scalar1=0.125, scalar2=None, op0=ALU.mult)
                    em.v("dve", "tensor_scalar", [Qp[1]], [bk], out=Qp[1][64:128, t0:t0 + w], in0=bk[64:128, 0:w],
                         scalar1=0.125, scalar2=None, op0=ALU.mult)
                if g == 0 and hp == 0:
                    pass
                for cid in range(32):
                    c_, r_ = cid // dil, cid % dil
                    s0 = 128 * dil * c_ + r_
                    if 128 * dil * c_ < first_tok:
                        continue
                    bk = nextbankB()
                    for kc in range(8):
                        em.mm(bk, bk[:, 0:128], xT[:, kc, s0:s0 + 127 * dil + 1:dil], Wvb[:, kc, :], kc == 0, kc == 7,
                              [Wvb] + xTv)
                    em.v("dve", "tensor_copy", [VV], [bk], out=Vp[:, cid, :, 0:64],
                         in_=bk[:, 0:128].rearrange("p (h e) -> p h e", h=2))
                if g == 0 and hp == 0:
                    em.v("pool", "memset", [VV], [], ap=Vp[:, :, :, 64:65], constant=1.0)
                items = []
                if g == 0:
                    items.append((15, 126, 128, 0, 1, 1))
                    for c_ in range(16, 32):
                        items.append((c_, 0, 128, 2 + 128 * (c_ - 16), 1, 2 if c_ == 16 else 1))
                elif g == 1:
                    items.append((3 * 4 + 2, 127, 128, 0, 1, 1))
                    items.append((3 * 4 + 3, 127, 128, 1, 1, 1))
                    for c_ in range(4, 8):
                        for r_ in range(4):
                            items.append((c_ * 4 + r_, 0, 128, 2 + 512 * (c_ - 4) + r_, 4, 2 if c_ == 4 else 1))
                else:
                    items.append((14, 127, 128, 0, 1, None))
                    items.append((15, 127, 128, 1, 1, None))
                    for r_ in range(16):
                        items.append((16 + r_, 0, 128, 2 + r_, 16, 2))
                for hd in range(2):
                    head = 2 * hp + hd
                    tb = (g * 8 + head) * 3

                    def kchunk(cid):
                        c_, r_ = cid // dil, cid % dil
                        s0 = 128 * dil * c_ + r_
                        return KT[:, s0:s0 + 127 * dil + 1:dil]

                    def QKb(ii):
                        cid, i0, i1, tq, tstep, pk = items[ii]
                        n = i1 - i0
                        sb = SbB[ii % 2]
                        qa = Qp[hd][:, tq:tq + (n - 1) * tstep + 1:tstep]
                        if pk is not None:
                            em.mm(sb, sb[:, 0:n], kchunk(cid - dil), qa, True, False, [KT, Qp[hd]])
                            em.mm(sb, sb[:, 0:n], ident[:], tabB[:, tb + pk, i0:i1], False, True, [ident, tabB])
                        em.mm(sb, sb[:, 128:128 + n], kchunk(cid), qa, True, False, [KT, Qp[hd]])
                        em.mm(sb, sb[:, 128:128 + n], ident[:], tabB[:, tb + 0, i0:i1], False, True, [ident, tabB])

                    def EXPb(ii):
                        cid, i0, i1, tq, tstep, pk = items[ii]
                        n = i1 - i0
                        sb = SbB[ii % 2]
                        pb = Pb[ii % 2]
                        if pk is not None and n == 128:
                            em.act(pb, pb[:].rearrange("p a b -> p (a b)"), sb[:, 0:256], AF.Exp, [sb])
                        else:
                            if pk is not None:
                                em.act(pb, pb[:, 0, 0:n], sb[:, 0:n], AF.Exp, [sb])
                            em.act(pb, pb[:, 1, 0:n], sb[:, 128:128 + n], AF.Exp, [sb])

                    def PVb(ii):
                        cid, i0, i1, tq, tstep, pk = items[ii]
                        n = i1 - i0
                        pb = Pb[ii % 2]
                        ob = ObB[ii % 2]
                        if pk is not None:
                            em.mm(ob, ob[0:65, 0:n], Vp[:, cid - dil, hd, :], pb[:, 0, 0:n], True, False, [VV, pb])
                        em.mm(ob, ob[0:65, 0:n], Vp[:, cid, hd, :], pb[:, 1, 0:n], pk is None, True, [VV, pb])
                        dst = acc[hd][0:65, tq:tq + (n - 1) * tstep + 1:tstep]
                        if g == 0:
                            em.v("dve", "tensor_copy", [acc[hd]], [ob], out=dst, in_=ob[0:65, 0:n])
                        else:
                            em.v("dve", "tensor_tensor", [acc[hd]], [ob, acc[hd]], out=dst, in0=dst, in1=ob[0:65, 0:n],
                                 op=ALU.add)

                    QKb(0)
                    if len(items) > 1:
                        QKb(1)
                    for ii in range(len(items)):
                        EXPb(ii)
                        if ii + 2 < len(items):
                            QKb(ii + 2)
                        PVb(ii)
            for hd in range(2):
                em.act(rE, rE[64:65, :], acc[hd][64:65, :], AF.Ln, [acc[hd]])
                em.act(rE, rE[64:65, :], rE[64:65, :], AF.Exp, [rE], scale=-1.0)
                for c0 in range(0, NTO, 512):
                    w = min(512, NTO - c0)
                    bk = nextbankB()
                    em.mm(bk, bk[0:64, 0:w], ones32[64:65, :], rE[64:65, c0:c0 + w], True, True, [ones32, rE])
                    if hd == 0:
                        em.v("dve", "tensor_tensor", [obT], [bk, acc[hd]], out=obT[0:64, hp, c0:c0 + w],
                             in0=acc[hd][0:64, c0:c0 + w], in1=bk[0:64, 0:w], op=ALU.mult)
                    else:
                        em.v("dve", "tensor_tensor", [obtmp], [bk, acc[hd]], out=obtmp[0:64, c0:c0 + w],
                             in0=acc[hd][0:64, c0:c0 + w], in1=bk[0:64, 0:w], op=ALU.mult)
                if hd == 1:
                    for c0 in range(0, NTO, 512):
                        w = min(512, NTO - c0)
                        bk = nextbankB()
                        em.mm(bk, bk[:, 0:w], shiftI[0:64, :], obtmp[0:64, c0:c0 + w], True, True, [shiftI, obtmp])
                        em.v("dve", "tensor_copy", [obT], [bk], out=obT[64:128, hp, c0:c0 + w], in_=bk[64:128, 0:w])
        dump3("obT", obT, 4)
        if stop_after == "B":
            return done()
        em.barrier()
        ar.lo, ar.hi = M1

        w_pa_r = dr["w_pa"].rearrange("(kc p) n -> p kc n", p=128)
        w_pb_r = dr["w_pb"].rearrange("(kc p) n -> p kc n", p=128)
        w_o_r = dr["w_o"].rearrange("(kc p) n -> p kc n", p=128)
        w_up_r = dr["w_up"].rearrange("(kc p) n -> p kc n", p=128)
        w_down_r = dr["w_down"].rearrange("(c p) n -> p c n", p=128)
        yT = ar.alloc("yT", [128, 8, NTO], BF16, top=True)
        WgA = [ar.alloc("WgA%d" % i, [128, 8, 128], BF16) for i in range(2)]
        WgB = [ar.alloc("WgB%d" % i, [128, 8, 128], BF16) for i in range(2)]
        Wpa = [ar.alloc("Wpa%d" % i, [128, 8, 128], BF16) for i in range(2)]
        Wpb = [ar.alloc("Wpb%d" % i, [128, 4, 128], BF16) for i in range(2)]
        g0s = [ar.alloc("g0s%d" % i, [128, 512], F32) for i in range(2)]
        g1s = [ar.alloc("g1s%d" % i, [128, 512], F32) for i in range(2)]
        tt = ar.alloc("tt", [128, 512], F32)
        tu = ar.alloc("tu", [128, 512], F32)
        BG = CST["bg"]
        it = 0
        for c in range(8):
            bi = c % 2
            em.dma("pool", WgA[bi][:], w_in_r[:, :, 7680 + c * 128:7680 + (c + 1) * 128], WgA[bi], True)
            em.dma("pool", WgB[bi][:], w_in_r[:, :, 8704 + c * 128:8704 + (c + 1) * 128], WgB[bi], True)
            em.dma("pool", Wpa[bi][:], w_pa_r[:, :, c * 128:(c + 1) * 128], Wpa[bi], True)
            em.dma("pool", Wpb[bi][:], w_pb_r[:, :, c * 128:(c + 1) * 128], Wpb[bi], True)
            for (t0, w, c0) in QT:
                st_ = it % 2
                it += 1
                G0, G1, PA, PB = banks[st_ * 4:st_ * 4 + 4]
                for kc in range(8):
                    em.mm(G0, G0[:, 0:w], WgA[bi][:, kc, :], xT[:, kc, c0:c0 + w], kc == 0, kc == 7, [WgA[bi], xview(c0)])
                for kc in range(8):
                    em.mm(G1, G1[:, 0:w], WgB[bi][:, kc, :], xT[:, kc, c0:c0 + w], kc == 0, kc == 7, [WgB[bi], xview(c0)])
                for fc in range(8):
                    em.mm(PA, PA[:, 0:w], Wpa[bi][:, fc, :], oaT[:, fc, t0:t0 + w], fc == 0, fc == 7, [Wpa[bi], oaT])
                for fc in range(4):
                    em.mm(PB, PB[:, 0:w], Wpb[bi][:, fc, :], obT[:, fc, t0:t0 + w], fc == 0, fc == 3, [Wpb[bi], obT])
                em.act(g0s[st_], g0s[st_][:, 0:w], G0[:, 0:w], AF.Sigmoid, [G0, cst], bias=cst[:, BG + c:BG + c + 1])
                em.act(g1s[st_], g1s[st_][:, 0:w], G1[:, 0:w], AF.Sigmoid, [G1, cst], bias=cst[:, BG + 8 + c:BG + 9 + c])
                em.v("dve", "tensor_tensor", [tt], [g0s[st_], PA], out=tt[:, 0:w], in0=g0s[st_][:, 0:w], in1=PA[:, 0:w],
                     op=ALU.mult)
                em.v("dve", "tensor_tensor", [tu], [g1s[st_], PB], out=tu[:, 0:w], in0=g1s[st_][:, 0:w], in1=PB[:, 0:w],
                     op=ALU.mult)
                em.v("dve", "tensor_tensor", [yT], [tt, tu], out=yT[:, c, t0:t0 + w], in0=tt[:, 0:w], in1=tu[:, 0:w],
                     op=ALU.add)
        dump3("yT", yT, 8)
        if stop_after == "C1":
            return done()
        em.barrier()
        ar.lo = M0[0]

        x1T = ar.alloc("x1T", [128, 8, NTO], BF16)
        M2 = ar.lo
        w_o = ar.alloc("w_o", [128, 8, D], BF16)
        lnp = ar.alloc("lnp", [128, 2, D], F32)
        xo = [ar.alloc("xo%d" % i, [128, D], F32) for i in range(2)]
        rrs = [ar.alloc("rr%d" % i, [128, D], F32) for i in range(2)]
        xns = [ar.alloc("xn%d" % i, [128, D], F32) for i in range(2)]
        stts = [ar.alloc("stt%d" % i, [128, 2, 6], F32) for i in range(2)]
        mvs = [ar.alloc("mv%d" % i, [128, 8], F32) for i in range(2)]
        x1t = [ar.alloc("x1t%d" % i, [128, D], F32) for i in range(2)]
        x1bs = [ar.alloc("x1b%d" % i, [128, D], BF16) for i in range(2)]
        stt = ar.alloc("stt", [128, 2, 6], F32)
        mv = ar.alloc("mv", [128, 8], F32)
        for hf in range(2):
            em.dma("pool", w_o[:, :, hf * 512:(hf + 1) * 512], w_o_r[:, :, hf * 512:(hf + 1) * 512], w_o, True, group="wo")
        em.dma("sp", lnp[:].rearrange("p a b -> p (a b)"), dr["lnp"][:, 0:2 * D], lnp, True)

        def layer_norm(src, xn, dst, nq, lnt, stt_, mv_):
            for c_ in range(2):
                em.v("dve", "bn_stats", [stt_], [src], out=stt_[0:nq, c_, :], in_=src[0:nq, c_ * 512:(c_ + 1) * 512])
            em.v("dve", "bn_aggr", [mv_], [stt_], out=mv_[0:nq, 0:2], in_=stt_[0:nq].rearrange("p c s -> p (c s)"))
            em.v("dve", "tensor_scalar", [mv_], [mv_], out=mv_[0:nq, 2:3], in0=mv_[0:nq, 1:2], scalar1=EPS, scalar2=None,
                 op0=ALU.add)
            em.v("pool", "tensor_tensor", [mv_], [mv_, neghalf], out=mv_[0:nq, 3:4], in0=mv_[0:nq, 2:3],
                 in1=neghalf[0:nq, 0:1], op=ALU.pow)
            em.v("dve", "tensor_scalar", [mv_], [mv_], out=mv_[0:nq, 4:5], in0=mv_[0:nq, 0:1], scalar1=mv_[0:nq, 3:4],
                 scalar2=-1.0, op0=ALU.mult, op1=ALU.mult)
            em.act(xn, xn[0:nq, :], src[0:nq, :], AF.Identity, [src, mv_], bias=mv_[0:nq, 4:5], scale=mv_[0:nq, 3:4])
            em.v("dve", "tensor_tensor", [xn], [xn, lnt], out=xn[0:nq, :], in0=xn[0:nq, :], in1=lnt[0:nq, 0, :],
                 op=ALU.mult)
            em.v("pool", "tensor_tensor", [dst], [xn, lnt], out=dst[0:nq, :], in0=xn[0:nq, :], in1=lnt[0:nq, 1, :],
                 op=ALU.add)

        BLK = [(0, 2)] + [(2 + 128 * i, 128) for i in range(16)]
        def c2_z(bi):
            t0, nq = BLK[bi]
            s_ = bi % 2
            Z = [banks[s_ * 2], banks[s_ * 2 + 1]]
            em.dma("sp", xo[s_][0:nq, :], dr["xown"][t0:t0 + nq, :], xo[s_], True)
            for n in range(2):
                for cc in range(8):
                    em.mm(Z[n], Z[n][0:nq, 0:512], yT[:, cc, t0:t0 + nq], w_o[:, cc, n * 512:(n + 1) * 512], cc == 0,
                          cc == 7, [yT, w_o])

        def c2_ln(bi):
            t0, nq = BLK[bi]
            s_ = bi % 2
            Z = [banks[s_ * 2], banks[s_ * 2 + 1]]
            for n in range(2):
                em.v("dve", "scalar_tensor_tensor", [rrs[s_]], [xo[s_], Z[n]], out=rrs[s_][0:nq, n * 512:(n + 1) * 512],
                     in0=xo[s_][0:nq, n * 512:(n + 1) * 512], scalar=ALPHA, in1=Z[n][0:nq, 0:512], op0=ALU.mult,
                     op1=ALU.add)
            layer_norm(rrs[s_], xns[s_], x1t[s_], nq, lnp, stts[s_], mvs[s_])
            if nq == 128:
                em.dma("sp", x1s[t0 - 2:t0 - 2 + 128, :], x1t[s_][:], x1t[s_], False)
            em.act(x1bs[s_], x1bs[s_][0:nq, :], x1t[s_][0:nq, :], AF.Copy, [x1t[s_]])

        def c2_tr(bi):
            t0, nq = BLK[bi]
            s_ = bi % 2
            TR = banks[4 + s_]
            TRbf = TR.ap.bitcast(BF16)
            x1b = x1bs[s_]
            for cc in range(8):
                em.op("pe", lambda e, cc=cc, nq=nq, TRbf=TRbf, x1b=x1b: e.transpose(TRbf[:, cc * nq:(cc + 1) * nq],
                                                                                     x1b[0:nq, cc * 128:(cc + 1) * 128],
                                                                                     ident[0:nq, 0:nq]),
                      reads=[x1b, ident], writes=[TR])
            em.v("dve", "tensor_copy", [x1T], [TR], out=x1T[:, :, t0:t0 + nq],
                 in_=TRbf[:, 0:8 * nq].rearrange("p (c q) -> p c q", c=8))

        c2_z(0)
        for bi in range(len(BLK)):
            if bi + 1 < len(BLK):
                c2_z(bi + 1)
            c2_ln(bi)
            if bi >= 1:
                c2_tr(bi - 1)
        c2_tr(len(BLK) - 1)
        dump3("x1T", x1T, 8)
        if stop_after == "C2":
            return done()
        em.barrier()
        ar.lo = M2
        ar.hi = M0[1]

        fT = ar.alloc("fT", [128, NCP, NT], BF16)
        M3 = ar.lo
        hA = [ar.alloc("hA%d" % i, [128, NTO], F32) for i in range(2)]
        hG = [ar.alloc("hG%d" % i, [128, NTO], F32) for i in range(2)]
        ca = ar.alloc("ca", [128, NT], F32)
        cg = ar.alloc("cg", [128, NT], F32)
        WuA = [ar.alloc("WuA%d" % i, [128, 8, 128], BF16) for i in range(2)]
        WuG = [ar.alloc("WuG%d" % i, [128, 8, 128], BF16) for i in range(2)]
        UT = [(0, 512), (512, 512), (1024, 512), (1536, 512), (2048, 2)]
        WC, BC = CST["wc"], CST["bc"]
        it = 0
        for cp in range(NCP):
            s_ = cp % 2
            em.dma("pool", WuA[s_][:], w_up_r[:, :, cp * 128:(cp + 1) * 128], WuA[s_], True)
            em.dma("pool", WuG[s_][:], w_up_r[:, :, DFF + cp * 128:DFF + (cp + 1) * 128], WuG[s_], True)
            for (u0, w) in UT:
                bs = it % 4
                it += 1
                A_, G_ = banks[2 * bs], banks[2 * bs + 1]
                for kc in range(8):
                    em.mm(A_, A_[:, 0:w], WuA[s_][:, kc, :], x1T[:, kc, u0:u0 + w], kc == 0, kc == 7, [WuA[s_], x1T])
                for kc in range(8):
                    em.mm(G_, G_[:, 0:w], WuG[s_][:, kc, :], x1T[:, kc, u0:u0 + w], kc == 0, kc == 7, [WuG[s_], x1T])
                em.act(hA[s_], hA[s_][:, u0:u0 + w], A_[:, 0:w], AF.Copy, [A_])
                em.act(hG[s_], hG[s_][:, u0:u0 + w], G_[:, 0:w], AF.Copy, [G_])
            for (hb, cc_, col) in ((hA[s_], ca, cp), (hG[s_], cg, NCP + cp)):
                em.v("dve", "tensor_scalar", [hb], [hb, cst], out=hb[:, 0:2], in0=hb[:, 0:2], scalar1=flag_ap,
                     scalar2=None, op0=ALU.mult)
                em.act(cc_, cc_[:], hb[:, 2:NTO], AF.Identity, [hb, cst], bias=cst[:, BC + col:BC + col + 1],
                       scale=cst[:, WC + 88 + col:WC + 88 + col + 1])
                em.v("dve", "scalar_tensor_tensor", [cc_], [hb, cst, cc_], out=cc_[:], in0=hb[:, 1:NTO - 1],
                     scalar=cst[:, WC + 44 + col:WC + 44 + col + 1], in1=cc_[:], op0=ALU.mult, op1=ALU.add)
                em.v("dve", "scalar_tensor_tensor", [cc_], [hb, cst, cc_], out=cc_[:], in0=hb[:, 0:NT],
                     scalar=cst[:, WC + col:WC + col + 1], in1=cc_[:], op0=ALU.mult, op1=ALU.add)
            em.act(ca, ca[:], ca[:], AF.Gelu, [ca])
            em.v("dve", "tensor_tensor", [fT], [ca, cg], out=fT[:, cp, :], in0=ca[:], in1=cg[:], op=ALU.mult)
        if stop_after == "D1":
            return done()
        em.barrier()
        ar.lo = M3

        w_dn = ar.alloc("w_dn", [128, NCP, D], BF16)
        lnp2 = ar.alloc("lnp2", [128, 2, D], F32)
        xb_ = [ar.alloc("xb%d" % i, [128, D], F32) for i in range(2)]
        r2s = [ar.alloc("r2_%d" % i, [128, D], F32, at=x1T.off + i * 4096) for i in range(2)]
        xn2 = [ar.alloc("xn2_%d" % i, [128, D], F32, at=x1T.off + 8192 + i * 4096) for i in range(2)]
        stt2 = [ar.alloc("stt2_%d" % i, [128, 2, 6], F32, at=x1T.off + 16384 + i * 64) for i in range(2)]
        mv2 = [ar.alloc("mv2_%d" % i, [128, 8], F32, at=x1T.off + 16640 + i * 64) for i in range(2)]
        ot = [ar.alloc("ot%d" % i, [128, D], F32) for i in range(2)]
        stt = ar.alloc("stt2", [128, 2, 6], F32)
        mv = ar.alloc("mv2", [128, 8], F32)
        for hf in range(2):
            em.dma("pool", w_dn[:, hf * 11:(hf + 1) * 11, :], w_down_r[:, hf * 11:(hf + 1) * 11, :], w_dn, True, group="wd")
        em.dma("sp", lnp2[:].rearrange("p a b -> p (a b)"), dr["lnp"][:, 2 * D:4 * D], lnp2, True)
        em.dma("sp", xb_[0][:], x1s[0:128, :], xb_[0], True)
        for blk in range(16):
            s_ = blk % 2
            Z = [banks[s_ * 2], banks[s_ * 2 + 1]]
            if blk + 1 < 16:
                em.dma("sp", xb_[1 - s_][:], x1s[(blk + 1) * 128:(blk + 2) * 128, :], xb_[1 - s_], True)
            for n in range(2):
                for cp in range(NCP):
                    em.mm(Z[n], Z[n][:, 0:512], fT[:, cp, blk * 128:(blk + 1) * 128], w_dn[:, cp, n * 512:(n + 1) * 512],
                          cp == 0, cp == NCP - 1, [fT, w_dn])
                em.v("dve", "scalar_tensor_tensor", [r2s[s_]], [xb_[s_], Z[n]], out=r2s[s_][:, n * 512:(n + 1) * 512],
                     in0=xb_[s_][:, n * 512:(n + 1) * 512], scalar=ALPHA, in1=Z[n][:, 0:512], op0=ALU.mult, op1=ALU.add)
            layer_norm(r2s[s_], xn2[s_], ot[s_], 128, lnp2, stt2[s_], mv2[s_])
            em.dma("sp", y_out[blk * 128:(blk + 1) * 128, :], ot[s_][:], ot[s_], False)
        return done()


def kernel(**inputs):
    in_maps = prep_inputs(**{k: np.asarray(v) for k, v in inputs.items()})
    nc = build()
    res = run_bass_kernel_spmd(nc, in_maps, core_ids=list(range(8)))
    out = np.zeros((4, S, D), np.float32)
    for c in range(8):
        b, half = c // 2, c % 2
        out[b, half * NT:(half + 1) * NT] = res.results[c]["y"]
    return out
```
